# Optimizing a Trainium2 kernel written in Bass

```python
import math
import jax, jax.numpy as jnp
from jax import lax
import numpy as np

D_MODEL = 1024
BATCH = 16
SEQ = 256
DEPTH = 2
DEC_BATCH = 8
DEC_SEQ = 4096
PAST_LEN = 256

GRID_W = 64
EPS = 1e-6
GATE_FLOOR = 1e-30
GDN_HEADS = 4
GDN_DK = 64
GDN_DV = 64
GDN_CONV = 5
GDN_CHUNK = 64
GDN_QK = GDN_HEADS * GDN_DK
GDN_W = GDN_HEADS * GDN_DV
GDN_CONV_CH = 2 * GDN_QK + GDN_W
HG_HEADS = 4
HG_DK = 64
HG_DV = 64
HG_CHUNK = 64
HG_QK = HG_HEADS * HG_DK
HG_W = HG_HEADS * HG_DV
MLA_HEADS = 4
MLA_Q_RANK = 384
MLA_KV_RANK = 256
MLA_NOPE = 128
MLA_ROPE = 64
MLA_DV = 128
MLA_QK = MLA_NOPE + MLA_ROPE
MLA_W = MLA_HEADS * MLA_DV
ROPE_BASE = 10000.0
Q_BLOCK = 128
MIX_W = GDN_W + HG_W + MLA_W
IN_SIZES = (GDN_QK, GDN_QK, GDN_W, GDN_W, 2 * GDN_HEADS, 2 * GDN_HEADS,
            HG_QK, HG_W, 2 * HG_QK, HG_W,
            MLA_Q_RANK, MLA_KV_RANK, MLA_ROPE)
IN_DIM = sum(IN_SIZES)
D_FF = -(-8 * D_MODEL // (3 * 256)) * 256

kernel_name = 'hybrid_gdn_hgrn2_mla_diffusion_step'

F32 = jnp.float32


def _split(x, sizes):
    out, off = [], 0
    for s in sizes:
        out.append(x[..., off:off + s])
        off += s
    return out


def rmsnorm(x, w):
    xf = x.astype(F32)
    y = xf * lax.rsqrt(jnp.mean(xf * xf, axis=-1, keepdims=True) + EPS)
    return (y * w.astype(F32)).astype(x.dtype)


def l2norm(x):
    return x * lax.rsqrt(jnp.sum(x * x, axis=-1, keepdims=True) + EPS)


def centred_dwconv(x, w):
    k = w.shape[-1]
    return lax.conv_general_dilated(
        x, w.T[:, None, :].astype(x.dtype), window_strides=(1,),
        padding=[(k // 2, k // 2)], dimension_numbers=('NWC', 'WIO', 'NWC'),
        feature_group_count=x.shape[-1])


def axial_rope(n):
    rows = n // GRID_W
    row = jnp.repeat(jnp.arange(rows, dtype=F32), GRID_W)
    col = jnp.tile(jnp.arange(GRID_W, dtype=F32), rows)
    nf = MLA_ROPE // 4
    inv = ROPE_BASE ** (-jnp.arange(nf, dtype=F32) / nf)
    ang = jnp.stack([row[:, None] * inv, col[:, None] * inv], axis=1)
    return jnp.cos(ang), jnp.sin(ang)


def apply_rope(x, cos, sin):
    xs = x.astype(F32).reshape(x.shape[:-1] + (2, 2, MLA_ROPE // 4))
    x1, x2 = xs[..., 0, :], xs[..., 1, :]
    out = jnp.stack([x1 * cos - x2 * sin, x2 * cos + x1 * sin], axis=-2)
    return out.reshape(x.shape).astype(x.dtype)


def _heads_first(t, n_chunks, chunk):
    b, _, h = t.shape[:3]
    t = t.reshape((b, n_chunks, chunk, h) + t.shape[3:])
    return t.transpose((0, 3, 1, 2) + tuple(range(4, t.ndim)))


def gdn_scan(q, k, v, g, beta, s0):
    bsz, t_len, h, dk = q.shape
    dv = v.shape[-1]
    c = GDN_CHUNK
    n = t_len // c
    qb = _heads_first(q * dk ** -0.5, n, c)
    kb = _heads_first(k, n, c)
    vb = _heads_first(v, n, c)
    gc = jnp.cumsum(_heads_first(g, n, c), axis=-1)
    bt = _heads_first(beta, n, c)[..., None]
    idx = jnp.arange(c)
    incl = idx[:, None] >= idx[None, :]
    strict = idx[:, None] > idx[None, :]
    diff = gc[..., :, None] - gc[..., None, :]
    decay = jnp.where(incl, jnp.exp(jnp.where(incl, diff, 0.0)), 0.0)
    k_beta = kb * bt
    m = jnp.where(strict, jnp.einsum('bhnik,bhnjk->bhnij', k_beta, kb) * decay, 0.0)
    a_mat = m + jnp.eye(c, dtype=m.dtype)
    rhs = jnp.concatenate([vb * bt, k_beta * jnp.exp(gc)[..., None]], axis=-1)
    sol = lax.linalg.triangular_solve(a_mat, rhs, left_side=True, lower=True, unit_diagonal=True)
    u, w = sol[..., :dv], sol[..., dv:]
    att = jnp.where(incl, jnp.einsum('bhnik,bhnjk->bhnij', qb, kb) * decay, 0.0)
    q_dec = qb * jnp.exp(gc)[..., None]
    k_tail = kb * jnp.exp(gc[..., -1:] - gc)[..., None]
    g_last = jnp.exp(gc[..., -1])

    def step(s, xs):
        att_i, u_i, w_i, qd_i, kt_i, gl_i = xs
        v_new = u_i - jnp.einsum('bhck,bhkv->bhcv', w_i, s)
        o = jnp.einsum('bhck,bhkv->bhcv', qd_i, s) + jnp.einsum('bhij,bhjv->bhiv', att_i, v_new)
        s = s * gl_i[..., None, None] + jnp.einsum('bhck,bhcv->bhkv', kt_i, v_new)
        return s, o

    xs = tuple(jnp.moveaxis(a, 2, 0) for a in (att, u, w, q_dec, k_tail, g_last))
    s_fin, o = lax.scan(step, s0.astype(F32), xs)
    return o.transpose(1, 0, 3, 2, 4).reshape(bsz, t_len, h, dv), s_fin


def hgrn_scan(q, k, v, logf, s0):
    bsz, t_len, h, _ = q.shape
    dv = v.shape[-1]
    c = HG_CHUNK
    n = t_len // c
    idx = jnp.arange(c)
    causal = (idx[:, None] >= idx[None, :])[:, :, None]

    def step(s, xs):
        qc, kc, vc, lf = xs
        bc = jnp.cumsum(lf, axis=2)
        diff = bc[:, :, :, None, :] - bc[:, :, None, :, :]
        dec = jnp.where(causal, jnp.exp(jnp.where(causal, diff, 0.0)), 0.0)
        att = jnp.einsum('bhtk,bhtsk,bhsk->bhts', qc, dec, kc)
        o = jnp.einsum('bhts,bhsv->bhtv', att, vc) + jnp.einsum('bhtk,bhkv->bhtv', qc * jnp.exp(bc), s)
        bl = bc[:, :, -1, :]
        s = s * jnp.exp(bl)[..., None] + jnp.einsum('bhsk,bhsv->bhkv', kc * jnp.exp(bl[:, :, None, :] - bc), vc)
        return s, o

    xs = tuple(jnp.moveaxis(_heads_first(a, n, c), 2, 0) for a in (q, k, v, logf))
    s_fin, o = lax.scan(step, s0.astype(F32), xs)
    return o.transpose(1, 0, 3, 2, 4).reshape(bsz, t_len, h, dv), s_fin


def _flip(t):
    return jnp.flip(t, axis=1)


def gdn_mixer(q, k, v, z, a, b, conv_w, a_log, dt_bias, norm_w, s_fwd, s_bwd):
    bsz, t_len, _ = q.shape
    qkv = jax.nn.silu(centred_dwconv(jnp.concatenate([q, k, v], axis=-1), conv_w)).astype(F32)
    q, k, v = _split(qkv, (GDN_QK, GDN_QK, GDN_W))
    q = l2norm(q.reshape(bsz, t_len, GDN_HEADS, GDN_DK))
    k = l2norm(k.reshape(bsz, t_len, GDN_HEADS, GDN_DK))
    v = v.reshape(bsz, t_len, GDN_HEADS, GDN_DV)
    a = a.astype(F32).reshape(bsz, t_len, 2, GDN_HEADS)
    b = b.astype(F32).reshape(bsz, t_len, 2, GDN_HEADS)
    g = -jnp.exp(a_log.astype(F32)) * jax.nn.softplus(a + dt_bias.astype(F32))
    beta = jax.nn.sigmoid(b)
    o_f, sf = gdn_scan(q, k, v, g[:, :, 0], beta[:, :, 0], s_fwd)
    o_b, sb = gdn_scan(_flip(q), _flip(k), _flip(v), _flip(g[:, :, 1]), _flip(beta[:, :, 1]), s_bwd)
    o = o_f + _flip(o_b)
    o = rmsnorm(o, norm_w) * jax.nn.silu(z.astype(F32).reshape(bsz, t_len, GDN_HEADS, GDN_DV))
    return o.reshape(bsz, t_len, GDN_W), sf, sb


def hgrn_mixer(q, i, f, g, lb, norm_w, s_fwd, s_bwd):
    bsz, t_len, _ = q.shape
    q = q.astype(F32).reshape(bsz, t_len, HG_HEADS, HG_DK)
    v = i.astype(F32).reshape(bsz, t_len, HG_HEADS, HG_DV)
    f = f.astype(F32).reshape(bsz, t_len, 2, HG_HEADS, HG_DK)
    lb = lb.astype(F32).reshape(HG_HEADS, HG_DK)
    gate = lb + (1.0 - lb) * jax.nn.sigmoid(f)
    logf = jnp.log(jnp.maximum(gate, GATE_FLOOR))
    k = (1.0 - lb) * jax.nn.sigmoid(-f)
    o_f, sf = hgrn_scan(q, k[:, :, 0], v, logf[:, :, 0], s_fwd)
    o_b, sb = hgrn_scan(_flip(q), _flip(k[:, :, 1]), _flip(v), _flip(logf[:, :, 1]), s_bwd)
    o = o_f + _flip(o_b)
    o = rmsnorm(o, norm_w) * jax.nn.sigmoid(g.astype(F32).reshape(bsz, t_len, HG_HEADS, HG_DV))
    return o.reshape(bsz, t_len, HG_W), sf, sb


def mla_keys(c_kv, k_rope, w_ukv):
    bsz, s_len, _ = c_kv.shape
    kv = (c_kv @ w_ukv).reshape(bsz, s_len, MLA_HEADS, MLA_NOPE + MLA_DV)
    k = jnp.concatenate([kv[..., :MLA_NOPE],
                         jnp.broadcast_to(k_rope[:, :, None, :], (bsz, s_len, MLA_HEADS, MLA_ROPE))], axis=-1)
    return k, kv[..., MLA_NOPE:]


def blocked_attention(q, k, v):
    bsz, t_len, h, dq = q.shape
    nb = t_len // Q_BLOCK
    scale = dq ** -0.5
    qb = q.reshape(bsz, nb, Q_BLOCK, h, dq).transpose(1, 0, 2, 3, 4)

    def one_block(qi):
        s = jnp.einsum('bqhd,bshd->bhqs', qi, k, preferred_element_type=F32) * scale
        p = jax.nn.softmax(s, axis=-1)
        return jnp.einsum('bhqs,bshd->bqhd', p.astype(v.dtype), v)

    o = lax.map(one_block, qb)
    return o.transpose(1, 0, 2, 3, 4).reshape(bsz, t_len, h, v.shape[-1])


def trunk_layer(x, cond, p, gdn_s0, hgrn_s0, rope, ctx_ckv, ctx_kr):
    bsz, t_len, _ = x.shape
    mod = (jax.nn.silu(cond) @ p['w_ada'] + p['b_ada'])[:, None, :]
    sh1, sc1, gt1, sh2, sc2, gt2 = jnp.split(mod, 6, axis=-1)
    h = rmsnorm(x, p['g_pre_mix']) * (1.0 + sc1) + sh1
    (gq, gk, gv, gz, ga, gb, hq, hi, hf, hg, mcq, mckv, mkr) = _split(h @ p['w_in'], IN_SIZES)
    o_gdn, gsf, gsb = gdn_mixer(gq, gk, gv, gz, ga, gb, p['gdn_conv_w'], p['gdn_a_log'],
                                p['gdn_dt_bias'], p['gdn_norm_w'], gdn_s0[:, 0], gdn_s0[:, 1])
    o_hg, hsf, hsb = hgrn_mixer(hq, hi, hf, hg, p['hgrn_lb'], p['hgrn_norm_w'],
                                hgrn_s0[:, 0], hgrn_s0[:, 1])
    q = (rmsnorm(mcq, p['mla_q_norm_w']) @ p['mla_w_uq']).reshape(bsz, t_len, MLA_HEADS, MLA_QK)
    c_kv = rmsnorm(mckv, p['mla_kv_norm_w'])
    if rope is None:
        k, v = mla_keys(c_kv, mkr, p['mla_w_ukv'])
    else:
        cos, sin = rope
        q = jnp.concatenate([q[..., :MLA_NOPE],
                             apply_rope(q[..., MLA_NOPE:], cos[:, None], sin[:, None])], axis=-1)
        k_lat, v_lat = mla_keys(c_kv, apply_rope(mkr, cos, sin), p['mla_w_ukv'])
        k_ctx, v_ctx = mla_keys(ctx_ckv, ctx_kr, p['mla_w_ukv'])
        k = jnp.concatenate([k_lat, k_ctx], axis=1)
        v = jnp.concatenate([v_lat, v_ctx], axis=1)
    o_mla = blocked_attention(q, k, v).reshape(bsz, t_len, MLA_W)
    mix = jnp.concatenate([o_gdn.astype(x.dtype), o_hg.astype(x.dtype), o_mla.astype(x.dtype)], axis=-1)
    x = x + gt1 * rmsnorm(mix @ p['w_out'], p['g_post_mix'])
    h = rmsnorm(x, p['g_pre_ffn']) * (1.0 + sc2) + sh2
    ff_a, ff_b = jnp.split(h @ p['w_ffn_in'], 2, axis=-1)
    x = x + gt2 * rmsnorm((jax.nn.silu(ff_a) * ff_b) @ p['w_ffn_out'], p['g_post_ffn'])
    return x, jnp.stack([gsf, gsb], axis=1), jnp.stack([hsf, hsb], axis=1), c_kv, mkr


def setup_inputs(seed: int = 0) -> dict:
    key = jax.random.key(seed)
    ks = jax.random.split(key, 32)

    def nrm(k, shape, s):
        return jax.random.normal(k, shape, F32) * s

    def gain(k, shape):
        return 1.0 + 0.05 * jax.random.normal(k, shape, F32)

    dt = jnp.exp(jax.random.uniform(ks[18], (DEPTH, 2, GDN_HEADS), F32, math.log(1e-3), math.log(1e-1)))
    return {
        'x_prompt': nrm(ks[0], (BATCH, SEQ, D_MODEL), 1.0),
        'x_sample': nrm(ks[1], (DEC_BATCH, DEC_SEQ, D_MODEL), 1.0),
        'cache_mla_ckv': nrm(ks[2], (DEC_BATCH, DEPTH, PAST_LEN, MLA_KV_RANK), 1.0),
        'cache_mla_krope': nrm(ks[3], (DEC_BATCH, DEPTH, PAST_LEN, MLA_ROPE), 1.0),
        'state_gdn': nrm(ks[4], (DEC_BATCH, DEPTH, 2, GDN_HEADS, GDN_DK, GDN_DV), 0.3),
        'state_hgrn': nrm(ks[5], (DEC_BATCH, DEPTH, 2, HG_HEADS, HG_DK, HG_DV), 0.5),
        'c': nrm(ks[6], (DEC_BATCH, D_MODEL), 1.0),
        'c_ctx': nrm(ks[7], (D_MODEL,), 1.0),
        'w_ada': nrm(ks[8], (DEPTH, D_MODEL, 6 * D_MODEL), 0.5 * D_MODEL ** -0.5),
        'b_ada': nrm(ks[9], (DEPTH, 6 * D_MODEL), 0.02),
        'g_pre_mix': gain(ks[10], (DEPTH, D_MODEL)),
        'g_post_mix': gain(ks[11], (DEPTH, D_MODEL)),
        'g_pre_ffn': gain(ks[12], (DEPTH, D_MODEL)),
        'g_post_ffn': gain(ks[13], (DEPTH, D_MODEL)),
        'w_in': nrm(ks[14], (DEPTH, D_MODEL, IN_DIM), D_MODEL ** -0.5),
        'w_out': nrm(ks[15], (DEPTH, MIX_W, D_MODEL), MIX_W ** -0.5),
        'gdn_conv_w': nrm(ks[16], (DEPTH, GDN_CONV_CH, GDN_CONV), GDN_CONV ** -0.5),
        'gdn_a_log': jnp.log(jax.random.uniform(ks[17], (DEPTH, 2, GDN_HEADS), F32, 1.0, 16.0)),
        'gdn_dt_bias': dt + jnp.log(-jnp.expm1(-dt)),
        'gdn_norm_w': gain(ks[19], (DEPTH, GDN_DV)),
        'hgrn_lb': nrm(ks[20], (DEPTH, HG_QK), 1.0),
        'hgrn_norm_w': gain(ks[21], (DEPTH, HG_DV)),
        'mla_q_norm_w': gain(ks[22], (DEPTH, MLA_Q_RANK)),
        'mla_w_uq': nrm(ks[23], (DEPTH, MLA_Q_RANK, MLA_HEADS * MLA_QK), MLA_Q_RANK ** -0.5),
        'mla_kv_norm_w': gain(ks[24], (DEPTH, MLA_KV_RANK)),
        'mla_w_ukv': nrm(ks[25], (DEPTH, MLA_KV_RANK, MLA_HEADS * (MLA_NOPE + MLA_DV)), MLA_KV_RANK ** -0.5),
        'w_ffn_in': nrm(ks[26], (DEPTH, D_MODEL, 2 * D_FF), D_MODEL ** -0.5),
        'w_ffn_out': nrm(ks[27], (DEPTH, D_FF, D_MODEL), D_FF ** -0.5),
    }


def reference(x_prompt, x_sample, cache_mla_ckv, cache_mla_krope, state_gdn, state_hgrn, c, c_ctx,
              w_ada, b_ada, g_pre_mix, g_post_mix, g_pre_ffn, g_post_ffn, w_in, w_out,
              gdn_conv_w, gdn_a_log, gdn_dt_bias, gdn_norm_w, hgrn_lb, hgrn_norm_w,
              mla_q_norm_w, mla_w_uq, mla_kv_norm_w, mla_w_ukv, w_ffn_in, w_ffn_out):
    gamma = jax.nn.softmax(hgrn_lb.astype(F32), axis=0)
    lower_bounds = jnp.cumsum(gamma, axis=0) - gamma[0:1]

    def layer_params(l):
        return {'w_ada': w_ada[l], 'b_ada': b_ada[l], 'g_pre_mix': g_pre_mix[l],
                'g_post_mix': g_post_mix[l], 'g_pre_ffn': g_pre_ffn[l], 'g_post_ffn': g_post_ffn[l],
                'w_in': w_in[l], 'w_out': w_out[l], 'gdn_conv_w': gdn_conv_w[l],
                'gdn_a_log': gdn_a_log[l], 'gdn_dt_bias': gdn_dt_bias[l], 'gdn_norm_w': gdn_norm_w[l],
                'hgrn_lb': lower_bounds[l], 'hgrn_norm_w': hgrn_norm_w[l],
                'mla_q_norm_w': mla_q_norm_w[l], 'mla_w_uq': mla_w_uq[l],
                'mla_kv_norm_w': mla_kv_norm_w[l], 'mla_w_ukv': mla_w_ukv[l],
                'w_ffn_in': w_ffn_in[l], 'w_ffn_out': w_ffn_out[l]}

    bp = x_prompt.shape[0]
    zero_gdn = jnp.zeros((bp, 2, GDN_HEADS, GDN_DK, GDN_DV), F32)
    zero_hg = jnp.zeros((bp, 2, HG_HEADS, HG_DK, HG_DV), F32)
    xp = x_prompt
    ckv_list, kr_list, gdn_list, hg_list = [], [], [], []
    for l in range(DEPTH):
        xp, s_g, s_h, ckv_l, kr_l = trunk_layer(xp, c_ctx[None, :], layer_params(l),
                                                zero_gdn, zero_hg, None, None, None)
        ckv_list.append(ckv_l)
        kr_list.append(kr_l)
        gdn_list.append(s_g)
        hg_list.append(s_h)
    new_mla_ckv = jnp.stack(ckv_list, axis=1).astype(x_prompt.dtype)
    new_mla_krope = jnp.stack(kr_list, axis=1).astype(x_prompt.dtype)
    new_state_gdn = jnp.stack(gdn_list, axis=1).astype(x_prompt.dtype)
    new_state_hgrn = jnp.stack(hg_list, axis=1).astype(x_prompt.dtype)

    rope = axial_rope(x_sample.shape[1])
    xs = x_sample
    for l in range(DEPTH):
        xs, _, _, _, _ = trunk_layer(xs, c, layer_params(l), state_gdn[:, l], state_hgrn[:, l],
                                     rope, cache_mla_ckv[:, l], cache_mla_krope[:, l])

    return (xp, xs, new_mla_ckv, new_mla_krope, new_state_gdn, new_state_hgrn)
```

```python
import math
from contextlib import ExitStack

import numpy as np
import concourse.bass as bass
import concourse.mybir as mybir
from concourse.bass_utils import run_bass_kernel_spmd

F32 = mybir.dt.float32
BF16 = mybir.dt.bfloat16
AF = mybir.ActivationFunctionType
ALU = mybir.AluOpType
AX = mybir.AxisListType

D = 1024
L = 2
EPS = 1e-6
IN_DIM = 3024
DFF = 2816
NEG = -30000.0
SCALE = 192 ** -0.5


class Buf:
    __slots__ = ("name", "w", "r", "excl", "rb")

    def __init__(self, name="", excl=False):
        self.name = name
        self.w = None
        self.r = []
        self.rb = 0
        self.excl = excl


class Tl:
    __slots__ = ("ap", "b")

    def __init__(self, ap, b=None):
        self.ap = ap
        self.b = b if b is not None else Buf()

    def __getitem__(self, k):
        return self.ap[k]


class Sched:
    def __init__(self, nc):
        self.nc = nc
        self.engs = {"pe": nc.tensor, "dve": nc.vector, "act": nc.scalar, "pool": nc.gpsimd, "sp": nc.sync}
        self.sems, self.cnt = {}, {}
        self.seen = {k: {} for k in self.engs}
        for k in self.engs:
            self.sems[k] = nc.alloc_semaphore("s_" + k)
            self.cnt[k] = 0
        self.free_dma = []
        import os
        self.limit = int(os.environ.get('KLIMIT', '100000000'))
        self.ninst = 0
        self.nwait = 0
        self.ndma = 0

    def _deps(self, reads, writes, eng=None):
        deps = {}
        for b in reads:
            d = b.w
            if d is not None and deps.get(d[0], 0) < d[1]:
                deps[d[0]] = d[1]
            if b.excl:
                for d in b.r:
                    if d[0] != eng and deps.get(d[0], 0) < d[1]:
                        deps[d[0]] = d[1]
        for b in writes:
            d = b.w
            if d is not None and deps.get(d[0], 0) < d[1]:
                deps[d[0]] = d[1]
            for d in b.r:
                if deps.get(d[0], 0) < d[1]:
                    deps[d[0]] = d[1]
        return deps

    def _wait(self, eng, deps):
        e = self.engs[eng]
        seen = self.seen[eng]
        for k, c in deps.items():
            if eng == "pe" and k == "pe":
                continue
            if seen.get(k, 0) >= c:
                continue
            if k not in self.engs:
                c = self.cnt[k]
            e.wait_ge(self.sems[k], c)
            seen[k] = c
            self.nwait += 1

    def _mark(self, key, reads, writes):
        d = (key, self.cnt[key])
        for b in writes:
            b.w = d
            b.r = []
        for b in reads:
            r = b.r
            r.append(d)
            if len(r) > 48:
                m = {}
                for k, c in r:
                    if m.get(k, 0) < c:
                        m[k] = c
                b.r = list(m.items())

    def op(self, eng, fn, reads=(), writes=(), rb=0):
        if self.ninst >= self.limit:
            return
        reads = [t.b if isinstance(t, Tl) else t for t in reads]
        writes = [t.b if isinstance(t, Tl) else t for t in writes]
        self._wait(eng, self._deps(reads, writes, eng))
        if eng == "pe" and writes and writes[0].excl:
            b = writes[0]
            if b.rb != rb:
                d = b.w
                if d is not None and d[0] == "pe" and self.seen["pe"].get("pe", 0) < d[1]:
                    self.engs["pe"].wait_ge(self.sems["pe"], d[1])
                    self.seen["pe"]["pe"] = d[1]
                    self.nwait += 1
                b.rb = rb
        ins = fn(self.engs[eng])
        self.cnt[eng] += 1
        ins.then_inc(self.sems[eng], 1)
        self._mark(eng, reads, writes)
        self.ninst += 1

    def dma(self, q, out, in_, reads=(), writes=(), key=None):
        reads = [t.b if isinstance(t, Tl) else t for t in reads]
        writes = [t.b if isinstance(t, Tl) else t for t in writes]
        if self.ninst >= self.limit:
            return
        if key not in self.sems:
            self.sems[key] = self.nc.alloc_semaphore("d_" + key)
            self.cnt[key] = 0
        self._wait(q, self._deps(reads, writes))
        ins = self.engs[q].dma_start(out=out, in_=in_)
        self.cnt[key] += 16
        ins.then_inc(self.sems[key], 16)
        self._mark(key, reads, writes)
        self.ninst += 1
        self.ndma += 1

    def barrier(self):
        for eng in self.engs:
            deps = {k: c for k, c in self.cnt.items() if c > 0}
            e = self.engs[eng]
            seen = self.seen[eng]
            for k, c in deps.items():
                if seen.get(k, 0) >= c:
                    continue
                e.wait_ge(self.sems[k], c)
                seen[k] = c
                self.nwait += 1


class Seq:
    def __init__(self, name, T, ci, sample):
        self.name, self.T, self.ci, self.sample = name, T, ci, sample
        self.Tk = T + (256 if sample else 0)
        self.NB = T // 128


class KB:
    def __init__(self, Ts=4096, Tp=256, dbg=()):
        self.Ts, self.Tp = Ts, Tp
        self.dbg = set(dbg)
        nc = self.nc = bass.Bass("TRN2", target_bir_lowering=False)
        self.S = Sched(nc)
        self.seqs = [Seq("s", Ts, 0, True), Seq("p0", Tp, 1, False), Seq("p1", Tp, 1, False)]
        self.uid = 0
        self.dkeys = {}
        self.dkn = {}
        self._declare_io()
        self._build()

    def din(self, name, shape, dt=F32):
        return Tl(self.nc.dram_tensor(name, list(shape), dt, kind="ExternalInput").ap())

    def dout(self, name, shape, dt=F32):
        return Tl(self.nc.dram_tensor(name, list(shape), dt, kind="ExternalOutput").ap())

    def dscr(self, name, shape, dt=F32):
        kind = "ExternalOutput" if name in self.dbg else "Internal"
        return Tl(self.nc.dram_tensor(name, list(shape), dt, kind=kind).ap())

    def sb(self, shape, dt=F32, name=None, stack=None):
        self.uid += 1
        name = f"{name or 't'}_{self.uid}"
        g = self.nc.sbuf_tensor(name, list(shape), dt)
        h = (stack or self.gstack).enter_context(g)
        return Tl(h.ap())

    def sbn(self, n, shape, dt=F32, name=None, stack=None):
        return [self.sb(shape, dt, name, stack) for _ in range(n)]

    def ps(self):
        i = self.ps_i
        self.ps_i = (i + 1) % 8
        return self.psb[i]

    def dmak(self, t, q):
        k = (id(t.b), q)
        if k not in self.dkeys:
            n = self.dkn.get(q, 0)
            self.dkn[q] = n + 1
            self.dkeys[k] = (f"{q}{n}", t.b)
        return self.dkeys[k][0]

    def phase_end(self):
        self.S.barrier()
        self.dkeys = {}
        self.dkn = {}

    def load(self, q, dst, src, dstap=None, srcap=None):
        self.S.dma(q, dstap if dstap is not None else dst.ap, srcap if srcap is not None else src.ap,
                   reads=[src], writes=[dst], key=self.dmak(dst, q))

    def store(self, q, dst, src, dstap=None, srcap=None):
        self.S.dma(q, dstap if dstap is not None else dst.ap, srcap if srcap is not None else src.ap,
                   reads=[src], writes=[dst], key=self.dmak(src, q))

    def _declare_io(self):
        Ts, Tp = self.Ts, self.Tp
        I = self.I = {}
        I["x_s"] = self.din("x_s", [Ts, D])
        I["x_p0"] = self.din("x_p0", [Tp, D])
        I["x_p1"] = self.din("x_p1", [Tp, D])
        I["cond"] = self.din("cond", [128, 8, 2])
        I["ctx_ckvT"] = self.din("ctx_ckvT", [L, 128, 2, 256])
        I["ctx_krT"] = self.din("ctx_krT", [L, 64, 256])
        I["st_gdn"] = self.din("st_gdn", [L, 2, 128, 2, 64])
        I["st_hg"] = self.din("st_hg", [L, 2, 128, 2, 64])
        I["w_ada"] = self.din("w_ada", [L, 128, 8, 6 * D])
        I["b_ada"] = self.din("b_ada", [L, 128, 48])
        I["gains"] = self.din("gains", [L, 128, 4, 8])
        I["w_in"] = self.din("w_in", [L, 128, 8, IN_DIM])
        I["w_out"] = self.din("w_out", [L, 128, 8, D])
        I["w_f1"] = self.din("w_f1", [L, 128, 8, 2 * DFF])
        I["w_f2"] = self.din("w_f2", [L, 128, 22, D])
        I["w_uq"] = self.din("w_uq", [L, 128, 3, 768])
        I["w_ukv"] = self.din("w_ukv", [L, 128, 2, 1024])
        I["convd"] = self.din("convd", [L, 128, 6, 5, 128])
        I["gdn_ad"] = self.din("gdn_ad", [L, 128, 2, 8])
        I["normw"] = self.din("normw", [L, 128, 2, 64])
        I["lbraw"] = self.din("lbraw", [128, 2, 2])
        I["mlanw"] = self.din("mlanw", [L, 128, 5])
        I["kvnw_rep"] = self.din("kvnw_rep", [L, 128, 256])
        I["consts"] = self.din("consts", [128, 13, 128])
        I["scanmask"] = self.din("scanmask", [128, 256])
        I["ropeT"] = self.din("ropeT", [2, 64, Ts])
        O = self.O = {}
        O["y_s"] = self.dout("y_s", [Ts, D])
        O["y_p0"] = self.dout("y_p0", [Tp, D])
        O["y_p1"] = self.dout("y_p1", [Tp, D])
        O["ckv"] = self.dout("ckv", [2, L, Tp, 256])
        O["kr"] = self.dout("kr", [2, L, Tp, 64])
        O["sg"] = self.dout("sg", [2, L, 2, 4, 64, 64])
        O["sh"] = self.dout("sh", [2, L, 2, 4, 64, 64])
        Sc = self.Sc = {}
        Sc["gvec"] = self.dscr("gvec", [4, D])
        for s in self.seqs:
            n, T, Tk = s.name, s.T, s.Tk
            Sc["xmid_" + n] = self.dscr("xmid_" + n, [T, D])
            Sc["actT_" + n] = self.dscr("actT_" + n, [DFF, T], BF16)
            Sc["x1_" + n] = self.dscr("x1_" + n, [T, D])
            Sc["qkvT_" + n] = self.dscr("qkvT_" + n, [768, T], BF16)
            Sc["zab_" + n] = self.dscr("zab_" + n, [T, 272])
            Sc["hqT_" + n] = self.dscr("hqT_" + n, [256, T])
            Sc["hfT_" + n] = self.dscr("hfT_" + n, [512, T])
            Sc["hv_" + n] = self.dscr("hv_" + n, [T, 256], BF16)
            Sc["hg_" + n] = self.dscr("hg_" + n, [T, 256])
            Sc["QnT_" + n] = self.dscr("QnT_" + n, [4, 128, T], BF16)
            Sc["QrT_" + n] = self.dscr("QrT_" + n, [4, 64, T], BF16)
            Sc["KnT_" + n] = self.dscr("KnT_" + n, [4, 128, Tk], BF16)
            Sc["krT_" + n] = self.dscr("krT_" + n, [64, Tk], BF16)
            Sc["V_" + n] = self.dscr("V_" + n, [Tk, 512], BF16)
            Sc["mixT_" + n] = self.dscr("mixT_" + n, [1024, T], BF16)
            Sc["gsh_" + n] = self.dscr("gsh_" + n, [T // 128, 128, 1024], BF16)
            for d in range(2):
                Sc[f"og{d}_" + n] = self.dscr(f"og{d}_" + n, [T, 256])
                Sc[f"oh{d}_" + n] = self.dscr(f"oh{d}_" + n, [T, 256])

    def _build(self):
        nc, S = self.nc, self.S
        with ExitStack() as gst:
            self.gstack = gst
            pall = gst.enter_context(nc.psum_tensor("psall", [128, 8, 512], F32)).ap()
            self.pall = pall
            self.psb = [Tl(pall[:, i, :], Buf(f'ps{i}', excl=True)) for i in range(8)]
            self.ps_i = 0
            self._consts()
            for l in range(L):
                self.l = l
                self.phase0(l)
                if "stop0" in self.dbg:
                    break
                self.phaseA(l)
                if "stopA" in self.dbg:
                    break
                if "skipB" in self.dbg:
                    with ExitStack() as st:
                        z = self.sb([128, 512], BF16, "zz", st)
                        S.op("pool", lambda e: e.memset(z.ap, 0.0), writes=[z])
                        for s_ in self.seqs:
                            for c in range(4):
                                for t0 in range(0, s_.T, 512):
                                    n_ = min(512, s_.T - t0)
                                    self.store("sp", self.Sc["mixT_" + s_.name], z, dstap=self.Sc["mixT_" + s_.name].ap[c * 128:(c + 1) * 128, t0:t0 + n_], srcap=z.ap[:, 0:n_])
                        self.phase_end()
                else:
                    self.phaseB(l)
                if "stopB" in self.dbg:
                    break
                self.phaseC(l)
                if "stopC" in self.dbg:
                    break
                self.phaseD(l)
                if "stopD0" in self.dbg:
                    break
            self.phase_end()

    def _consts(self):
        S = self.S
        C = self.C = {}
        cf = self.sb([128, 13, 128], F32, "constf")
        self.load("sp", cf, self.I["consts"])
        names = ["ident", "ones", "blockones", "Lf", "Lb", "negf", "negb", "noteye", "hmfA", "hmbA", "RT", "hmfB", "hmbB"]
        for i, n in enumerate(names):
            C[n] = Tl(cf.ap[:, i, :], cf.b)
        cb = self.sb([128, 13, 128], BF16, "constb")
        S.op("dve", lambda e: e.tensor_copy(cb.ap, cf.ap), reads=[cf], writes=[cb])
        for i, n in enumerate(names):
            C[n + "_b"] = Tl(cb.ap[:, i, :], cb.b)
        m05 = self.sb([128, 512], F32, "m05")
        p05 = self.sb([128, 8], F32, "p05")
        S.op("pool", lambda e: e.memset(m05.ap, -0.5), writes=[m05])
        S.op("pool", lambda e: e.memset(p05.ap, 0.5), writes=[p05])
        C["m05"], C["p05"] = m05, p05
        epsb = self.sb([128, 1], F32, "epsb")
        S.op("pool", lambda e: e.memset(epsb.ap, EPS), writes=[epsb])
        C["epsb"] = epsb
        sm = self.sb([128, 256], F32, "scanmask")
        self.load("sp", sm, self.I["scanmask"])
        C["scanmask"] = sm
        cond = self.sb([128, 8, 2], F32, "cond")
        self.load("sp", cond, self.I["cond"])
        sc = self.sb([128, 8, 2], BF16, "scond")
        S.op("act", lambda e: e.activation(sc.ap, cond.ap, AF.Silu), reads=[cond], writes=[sc])
        C["scond"] = sc
        self.modF = self.sb([128, 48, 2], F32, "modF")
        self.scale1 = self.sb([128, 8, 2], F32, "scale1")
        self.scale2 = self.sb([128, 8, 2], F32, "scale2")
        self.gg = self.sb([128, 2, 2, 8], F32, "gg")
        self.lbv = self.sb([128, 2], F32, "lbv")
        self.oml = self.sb([128, 2], F32, "oml")
        self.noml = self.sb([128, 2], F32, "noml")
        self.qk2 = {s.name: self.sb([128, 2, 4], F32, "qk2") for s in self.seqs}

    def rstd(self, out, src, mul, n, parts=128):
        S = self.S
        src_ap = src.ap[0:parts, 0:n] if isinstance(src, Tl) else src[0:parts, 0:n]
        S.op("act", lambda e: e.activation(out.ap[0:parts, 0:n], src_ap, AF.Ln, bias=self.C["epsb"].ap[0:parts, :], scale=mul), reads=([src] if isinstance(src, Tl) else []) + [self.C["epsb"]], writes=[out])
        S.op("act", lambda e: e.activation(out.ap[0:parts, 0:n], out.ap[0:parts, 0:n], AF.Exp, scale=-0.5), reads=[out], writes=[out])

    def phase0(self, l):
        S, C, I = self.S, self.C, self.I
        with ExitStack() as st:
            wa = self.sbn(2, [128, 8, 1024], BF16, "wada", st)
            bada = self.sb([128, 48], F32, "bada", st)
            gains = self.sb([128, 4, 8], F32, "gains", st)
            self.load("sp", bada, I["b_ada"], srcap=I["b_ada"].ap[l])
            self.load("sp", gains, I["gains"], srcap=I["gains"].ap[l])
            mp = self.ps()
            mpv = mp.ap[:, 0:96].rearrange("p (j c) -> p j c", c=2)
            for g in range(6):
                w = wa[g % 2]
                self.load("pool", w, I["w_ada"], srcap=I["w_ada"].ap[l, :, :, g * 1024:(g + 1) * 1024])
                for j in range(8):
                    for k in range(8):
                        S.op("pe", lambda e: e.matmul(mpv[:, g * 8 + j, :], lhsT=w.ap[:, k, j * 128:(j + 1) * 128], rhs=C["scond"].ap[:, k, :],
                                                      start=(k == 0), stop=(k == 7)), reads=[w, C["scond"]], writes=[mp])
            modF = self.modF
            S.op("dve", lambda e: e.tensor_tensor(modF.ap, mpv, bada.ap.unsqueeze(2).to_broadcast([128, 48, 2]), ALU.add),
                 reads=[mp, bada], writes=[modF])
            for (dst, mi, gi) in ((self.scale1, 1, 0), (self.scale2, 4, 2)):
                S.op("dve", lambda e: e.scalar_tensor_tensor(dst.ap, modF.ap[:, mi * 8:(mi + 1) * 8, :], 1.0,
                                                             gains.ap[:, gi, :].unsqueeze(2).to_broadcast([128, 8, 2]), op0=ALU.add, op1=ALU.mult),
                     reads=[modF, gains], writes=[dst])
            for (w_, mi, gi) in ((0, 2, 1), (1, 5, 3)):
                S.op("dve", lambda e: e.tensor_tensor(self.gg.ap[:, w_].rearrange("p c j -> p j c"), modF.ap[:, mi * 8:(mi + 1) * 8, :],
                                                      gains.ap[:, gi, :].unsqueeze(2).to_broadcast([128, 8, 2]), ALU.mult),
                     reads=[modF, gains], writes=[self.gg])
            lr = self.sb([128, 2, 2], F32, "lbraw", st)
            self.load("sp", lr, I["lbraw"])
            g0 = self.sb([128, 2], F32, "g0", st)
            a_, b_ = (0, 1) if l == 0 else (1, 0)
            S.op("dve", lambda e: e.tensor_tensor(g0.ap, lr.ap[:, a_, :], lr.ap[:, b_, :], ALU.subtract), reads=[lr], writes=[g0])
            S.op("act", lambda e: e.activation(g0.ap, g0.ap, AF.Sigmoid), reads=[g0], writes=[g0])
            if l == 0:
                S.op("dve", lambda e: e.tensor_tensor(self.lbv.ap, g0.ap, g0.ap, ALU.subtract), reads=[g0], writes=[self.lbv])
            else:
                S.op("dve", lambda e: e.tensor_copy(self.lbv.ap, g0.ap), reads=[g0], writes=[self.lbv])
            S.op("dve", lambda e: e.tensor_scalar(self.oml.ap, self.lbv.ap, -1.0, 1.0, op0=ALU.mult, op1=ALU.add), reads=[self.lbv], writes=[self.oml])
            S.op("dve", lambda e: e.tensor_scalar(self.noml.ap, self.oml.ap, -1.0, None, op0=ALU.mult), reads=[self.oml], writes=[self.noml])
            self.phase_end()

    def norm_transpose(self, st_tiles, xsrc_fn, nblk, scale, bias_j0, ci, hT, banks=None):
        S, C = self.S, self.C
        xn_s, junk_s, ss, rs = st_tiles
        hps = banks if banks is not None else [self.ps() for _ in range(4)]
        hv = [p.ap.bitcast(BF16).rearrange("p (c t) -> p c t", c=2) for p in hps]
        for blk in range(nblk):
            xt = xsrc_fn(blk)
            jk = junk_s[blk % 2]
            S.op("act", lambda e: e.activation(jk.ap, xt.ap, AF.Square, accum_out=ss.ap[:, blk:blk + 1]), reads=[xt], writes=[jk, ss])
        self.rstd(rs, ss, 1.0 / D, nblk)
        for blk in range(nblk):
            xt = xsrc_fn(blk)
            xn = xn_s[blk % 2]
            if blk % 2 == 0:
                S.op("dve", lambda e: e.tensor_scalar(xn.ap, xt.ap, rs.ap[:, blk:blk + 1], None, op0=ALU.mult), reads=[xt, rs], writes=[xn])
            else:
                S.op("act", lambda e: e.activation(xn.ap, xt.ap, AF.Copy, scale=rs.ap[:, blk:blk + 1]), reads=[xt, rs], writes=[xn])
            for c in range(8):
                S.op("pe", lambda e: e.transpose(hv[c // 2][:, c % 2, blk * 128:(blk + 1) * 128], xn.ap[:, c * 128:(c + 1) * 128], C["ident_b"].ap),
                     reads=[xn, C["ident_b"]], writes=[hps[c // 2]])
        n = nblk * 128
        for c in range(8):
            src = hv[c // 2][:, c % 2, 0:n]
            sc_ap = scale.ap[:, c, ci:ci + 1]
            b_ap = self.modF.ap[:, bias_j0 + c, ci:ci + 1]
            if c % 2 == 0:
                S.op("act", lambda e: e.activation(hT.ap[:, c, 0:n], src, AF.Identity, scale=sc_ap, bias=b_ap),
                     reads=[hps[c // 2], scale, self.modF], writes=[hT])
            else:
                S.op("dve", lambda e: e.tensor_scalar(hT.ap[:, c, 0:n], src, sc_ap, b_ap, op0=ALU.mult, op1=ALU.add),
                     reads=[hps[c // 2], scale, self.modF], writes=[hT])

    def phaseA(self, l):
        S, C, I, Sc, O = self.S, self.C, self.I, self.Sc, self.O
        with ExitStack() as st:
            Win = self.sb([128, 8, IN_DIM], BF16, "Win", st)
            for k in range(0, 8, 2):
                self.S.dma("pool", Win.ap[:, k:k + 2, :], I["w_in"].ap[l, :, k:k + 2, :], reads=[I["w_in"]], writes=[Win], key=f"win{k}")
            Wuq = self.sb([128, 3, 768], BF16, "Wuq", st)
            self.load("pool", Wuq, I["w_uq"], srcap=I["w_uq"].ap[l])
            Wukv = self.sb([128, 2, 1024], BF16, "Wukv", st)
            self.load("pool", Wukv, I["w_ukv"], srcap=I["w_ukv"].ap[l])
            nw = self.sb([128, 5], F32, "mlanw", st)
            self.load("sp", nw, I["mlanw"], srcap=I["mlanw"].ap[l])
            kvrep = self.sb([128, 256], F32, "kvrep", st)
            self.load("sp", kvrep, I["kvnw_rep"], srcap=I["kvnw_rep"].ap[l])
            xt_s = self.sbn(4, [128, D], F32, "xt", st)
            nt_tiles = (self.sbn(2, [128, D], BF16, "xn", st), self.sbn(1, [128, D], BF16, "junk", st) * 2,
                        self.sb([128, 4], F32, "ss", st), self.sb([128, 4], F32, "rs", st))
            hT = self.sb([128, 8, 512], BF16, "hT", st)
            qkv_st = self.sbn(1, [128, 6, 512], BF16, "qkvst", st) * 2
            hq_st = self.sbn(1, [128, 2, 512], F32, "hqst", st) * 2
            hf_st = self.sbn(1, [128, 2, 512], F32, "hfst", st) * 2
            zab_st = self.sbn(1, [128, 4, 272], F32, "zabst", st) * 2
            hv_st = self.sbn(1, [128, 4, 256], BF16, "hvst", st) * 2
            hg_st = self.sbn(1, [128, 4, 256], F32, "hgst", st) * 2
            mla2 = self.sbn(2, [128, 6, 512], F32, "mla", st)
            sq = self.sb([128, 3, 512], F32, "sq", st)
            rq = self.sbn(2, [128, 512], F32, "rq", st)
            cn = self.sb([128, 5, 512], BF16, "cn", st)
            Qn_st = self.sbn(1, [128, 4, 512], BF16, "Qnst", st) * 2
            Qr_st = self.sbn(1, [64, 4, 512], BF16, "Qrst", st) * 2
            Kn_st = self.sbn(1, [128, 4, 512], BF16, "Knst", st) * 2
            kr_st = self.sbn(1, [64, 512], BF16, "krst", st) * 2
            V_st = self.sbn(1, [128, 4, 512], BF16, "Vst", st) * 2
            xr = self.sbn(1, [64, 512], BF16, "xr", st) * 2
            t1 = self.sbn(1, [64, 512], F32, "t1", st) * 2
            t2 = self.sbn(1, [64, 512], F32, "t2", st) * 2
            sqb = self.sb([128, 4, 512], BF16, "sqb", st)
            sqr = self.sb([64, 4, 512], BF16, "sqr", st)
            mx = self.sb([128, 1], F32, "mx", st)
            pck = self.sbn(1, [128, 4, 320], F32, "pck", st) * 2
            pss = self.sb([128, 4], F32, "pss", st)
            prs = self.sb([128, 4], F32, "prs", st)
            ctxc = self.sb([128, 2, 256], BF16, "ctxc", st)
            ctxk = self.sb([64, 256], BF16, "ctxk", st)
            ropet = self.sb([64, 2, 512], F32, "ropet", st)
            cos, sin = ropet.ap[:, 0, :], ropet.ap[:, 1, :]
            ev = [0]
            cur = [None]
            rot = {"a": 5, "b": 1}

            def ps1():
                rot["a"] = (rot["a"] + 1) % 6
                return self.psb[rot["a"]]

            def ps2():
                rot["b"] ^= 1
                return self.psb[6 + rot["b"]]

            def PS():
                return cur[0]()

            def drive(chains):
                chains = list(chains)
                while chains:
                    for c in list(chains):
                        cur[0] = c[1]
                        try:
                            next(c[0])
                        except StopIteration:
                            chains.remove(c)

            def evac(dst_ap, dst_tl, src_ps, func=None):
                ev[0] += 1
                if ev[0] % 2:
                    S.op("act", lambda e: e.copy(dst_ap, src_ps.ap if isinstance(src_ps, Tl) else src_ps[0]), reads=[src_ps if isinstance(src_ps, Tl) else src_ps[1]], writes=[dst_tl])
                else:
                    S.op("dve", lambda e: e.tensor_copy(dst_ap, src_ps.ap if isinstance(src_ps, Tl) else src_ps[0]), reads=[src_ps if isinstance(src_ps, Tl) else src_ps[1]], writes=[dst_tl])

            def fm_proj(col0, M, n, W=Win, rhs=hT, nk=8):
                p = PS()
                for k in range(nk):
                    S.op("pe", lambda e: e.matmul(p.ap[0:M, 0:n], lhsT=W.ap[:, k, col0:col0 + M], rhs=rhs.ap[:, k, 0:n], start=(k == 0), stop=(k == nk - 1)),
                         reads=[W, rhs], writes=[p])
                return p

            def maxnorm(sqn_tl, sqr_tl, n, col):
                for h in range(4):
                    p = PS()
                    S.op("pe", lambda e: e.matmul(p.ap[:, 0:n], lhsT=C["ones_b"].ap, rhs=sqn_tl.ap[:, h, 0:n], start=True, stop=False), reads=[C["ones_b"], sqn_tl], writes=[p])
                    rr = sqr_tl.ap[:, h, 0:n] if len(sqr_tl.ap.shape) == 3 else sqr_tl.ap[:, 0:n]
                    S.op("pe", lambda e: e.matmul(p.ap[:, 0:n], lhsT=C["ones_b"].ap[0:64, :], rhs=rr, start=False, stop=True), reads=[C["ones_b"], sqr_tl], writes=[p])
                    S.op("dve", lambda e: e.tensor_reduce(mx.ap, p.ap[:, 0:n], AX.X, ALU.max), reads=[p], writes=[mx])
                    S.op("dve", lambda e: e.tensor_tensor(qk2.ap[:, col, h:h + 1], qk2.ap[:, col, h:h + 1], mx.ap, ALU.max), reads=[mx, qk2], writes=[qk2])

            def rope_apply(dst_ap, dst_tl, src_ap, src_tl, n, tok0, slot):
                x_, a_, b_ = xr[slot], t1[slot], t2[slot]
                S.op("act", lambda e: e.copy(x_.ap[:, 0:n], src_ap), reads=[src_tl], writes=[x_])
                rp = PS()
                S.op("pe", lambda e: e.matmul(rp.ap[0:64, 0:n], lhsT=C["RT_b"].ap[0:64, 0:64], rhs=x_.ap[:, 0:n], start=True, stop=True), reads=[C["RT_b"], x_], writes=[rp])
                S.op("dve", lambda e: e.tensor_tensor(a_.ap[:, 0:n], rp.ap[0:64, 0:n], sin[:, 0:n], ALU.mult), reads=[rp, ropet], writes=[a_])
                S.op("dve", lambda e: e.tensor_tensor(b_.ap[:, 0:n], src_ap, cos[:, 0:n], ALU.mult), reads=[src_tl, ropet], writes=[b_])
                S.op("pool", lambda e: e.tensor_tensor(dst_ap, a_.ap[:, 0:n], b_.ap[:, 0:n], ALU.add), reads=[a_, b_], writes=[dst_tl])

            def kv_from_cn(s, n, tok0, it, cn_tl, kc0, kr_src_ap, kr_src_tl, do_rope, key_off):
                nm = s.name
                Kn, krs, Vs = Kn_st[it % 2], kr_st[it % 2], V_st[it % 2]
                for h in range(4):
                    p = PS()
                    for k in range(2):
                        S.op("pe", lambda e: e.matmul(p.ap[:, 0:n], lhsT=Wukv.ap[:, k, h * 128:(h + 1) * 128], rhs=cn_tl.ap[:, kc0 + k, 0:n], start=(k == 0), stop=(k == 1)),
                             reads=[Wukv, cn_tl], writes=[p])
                    evac(Kn.ap[:, h, 0:n], Kn, p.ap[:, 0:n] if False else (p.ap[:, 0:n], p))
                self.store("sp", Sc["KnT_" + nm], Kn, dstap=Sc["KnT_" + nm].ap[:, :, key_off + tok0:key_off + tok0 + n].rearrange("h p t -> p h t"), srcap=Kn.ap[:, :, 0:n])
                if do_rope:
                    rope_apply(krs.ap[:, 0:n], krs, kr_src_ap, kr_src_tl, n, tok0, 0)
                else:
                    S.op("act", lambda e: e.copy(krs.ap[:, 0:n], kr_src_ap), reads=[kr_src_tl], writes=[krs])
                self.store("sp", Sc["krT_" + nm], krs, dstap=Sc["krT_" + nm].ap[:, key_off + tok0:key_off + tok0 + n], srcap=krs.ap[:, 0:n])
                for blk in range(n // 128):
                    p = PS()
                    for k in range(2):
                        S.op("pe", lambda e: e.matmul(p.ap, lhsT=cn_tl.ap[:, kc0 + k, blk * 128:(blk + 1) * 128], rhs=Wukv.ap[:, k, 512:1024], start=(k == 0), stop=(k == 1)),
                             reads=[Wukv, cn_tl], writes=[p])
                    evac(Vs.ap[:, blk, :], Vs, p)
                self.store("sp", Sc["V_" + nm], Vs, dstap=Sc["V_" + nm].ap[key_off + tok0:key_off + tok0 + n, :].rearrange("(b p) n -> p b n", p=128), srcap=Vs.ap[:, 0:n // 128, :])
                S.op("pool", lambda e: e.tensor_tensor(sqb.ap[:, :, 0:n], Kn.ap[:, :, 0:n], Kn.ap[:, :, 0:n], ALU.mult), reads=[Kn], writes=[sqb])
                S.op("pool", lambda e: e.tensor_tensor(sqr.ap[:, 0, 0:n], krs.ap[:, 0:n], krs.ap[:, 0:n], ALU.mult), reads=[krs], writes=[sqr])
                maxnorm(sqb, Tl(sqr.ap[:, 0, :], sqr.b), n, 1)

            for s in self.seqs:
                nm, T, ci = s.name, s.T, s.ci
                qk2 = self.qk2[nm]
                S.op("pool", lambda e: e.memset(qk2.ap, 0.0), writes=[qk2])
                X = I["x_" + nm] if l == 0 else Sc["x1_" + nm]
                TT = min(512, T)
                nblk = TT // 128
                it = 0
                cur[0] = ps2
                if s.sample:
                    self.load("pool", ctxc, I["ctx_ckvT"], srcap=I["ctx_ckvT"].ap[l])
                    self.load("pool", ctxk, I["ctx_krT"], srcap=I["ctx_krT"].ap[l])
                    kv_from_cn(s, 256, 0, 0, ctxc, 0, ctxk.ap, ctxk, False, T)
                def xload(ti_):
                    for blk in range(nblk):
                        self.load("sp", xt_s[blk], X, srcap=X.ap[ti_ * TT + blk * 128:ti_ * TT + (blk + 1) * 128, :])
                xload(0)

                def stage1(ti):
                    tok0 = ti * TT
                    n = TT
                    it = ti
                    mla = mla2[ti % 2]

                    def xsrc(blk):
                        return xt_s[blk]
                    self.norm_transpose(nt_tiles, xsrc, nblk, self.scale1, 0, ci, hT, banks=[PS() for _ in range(4)])
                    yield
                    if ti + 1 < T // TT:
                        xload(ti + 1)
                        yield
                    n = TT
                    qs = qkv_st[it % 2]
                    for c in range(6):
                        p = fm_proj(c * 128, 128, n)
                        yield
                        evac(qs.ap[:, c, 0:n], qs, (p.ap[:, 0:n], p))
                        yield
                    self.store("sp", Sc["qkvT_" + nm], qs, dstap=Sc["qkvT_" + nm].ap[:, tok0:tok0 + n].rearrange("(c p) t -> p c t", p=128), srcap=qs.ap[:, :, 0:n])
                    yield
                    hs = hq_st[it % 2]
                    for c in range(2):
                        p = fm_proj(768 + c * 128, 128, n)
                        yield
                        evac(hs.ap[:, c, 0:n], hs, (p.ap[:, 0:n], p))
                        yield
                    self.store("sp", Sc["hqT_" + nm], hs, dstap=Sc["hqT_" + nm].ap[:, tok0:tok0 + n].rearrange("(c p) t -> p c t", p=128), srcap=hs.ap[:, :, 0:n])
                    yield
                    fs = hf_st[it % 2]
                    for c in range(4):
                        p = fm_proj(1024 + c * 128, 128, n)
                        yield
                        evac(fs.ap[:, c % 2, 0:n], fs, (p.ap[:, 0:n], p))
                        yield
                        if c % 2 == 1:
                            self.store("sp", Sc["hfT_" + nm], fs, dstap=Sc["hfT_" + nm].ap[(c - 1) * 128:(c + 1) * 128, tok0:tok0 + n].rearrange("(c p) t -> p c t", p=128), srcap=fs.ap[:, :, 0:n])
                            yield
                    for c in range(5):
                        p = fm_proj(1536 + c * 128, 128, n)
                        yield
                        evac(mla.ap[:, c, 0:n], mla, (p.ap[:, 0:n], p))
                        yield
                    p = fm_proj(2176, 64, n)
                    yield
                    evac(mla.ap[0:64, 5, 0:n], mla, (p.ap[0:64, 0:n], p))
                    yield
                    zs, hvs, hgs = zab_st[it % 2], hv_st[it % 2], hg_st[it % 2]
                    for blk in range(nblk):
                        p = PS()
                        for k in range(8):
                            S.op("pe", lambda e: e.matmul(p.ap[:, 0:272], lhsT=hT.ap[:, k, blk * 128:(blk + 1) * 128], rhs=Win.ap[:, k, 2240:2512], start=(k == 0), stop=(k == 7)),
                                 reads=[Win, hT], writes=[p])
                            yield
                        evac(zs.ap[:, blk, :], zs, (p.ap[:, 0:272], p))
                        yield
                        p = PS()
                        for k in range(8):
                            S.op("pe", lambda e: e.matmul(p.ap, lhsT=hT.ap[:, k, blk * 128:(blk + 1) * 128], rhs=Win.ap[:, k, 2512:3024], start=(k == 0), stop=(k == 7)),
                                 reads=[Win, hT], writes=[p])
                            yield
                        S.op("act", lambda e: e.copy(hvs.ap[:, blk, :], p.ap[:, 0:256]), reads=[p], writes=[hvs])
                        yield
                        S.op("dve", lambda e: e.tensor_copy(hgs.ap[:, blk, :], p.ap[:, 256:512]), reads=[p], writes=[hgs])
                        yield
                    rr = lambda t_: t_.ap[tok0:tok0 + n, :].rearrange("(b p) n -> p b n", p=128)
                    self.store("sp", Sc["zab_" + nm], zs, dstap=rr(Sc["zab_" + nm]), srcap=zs.ap[:, 0:nblk, :])
                    yield
                    self.store("sp", Sc["hv_" + nm], hvs, dstap=rr(Sc["hv_" + nm]), srcap=hvs.ap[:, 0:nblk, :])
                    yield
                    self.store("sp", Sc["hg_" + nm], hgs, dstap=rr(Sc["hg_" + nm]), srcap=hgs.ap[:, 0:nblk, :])
                    yield
                    if not s.sample:
                        pk = pck[it % 2]
                        b_idx = 0 if nm == "p0" else 1
                        for blk in range(nblk):
                            p = PS()
                            for k in range(8):
                                S.op("pe", lambda e: e.matmul(p.ap[:, 0:320], lhsT=hT.ap[:, k, blk * 128:(blk + 1) * 128], rhs=Win.ap[:, k, 1920:2240], start=(k == 0), stop=(k == 7)),
                                     reads=[Win, hT], writes=[p])
                                yield
                            S.op("dve", lambda e: e.tensor_copy(pk.ap[:, blk, :], p.ap[:, 0:320]), reads=[p], writes=[pk])
                            yield
                            jk = nt_tiles[1][0]
                            S.op("act", lambda e: e.activation(jk.ap[:, 0:256], pk.ap[:, blk, 0:256], AF.Square, accum_out=pss.ap[:, blk:blk + 1]), reads=[pk], writes=[jk, pss])
                            yield
                        self.rstd(prs, pss, 1.0 / 256, nblk)
                        yield
                        for blk in range(nblk):
                            S.op("dve", lambda e: e.scalar_tensor_tensor(pk.ap[:, blk, 0:256], pk.ap[:, blk, 0:256], prs.ap[:, blk:blk + 1], kvrep.ap, op0=ALU.mult, op1=ALU.mult),
                                 reads=[pk, prs, kvrep], writes=[pk])
                            yield
                        self.store("sp", O["ckv"], pk, dstap=O["ckv"].ap[b_idx, l, tok0:tok0 + n, :].rearrange("(b p) n -> p b n", p=128), srcap=pk.ap[:, 0:nblk, 0:256])
                        yield
                        self.store("sp", O["kr"], pk, dstap=O["kr"].ap[b_idx, l, tok0:tok0 + n, :].rearrange("(b p) n -> p b n", p=128), srcap=pk.ap[:, 0:nblk, 256:320])
                        yield

                def stage2(ti):
                    tok0 = ti * TT
                    n = TT
                    it = ti
                    mla = mla2[ti % 2]
                    if s.sample:
                        self.load("sp", ropet, I["ropeT"], dstap=ropet.ap[:, :, 0:TT], srcap=I["ropeT"].ap[:, :, tok0:tok0 + TT].rearrange("a p t -> p a t"))
                        yield
                    for (c0, nc_, rqt) in ((0, 3, rq[0]), (3, 2, rq[1])):
                        S.op("act", lambda e: e.activation(sq.ap[:, 0:nc_, 0:n], mla.ap[:, c0:c0 + nc_, 0:n], AF.Square), reads=[mla], writes=[sq])
                        yield
                        p = PS()
                        for c in range(nc_):
                            S.op("pe", lambda e: e.matmul(p.ap[:, 0:n], lhsT=C["ones"].ap, rhs=sq.ap[:, c, 0:n], start=(c == 0), stop=(c == nc_ - 1)), reads=[C["ones"], sq], writes=[p])
                            yield
                        self.rstd(rqt, p, 1.0 / (128 * nc_), n)
                        yield
                        for c in range(nc_):
                            S.op("dve", lambda e: e.scalar_tensor_tensor(cn.ap[:, c0 + c, 0:n], mla.ap[:, c0 + c, 0:n], nw.ap[:, c0 + c:c0 + c + 1], rqt.ap[:, 0:n], op0=ALU.mult, op1=ALU.mult),
                                 reads=[mla, nw, rqt], writes=[cn])
                            yield
                    Qn, Qr = Qn_st[it % 2], Qr_st[it % 2]
                    for h in range(4):
                        p = fm_proj(h * 192, 128, n, W=Wuq, rhs=cn, nk=3)
                        yield
                        evac(Qn.ap[:, h, 0:n], Qn, (p.ap[:, 0:n], p))
                        yield
                        p = fm_proj(h * 192 + 128, 64, n, W=Wuq, rhs=cn, nk=3)
                        yield
                        if s.sample:
                            rope_apply(Qr.ap[:, h, 0:n], Qr, p.ap[0:64, 0:n], p, n, tok0, h % 2)
                            yield
                        else:
                            evac(Qr.ap[:, h, 0:n], Qr, (p.ap[0:64, 0:n], p))
                            yield
                    self.store("sp", Sc["QnT_" + nm], Qn, dstap=Sc["QnT_" + nm].ap[:, :, tok0:tok0 + n].rearrange("h p t -> p h t"), srcap=Qn.ap[:, :, 0:n])
                    yield
                    self.store("sp", Sc["QrT_" + nm], Qr, dstap=Sc["QrT_" + nm].ap[:, :, tok0:tok0 + n].rearrange("h p t -> p h t"), srcap=Qr.ap[:, :, 0:n])
                    yield
                    S.op("pool", lambda e: e.tensor_tensor(sqb.ap[:, :, 0:n], Qn.ap[:, :, 0:n], Qn.ap[:, :, 0:n], ALU.mult), reads=[Qn], writes=[sqb])
                    yield
                    S.op("pool", lambda e: e.tensor_tensor(sqr.ap[:, :, 0:n], Qr.ap[:, :, 0:n], Qr.ap[:, :, 0:n], ALU.mult), reads=[Qr], writes=[sqr])
                    yield
                    maxnorm(sqb, sqr, n, 0)
                    yield
                    kv_from_cn(s, n, tok0, it, cn, 3, mla.ap[0:64, 5, 0:n], mla, s.sample, 0)
                    yield

                ntile = T // TT
                drive([(stage1(0), ps1)])
                for ti in range(ntile):
                    ch = [(stage2(ti), ps2)]
                    if ti + 1 < ntile:
                        ch.insert(0, (stage1(ti + 1), ps1))
                    drive(ch)
            self.phase_end()

    def phaseB(self, l):
        S, C, I, Sc, O = self.S, self.C, self.I, self.Sc, self.O
        bc3 = lambda ap, shape, axis: ap.unsqueeze(axis).to_broadcast(shape)
        for s in self.seqs:
            nm, T, NB = s.name, s.T, s.NB
            bidx = 0 if nm == "p0" else 1
            with ExitStack() as st:
                sb = lambda shape, dt=F32, name=None: self.sb(shape, dt, name, st)
                convd = sb([128, 6, 5, 128], BF16, "convd")
                self.load("pool", convd, I["convd"], srcap=I["convd"].ap[l])
                gad = sb([128, 2, 8], F32, "gad")
                self.load("sp", gad, I["gdn_ad"], srcap=I["gdn_ad"].ap[l])
                ab = sb([128, NB, 16], F32, "ab")
                self.load("sp", ab, Sc["zab_" + nm], srcap=Sc["zab_" + nm].ap[:, 256:272].rearrange("(b p) n -> p b n", p=128))
                g_all, beta_all = sb([128, NB, 8], F32, "g_all"), sb([128, NB, 8], F32, "beta_all")
                xa, nx, l1 = sb([128, NB, 8], F32, "xa"), sb([128, NB, 8], F32, "nx"), sb([128, NB, 8], F32, "l1")
                eA = sb([128, 8], F32, "eA")
                S.op("dve", lambda e: e.tensor_tensor(xa.ap, ab.ap[:, :, 0:8], bc3(gad.ap[:, 1, :], [128, NB, 8], 1), ALU.add), reads=[ab, gad], writes=[xa])
                S.op("dve", lambda e: e.tensor_scalar(nx.ap, xa.ap, -1.0, None, op0=ALU.mult), reads=[xa], writes=[nx])
                S.op("dve", lambda e: e.tensor_tensor(nx.ap, nx.ap, xa.ap, ALU.min), reads=[xa, nx], writes=[nx])
                S.op("act", lambda e: e.activation(l1.ap, nx.ap, AF.Exp), reads=[nx], writes=[l1])
                S.op("act", lambda e: e.activation(l1.ap, l1.ap, AF.Ln, bias=1.0), reads=[l1], writes=[l1])
                S.op("dve", lambda e: e.tensor_scalar(xa.ap, xa.ap, 0.0, None, op0=ALU.max), reads=[xa], writes=[xa])
                S.op("dve", lambda e: e.tensor_tensor(xa.ap, xa.ap, l1.ap, ALU.add), reads=[xa, l1], writes=[xa])
                S.op("act", lambda e: e.activation(eA.ap, gad.ap[:, 0, :], AF.Exp), reads=[gad], writes=[eA])
                S.op("dve", lambda e: e.scalar_tensor_tensor(g_all.ap, xa.ap, -1.0, bc3(eA.ap, [128, NB, 8], 1), op0=ALU.mult, op1=ALU.mult), reads=[xa, eA], writes=[g_all])
                S.op("act", lambda e: e.activation(beta_all.ap, ab.ap[:, :, 8:16], AF.Sigmoid), reads=[ab], writes=[beta_all])
                gsh_t = [Tl(Sc["gsh_" + nm].ap[b_]) for b_ in range(NB)]

                def make_dir(d):
                    def prep_set():
                        t = {}
                        t["rawt"] = sb([128, 6, 132], BF16, "rawt")
                        t["qk_f"], t["sq"], t["rn"] = sb([128, 4, 128], F32, "qk_f"), sb([128, 4, 128], F32, "sq"), sb([128, 4, 128], F32, "rn")
                        t["vT"] = sb([128, 2, 128], BF16, "vT")
                        shr = sb([128, 1024], BF16, "shr")
                        t["shr"] = shr
                        t["qhT"] = Tl(shr.ap[:, 0:256].rearrange("p (c t) -> p c t", c=2), shr.b)
                        t["khT"] = Tl(shr.ap[:, 256:512].rearrange("p (c t) -> p c t", c=2), shr.b)
                        t["kv_tm"] = Tl(shr.ap[:, 512:1024].rearrange("p (a c) -> p a c", a=2), shr.b)
                        for n_ in ("gm", "Dm", "Dst", "egc", "tq"):
                            t[n_] = sb([128, 4, 128], F32, n_)
                        for n_ in ("gc_sb", "eg", "bneg", "be", "tl"):
                            t[n_] = sb([128, 4], F32, n_)
                        t["Qb"], t["Pb"], t["Yb"] = self.sbn(2, [128, 4, 128], F32, "Qb", st), self.sbn(2, [128, 4, 128], F32, "Pb", st), self.sbn(2, [128, 4, 128], F32, "Yb", st)
                        t["att"] = sb([128, 4, 128], BF16, "att")
                        t["vb"], t["kbe"] = sb([128, 4, 64], F32, "vb"), sb([128, 4, 64], F32, "kbe")
                        t["psi"] = [0]
                        return t
                    PSETS = [prep_set()]
                    G_ = {k: self.sbn(3, shp, dt, k, st) for k, shp, dt in (("wT", [128, 2, 128], BF16), ("qdT", [128, 2, 128], BF16), ("attT", [128, 4, 128], BF16),
                                                                              ("ktail", [128, 4, 64], BF16), ("u", [128, 4, 64], F32), ("glS", [128, 2, 2], F32))}
                    Sg, Sgb = sb([128, 2, 64], F32, "Sg"), sb([128, 2, 64], BF16, "Sgb")
                    vnew = sb([128, 4, 64], BF16, "vnew")
                    ogst = self.sbn(2, [128, 256], F32, "ogst", st)
                    hq, hf = sb([128, 2, 128], F32, "hq"), sb([128, 2, 128], F32, "hf")
                    sg, gate, kk, lf, bcp, bcr, arg, E1, E2 = (sb([128, 2, 128], F32, n_) for n_ in ("sg", "gate", "kk", "lf", "bcp", "bcr", "arg", "E1", "E2"))
                    kt2 = sb([128, 2, 128], BF16, "kt2")
                    H_ = {k: self.sbn(2, shp, dt, k, st) for k, shp, dt in (("qtT", [128, 2, 128], BF16), ("ktT", [128, 2, 128], BF16), ("hattT", [128, 4, 128], BF16),
                                                                              ("k2tm", [128, 256], BF16), ("hv", [128, 256], BF16), ("e_r", [128, 4], F32), ("e_l", [128, 4], F32))}
                    qtf, ktf = sb([128, 2, 128], BF16, "qtf"), sb([128, 2, 128], BF16, "ktf")
                    fr = [sb([128, 4, 128], F32, "fr0"), sb([128, 4, 128], F32, "fr1")]
                    Sh, Shp = sb([128, 2, 64], F32, "Sh"), sb([128, 2, 64], BF16, "Shp")
                    ohst = self.sbn(2, [128, 256], F32, "ohst", st)

                    def heads():
                        for h in range(4):
                            yield h, h // 2, (h % 2) * 64

                    def run_chains(chains):
                        chains = list(chains)
                        while chains:
                            for c in list(chains):
                                try:
                                    next(c)
                                except StopIteration:
                                    chains.remove(c)

                    def gdn_prep(b, d, sl, ch):
                        P_ = PSETS[ch]
                        rawt, qk_f, sq, rn, vT, qhT, khT, kv_tm = (P_[k] for k in ("rawt", "qk_f", "sq", "rn", "vT", "qhT", "khT", "kv_tm"))
                        gm, Dm, Dst, egc, tq, gc_sb, eg, bneg, be, tl = (P_[k] for k in ("gm", "Dm", "Dst", "egc", "tq", "gc_sb", "eg", "bneg", "be", "tl"))
                        Qb, Pb, Yb, att, vb, kbe = (P_[k] for k in ("Qb", "Pb", "Yb", "att", "vb", "kbe"))

                        def gp_ps():
                            P_["psi"][0] ^= 1
                            return self.psb[2 * d + P_["psi"][0]]
                        t0 = b * 128
                        Lm = C["Lf"] if d == 0 else C["Lb"]
                        negm = C["negf"] if d == 0 else C["negb"]
                        lasts = (63, 127) if d == 0 else (0, 64)
                        shr = P_["shr"]
                        rnd = (b - 1, NB - 2 - b)
                        mine, other = rnd[d], rnd[1 - d]
                        if other < mine:
                            self.load("sp", shr, gsh_t[b])
                            yield
                        else:
                            lo, hi = max(t0 - 2, 0), min(t0 + 130, T)
                            S.op("pool", lambda e: e.memset(rawt.ap, 0.0), writes=[rawt])
                            yield
                            self.load("sp", rawt, Sc["qkvT_" + nm], dstap=rawt.ap[:, :, lo - (t0 - 2):hi - (t0 - 2)], srcap=Sc["qkvT_" + nm].ap[:, lo:hi].rearrange("(c p) t -> p c t", p=128))
                            yield
                            cA, cB = gp_ps(), gp_ps()
                            for cc in range(6):
                                cp_ = cA if cc < 4 else cB
                                o_ap = cp_.ap[:, (cc % 4) * 128:(cc % 4 + 1) * 128]
                                for j in range(5):
                                    S.op("pe", lambda e: e.matmul(o_ap, lhsT=convd.ap[:, cc, j, :], rhs=rawt.ap[:, cc, j:j + 128], start=(j == 0), stop=(j == 4)), reads=[convd, rawt], writes=[cp_])
                                    yield
                            S.op("act", lambda e: e.activation(qk_f.ap.rearrange("p c t -> p (c t)"), cA.ap, AF.Silu), reads=[cA], writes=[qk_f])
                            yield
                            S.op("act", lambda e: e.activation(vT.ap.rearrange("p c t -> p (c t)"), cB.ap[:, 0:256], AF.Silu), reads=[cB], writes=[vT])
                            yield
                            S.op("pool", lambda e: e.tensor_tensor(sq.ap, qk_f.ap, qk_f.ap, ALU.mult), reads=[qk_f], writes=[sq])
                            yield
                            sp_ = gp_ps()
                            S.op("pe", lambda e: e.matmul(sp_.ap, lhsT=C["blockones"].ap, rhs=sq.ap.rearrange("p c t -> p (c t)"), start=True, stop=True), reads=[C["blockones"], sq], writes=[sp_])
                            yield
                            self.rstd(Tl(rn.ap.rearrange("p c t -> p (c t)"), rn.b), sp_, 1.0, 512)
                            yield
                            S.op("dve", lambda e: e.scalar_tensor_tensor(qhT.ap, qk_f.ap[:, 0:2, :], 0.125, rn.ap[:, 0:2, :], op0=ALU.mult, op1=ALU.mult), reads=[qk_f, rn], writes=[qhT])
                            yield
                            S.op("dve", lambda e: e.tensor_tensor(khT.ap, qk_f.ap[:, 2:4, :], rn.ap[:, 2:4, :], ALU.mult), reads=[qk_f, rn], writes=[khT])
                            yield
                            tp = gp_ps()
                            tpv = tp.ap.bitcast(BF16)[:, 0:512].rearrange("p (a c) -> p a c", a=4)
                            for a_, src in enumerate((khT.ap[:, 0, :], khT.ap[:, 1, :], vT.ap[:, 0, :], vT.ap[:, 1, :])):
                                S.op("pe", lambda e: e.transpose(tpv[:, a_, :], src, C["ident_b"].ap), reads=[khT, vT, C["ident_b"]], writes=[tp])
                                yield
                            S.op("dve", lambda e: e.tensor_copy(kv_tm.ap.rearrange("p a c -> p (a c)"), tp.ap.bitcast(BF16)[:, 0:512]), reads=[tp], writes=[kv_tm])
                            yield
                            if mine < other:
                                self.store("pool", gsh_t[b], shr)
                                yield
                        g_d = g_all.ap[:, b, d * 4:(d + 1) * 4]
                        b_d = beta_all.ap[:, b, d * 4:(d + 1) * 4]
                        gp = gp_ps()
                        S.op("pe", lambda e: e.matmul(gp.ap[:, 0:4], lhsT=Lm.ap, rhs=g_d, start=True, stop=True), reads=[Lm, g_all], writes=[gp])
                        yield
                        S.op("dve", lambda e: e.tensor_copy(gc_sb.ap, gp.ap[:, 0:4]), reads=[gp], writes=[gc_sb])
                        yield
                        S.op("dve", lambda e: e.tensor_tensor(gm.ap, bc3(Lm.ap, [128, 4, 128], 1), bc3(g_d, [128, 4, 128], 2), ALU.mult), reads=[Lm, g_all], writes=[gm])
                        yield
                        gb = gp_ps()
                        gbv = gb.ap.rearrange("p (h j) -> p h j", h=4)
                        S.op("pe", lambda e: e.matmul(gb.ap, lhsT=C["ones"].ap, rhs=gm.ap.rearrange("p h j -> p (h j)"), start=True, stop=True), reads=[C["ones"], gm], writes=[gb])
                        yield
                        S.op("dve", lambda e: e.scalar_tensor_tensor(Dm.ap, gbv, -1.0, bc3(gc_sb.ap, [128, 4, 128], 2), op0=ALU.mult, op1=ALU.add), reads=[gb, gc_sb], writes=[Dm])
                        yield
                        S.op("act", lambda e: e.activation(egc.ap, gbv, AF.Exp), reads=[gb], writes=[egc])
                        yield
                        for j_ in range(2):
                            S.op("dve", lambda e: e.tensor_tensor(tl.ap[j_ * 64:(j_ + 1) * 64], gbv[j_ * 64:(j_ + 1) * 64, :, lasts[j_]], gc_sb.ap[j_ * 64:(j_ + 1) * 64], ALU.subtract), reads=[gb, gc_sb], writes=[tl])
                            yield
                        S.op("pool", lambda e: e.tensor_tensor(Dm.ap, Dm.ap, bc3(negm.ap, [128, 4, 128], 1), ALU.add), reads=[Dm, negm], writes=[Dm])
                        yield
                        S.op("act", lambda e: e.activation(Dm.ap, Dm.ap, AF.Exp), reads=[Dm], writes=[Dm])
                        yield
                        S.op("act", lambda e: e.activation(tl.ap, tl.ap, AF.Exp), reads=[tl], writes=[tl])
                        yield
                        S.op("act", lambda e: e.activation(eg.ap, gc_sb.ap, AF.Exp), reads=[gc_sb], writes=[eg])
                        yield
                        S.op("pool", lambda e: e.tensor_tensor(Dst.ap, Dm.ap, bc3(C["noteye"].ap, [128, 4, 128], 1), ALU.mult), reads=[Dm, C["noteye"]], writes=[Dst])
                        yield
                        S.op("dve", lambda e: e.tensor_scalar(bneg.ap, b_d, -1.0, None, op0=ALU.mult), reads=[beta_all], writes=[bneg])
                        yield
                        S.op("dve", lambda e: e.tensor_tensor(be.ap, b_d, eg.ap, ALU.mult), reads=[beta_all, eg], writes=[be])
                        yield
                        glS = G_["glS"][sl]
                        egv = egc.ap.rearrange("p (c two) j -> p c two j", two=2)
                        for j_ in range(2):
                            S.op("dve", lambda e: e.tensor_copy(glS.ap[0:64, :, j_], egv[0:64, :, 0, lasts[j_]]), reads=[egc], writes=[glS])
                            yield
                            S.op("dve", lambda e: e.tensor_copy(glS.ap[64:128, :, j_], egv[64:128, :, 1, lasts[j_]]), reads=[egc], writes=[glS])
                            yield
                        Gp, Ap = gp_ps(), gp_ps()
                        for h, cc, pb in heads():
                            S.op("pe", lambda e: e.matmul(Gp.ap[:, h * 128:(h + 1) * 128], lhsT=khT.ap[pb:pb + 64, cc, :], rhs=khT.ap[pb:pb + 64, cc, :], start=True, stop=True), reads=[khT], writes=[Gp], rb=pb)
                            yield
                        for h, cc, pb in heads():
                            S.op("pe", lambda e: e.matmul(Ap.ap[:, h * 128:(h + 1) * 128], lhsT=qhT.ap[pb:pb + 64, cc, :], rhs=khT.ap[pb:pb + 64, cc, :], start=True, stop=True), reads=[qhT, khT], writes=[Ap], rb=pb)
                            yield
                        f4 = lambda t_: t_.ap.rearrange("p h j -> p (h j)")
                        S.op("dve", lambda e: e.tensor_tensor(f4(tq), Gp.ap, f4(Dst), ALU.mult), reads=[Gp, Dst], writes=[tq])
                        yield
                        Q0 = Qb[0]
                        S.op("pool", lambda e: e.tensor_tensor(Q0.ap, tq.ap, bc3(bneg.ap, [128, 4, 128], 2), ALU.mult), reads=[tq, bneg], writes=[Q0])
                        yield
                        S.op("dve", lambda e: e.tensor_tensor(f4(att), Ap.ap, f4(Dm), ALU.mult), reads=[Ap, Dm], writes=[att])
                        yield
                        tp1, tp2 = gp_ps(), gp_ps()
                        v1 = tp1.ap.rearrange("p (h j) -> p h j", h=4)
                        v2 = tp2.ap.bitcast(BF16)[:, 0:512].rearrange("p (h j) -> p h j", h=4)
                        for h in range(4):
                            S.op("pe", lambda e: e.transpose(v1[:, h, :], Q0.ap[:, h, :], C["ident"].ap), reads=[Q0, C["ident"]], writes=[tp1])
                            yield
                        for h in range(4):
                            S.op("pe", lambda e: e.transpose(v2[:, h, :], att.ap[:, h, :], C["ident_b"].ap), reads=[att, C["ident_b"]], writes=[tp2])
                            yield
                        P0, Y0 = Pb[0], Yb[0]
                        S.op("act", lambda e: e.copy(P0.ap, v1), reads=[tp1], writes=[P0])
                        yield
                        S.op("dve", lambda e: e.tensor_tensor(Y0.ap, v1, bc3(C["ident"].ap, [128, 4, 128], 1), ALU.add), reads=[tp1, C["ident"]], writes=[Y0])
                        yield
                        attT = G_["attT"][sl]
                        S.op("act", lambda e: e.copy(attT.ap, v2), reads=[tp2], writes=[attT])
                        yield
                        for stp in range(5):
                            Qc, Pc, Yc = Qb[stp % 2], Pb[stp % 2], Yb[stp % 2]
                            Qn_, Pn_, Yn_ = Qb[(stp + 1) % 2], Pb[(stp + 1) % 2], Yb[(stp + 1) % 2]
                            qp = gp_ps()
                            for h in range(4):
                                S.op("pe", lambda e: e.matmul(qp.ap[:, h * 128:(h + 1) * 128], lhsT=Pc.ap[:, h, :], rhs=Qc.ap[:, h, :], start=True, stop=True), reads=[Pc, Qc], writes=[qp])
                                yield
                            S.op("act", lambda e: e.copy(f4(Qn_), qp.ap), reads=[qp], writes=[Qn_])
                            yield
                            if stp < 4:
                                pp = gp_ps()
                                for h in range(4):
                                    S.op("pe", lambda e: e.transpose(pp.ap[:, h * 128:(h + 1) * 128], Qn_.ap[:, h, :], C["ident"].ap), reads=[Qn_, C["ident"]], writes=[pp])
                                    yield
                                S.op("act", lambda e: e.copy(f4(Pn_), pp.ap), reads=[pp], writes=[Pn_])
                                yield
                            yp = gp_ps()
                            for h in range(4):
                                S.op("pe", lambda e: e.matmul(yp.ap[:, h * 128:(h + 1) * 128], lhsT=Qn_.ap[:, h, :], rhs=Yc.ap[:, h, :], start=True, stop=True), reads=[Qn_, Yc], writes=[yp])
                                yield
                            S.op("dve", lambda e: e.tensor_tensor(f4(Yn_), f4(Yc), yp.ap, ALU.add), reads=[yp, Yc], writes=[Yn_])
                            yield
                        Yf = Yb[1]
                        S.op("pool", lambda e: e.tensor_tensor(vb.ap, kv_tm.ap[:, 1, :].rearrange("p (h v) -> p h v", h=4), bc3(b_d, [128, 4, 64], 2), ALU.mult), reads=[kv_tm, beta_all], writes=[vb])
                        yield
                        S.op("pool", lambda e: e.tensor_tensor(kbe.ap, kv_tm.ap[:, 0, :].rearrange("p (h v) -> p h v", h=4), bc3(be.ap, [128, 4, 64], 2), ALU.mult), reads=[kv_tm, be], writes=[kbe])
                        yield
                        up, wp = gp_ps(), gp_ps()
                        for h in range(4):
                            S.op("pe", lambda e: e.matmul(up.ap[:, h * 64:(h + 1) * 64], lhsT=Yf.ap[:, h, :], rhs=vb.ap[:, h, :], start=True, stop=True), reads=[Yf, vb], writes=[up])
                            yield
                        for h, cc, pb in heads():
                            S.op("pe", lambda e: e.matmul(wp.ap[pb:pb + 64, cc * 128:(cc + 1) * 128], lhsT=kbe.ap[:, h, :], rhs=Yf.ap[:, h, :], start=True, stop=True), reads=[Yf, kbe], writes=[wp])
                            yield
                        u, wT, qdT, ktail = G_["u"][sl], G_["wT"][sl], G_["qdT"][sl], G_["ktail"][sl]
                        S.op("act", lambda e: e.copy(u.ap.rearrange("p h v -> p (h v)"), up.ap[:, 0:256]), reads=[up], writes=[u])
                        yield
                        S.op("dve", lambda e: e.tensor_copy(wT.ap.rearrange("p c t -> p (c t)"), wp.ap[:, 0:256]), reads=[wp], writes=[wT])
                        yield
                        S.op("dve", lambda e: e.tensor_tensor(qdT.ap[0:64], qhT.ap[0:64], egv[0:64, :, 0, :], ALU.mult), reads=[qhT, egc], writes=[qdT])
                        yield
                        S.op("dve", lambda e: e.tensor_tensor(qdT.ap[64:128], qhT.ap[64:128], egv[64:128, :, 1, :], ALU.mult), reads=[qhT, egc], writes=[qdT])
                        yield
                        S.op("pool", lambda e: e.tensor_tensor(ktail.ap, kv_tm.ap[:, 0, :].rearrange("p (h v) -> p h v", h=4), bc3(tl.ap, [128, 4, 64], 2), ALU.mult), reads=[kv_tm, tl], writes=[ktail])
                        yield

                    def gdn_recur(b, d, sl, i):
                        u, wT, qdT, ktail, attT, glS = (G_[k][sl] for k in ("u", "wT", "qdT", "ktail", "attT", "glS"))
                        op_ = self.psb[4 + d]
                        for j in ((0, 1) if d == 0 else (1, 0)):
                            r0 = j * 64
                            wsp = kvp = self.psb[4 + d]
                            for h, cc, pb in heads():
                                S.op("pe", lambda e: e.matmul(wsp.ap[r0:r0 + 64, 256 + h * 64:256 + (h + 1) * 64], lhsT=wT.ap[pb:pb + 64, cc, r0:r0 + 64], rhs=Sgb.ap[pb:pb + 64, cc, :], start=True, stop=True), reads=[wT, Sgb], writes=[wsp], rb=pb)
                                yield
                            S.op("dve", lambda e: e.tensor_tensor(vnew.ap[r0:r0 + 64].rearrange("p h v -> p (h v)"), u.ap[r0:r0 + 64].rearrange("p h v -> p (h v)"), wsp.ap[r0:r0 + 64, 256:512], ALU.subtract), reads=[u, wsp], writes=[vnew])
                            yield
                            for h, cc, pb in heads():
                                o_ap = op_.ap[r0:r0 + 64, h * 64:(h + 1) * 64]
                                S.op("pe", lambda e: e.matmul(o_ap, lhsT=qdT.ap[pb:pb + 64, cc, r0:r0 + 64], rhs=Sgb.ap[pb:pb + 64, cc, :], start=True, stop=False), reads=[qdT, Sgb], writes=[op_], rb=pb)
                                yield
                                S.op("pe", lambda e: e.matmul(o_ap, lhsT=attT.ap[r0:r0 + 64, h, r0:r0 + 64], rhs=vnew.ap[r0:r0 + 64, h, :], start=False, stop=True), reads=[attT, vnew], writes=[op_], rb=r0)
                                yield
                            for h, cc, pb in heads():
                                S.op("pe", lambda e: e.matmul(kvp.ap[pb:pb + 64, 256 + cc * 64:256 + (cc + 1) * 64], lhsT=ktail.ap[r0:r0 + 64, h, :], rhs=vnew.ap[r0:r0 + 64, h, :], start=True, stop=True), reads=[ktail, vnew], writes=[kvp], rb=r0)
                                yield
                            S.op("dve", lambda e: e.tensor_tensor(Sg.ap, Sg.ap, bc3(glS.ap[:, :, j], [128, 2, 64], 2), ALU.mult), reads=[Sg, glS], writes=[Sg])
                            yield
                            S.op("dve", lambda e: e.tensor_tensor(Sg.ap.rearrange("p c v -> p (c v)"), Sg.ap.rearrange("p c v -> p (c v)"), kvp.ap[:, 256:384], ALU.add), reads=[Sg, kvp], writes=[Sg])
                            yield
                            S.op("act", lambda e: e.copy(Sgb.ap, Sg.ap), reads=[Sg], writes=[Sgb])
                            yield
                        og = ogst[i % 2]
                        S.op("act", lambda e: e.copy(og.ap, op_.ap[:, 0:256]), reads=[op_], writes=[og])
                        yield
                        self.store("pool", Sc[f"og{d}_" + nm], og, dstap=Sc[f"og{d}_" + nm].ap[b * 128:(b + 1) * 128, :])
                        yield

                    def hg_prep(b, d, sl):
                        t0 = b * 128
                        frames = ((C["hmfA"], 15), (C["hmfB"], 47)) if d == 0 else ((C["hmbA"], 48), (C["hmbB"], 16))
                        qdT, hattT, k2tm, hv, e_l = (H_[k][sl] for k in ("qtT", "hattT", "k2tm", "hv", "e_l"))
                        self.load("sp", hq, Sc["hqT_" + nm], srcap=Sc["hqT_" + nm].ap[:, t0:t0 + 128].rearrange("(c p) t -> p c t", p=128))
                        yield
                        self.load("sp", hf, Sc["hfT_" + nm], srcap=Sc["hfT_" + nm].ap[d * 256:(d + 1) * 256, t0:t0 + 128].rearrange("(c p) t -> p c t", p=128))
                        yield
                        self.load("sp", hv, Sc["hv_" + nm], srcap=Sc["hv_" + nm].ap[t0:t0 + 128, :])
                        yield
                        S.op("act", lambda e: e.activation(sg.ap, hf.ap, AF.Sigmoid), reads=[hf], writes=[sg])
                        yield
                        for cc in range(2):
                            S.op("dve", lambda e: e.tensor_scalar(gate.ap[:, cc, :], sg.ap[:, cc, :], self.oml.ap[:, cc:cc + 1], self.lbv.ap[:, cc:cc + 1], op0=ALU.mult, op1=ALU.add),
                                 reads=[sg, self.oml, self.lbv], writes=[gate])
                            yield
                        S.op("pool", lambda e: e.tensor_scalar(kk.ap, gate.ap, -1.0, 1.0, op0=ALU.mult, op1=ALU.add), reads=[gate], writes=[kk])
                        yield
                        S.op("dve", lambda e: e.tensor_scalar(gate.ap, gate.ap, 1e-30, None, op0=ALU.max), reads=[gate], writes=[gate])
                        yield
                        S.op("act", lambda e: e.activation(lf.ap, gate.ap, AF.Ln), reads=[gate], writes=[lf])
                        yield
                        fl = lambda t_: t_.ap.rearrange("p c t -> p (c t)")
                        v4 = lambda t_: t_.ap.rearrange("p c (j t) -> p (c j) t", j=2)
                        S.op("dve", lambda e: e.tensor_tensor_scan(fl(bcp), C["scanmask"].ap, fl(lf), 0.0, ALU.mult, ALU.add), reads=[C["scanmask"], lf], writes=[bcp])
                        yield
                        bl = v4(bcp)[:, :, 63:64]
                        if d == 0:
                            bx = bcp
                        else:
                            S.op("dve", lambda e: e.tensor_tensor(bcr.ap, lf.ap, bcp.ap, ALU.subtract), reads=[lf, bcp], writes=[bcr])
                            yield
                            S.op("dve", lambda e: e.tensor_tensor(v4(bcr), v4(bcr), bl.to_broadcast([128, 4, 64]), ALU.add), reads=[bcr, bcp], writes=[bcr])
                            yield
                            bx = bcr
                        S.op("act", lambda e: e.activation(e_l.ap, bl.rearrange("p a b -> p (a b)"), AF.Exp), reads=[bcp], writes=[e_l])
                        yield
                        S.op("act", lambda e: e.activation(E1.ap, bx.ap, AF.Exp), reads=[bx], writes=[E1])
                        yield
                        S.op("dve", lambda e: e.tensor_tensor(qdT.ap, hq.ap, E1.ap, ALU.mult), reads=[hq, E1], writes=[qdT])
                        yield
                        S.op("dve", lambda e: e.tensor_tensor(v4(arg), bl.to_broadcast([128, 4, 64]), v4(bx), ALU.subtract), reads=[bx, bcp], writes=[arg])
                        yield
                        S.op("act", lambda e: e.activation(E2.ap, arg.ap, AF.Exp), reads=[arg], writes=[E2])
                        yield
                        S.op("pool", lambda e: e.tensor_tensor(kt2.ap, kk.ap, E2.ap, ALU.mult), reads=[kk, E2], writes=[kt2])
                        yield
                        tp = self.psb[6 + d]
                        tpv = tp.ap.bitcast(BF16)[:, 768:1024].rearrange("p (c k) -> p c k", c=2)
                        for cc in range(2):
                            S.op("pe", lambda e: e.transpose(tpv[:, cc, :], kt2.ap[:, cc, :], C["ident_b"].ap), reads=[kt2, C["ident_b"]], writes=[tp])
                            yield
                        S.op("act", lambda e: e.copy(k2tm.ap, tp.ap.bitcast(BF16)[:, 768:1024]), reads=[tp], writes=[k2tm])
                        yield
                        for fi, (hm, r) in enumerate(frames):
                            br = v4(bx)[:, :, r:r + 1]
                            S.op("dve", lambda e: e.tensor_tensor(v4(arg), v4(bx), br.to_broadcast([128, 4, 64]), ALU.subtract), reads=[bx], writes=[arg])
                            yield
                            S.op("dve", lambda e: e.tensor_scalar(arg.ap, arg.ap, 40.0, -40.0, op0=ALU.min, op1=ALU.max), reads=[arg], writes=[arg])
                            yield
                            S.op("act", lambda e: e.activation(E1.ap, arg.ap, AF.Exp), reads=[arg], writes=[E1])
                            yield
                            S.op("act", lambda e: e.activation(E2.ap, arg.ap, AF.Exp, scale=-1.0), reads=[arg], writes=[E2])
                            yield
                            S.op("dve", lambda e: e.tensor_tensor(qtf.ap, hq.ap, E1.ap, ALU.mult), reads=[hq, E1], writes=[qtf])
                            yield
                            S.op("pool", lambda e: e.tensor_tensor(ktf.ap, kk.ap, E2.ap, ALU.mult), reads=[kk, E2], writes=[ktf])
                            yield
                            ap_ = self.psb[6 + d]
                            for h, cc, pb in heads():
                                S.op("pe", lambda e: e.matmul(ap_.ap[:, h * 128:(h + 1) * 128], lhsT=ktf.ap[pb:pb + 64, cc, :], rhs=qtf.ap[pb:pb + 64, cc, :], start=True, stop=True), reads=[ktf, qtf], writes=[ap_], rb=pb)
                                yield
                            S.op("dve", lambda e: e.tensor_tensor(fr[fi].ap, ap_.ap.rearrange("p (h t) -> p h t", h=4), bc3(hm.ap, [128, 4, 128], 1), ALU.mult), reads=[ap_, hm], writes=[fr[fi]])
                            yield
                        S.op("pool", lambda e: e.tensor_tensor(hattT.ap, fr[0].ap, fr[1].ap, ALU.add), reads=[fr[0], fr[1]], writes=[hattT])
                        yield

                    def hg_recur(b, d, sl, i):
                        qdT, hattT, k2tm, hv, e_l = (H_[k][sl] for k in ("qtT", "hattT", "k2tm", "hv", "e_l"))
                        op_ = self.psb[6 + d]
                        for j in ((0, 1) if d == 0 else (1, 0)):
                            r0 = j * 64
                            elj = e_l.ap.rearrange("p (c j) -> p c j", j=2)[:, :, j]
                            S.op("act", lambda e: e.copy(Shp.ap, Sh.ap), reads=[Sh], writes=[Shp])
                            yield
                            kvp = self.psb[6 + d]
                            for h, cc, pb in heads():
                                o_ap = op_.ap[r0:r0 + 64, h * 64:(h + 1) * 64]
                                S.op("pe", lambda e: e.matmul(o_ap, lhsT=qdT.ap[pb:pb + 64, cc, r0:r0 + 64], rhs=Shp.ap[pb:pb + 64, cc, :], start=True, stop=False), reads=[qdT, Shp], writes=[op_], rb=pb)
                                yield
                                S.op("pe", lambda e: e.matmul(o_ap, lhsT=hattT.ap[r0:r0 + 64, h, r0:r0 + 64], rhs=hv.ap[r0:r0 + 64, h * 64:(h + 1) * 64], start=False, stop=True), reads=[hattT, hv], writes=[op_], rb=r0)
                                yield
                            for h, cc, pb in heads():
                                S.op("pe", lambda e: e.matmul(kvp.ap[pb:pb + 64, 256 + cc * 64:256 + (cc + 1) * 64], lhsT=k2tm.ap[r0:r0 + 64, h * 64:(h + 1) * 64], rhs=hv.ap[r0:r0 + 64, h * 64:(h + 1) * 64], start=True, stop=True),
                                     reads=[k2tm, hv], writes=[kvp], rb=r0)
                                yield
                            S.op("dve", lambda e: e.tensor_tensor(Sh.ap, Sh.ap, bc3(elj, [128, 2, 64], 2), ALU.mult), reads=[Sh, e_l], writes=[Sh])
                            yield
                            S.op("dve", lambda e: e.tensor_tensor(Sh.ap.rearrange("p c v -> p (c v)"), Sh.ap.rearrange("p c v -> p (c v)"), kvp.ap[:, 256:384], ALU.add), reads=[Sh, kvp], writes=[Sh])
                            yield
                        oh = ohst[i % 2]
                        S.op("act", lambda e: e.copy(oh.ap, op_.ap[:, 0:256]), reads=[op_], writes=[oh])
                        yield
                        self.store("pool", Sc[f"oh{d}_" + nm], oh, dstap=Sc[f"oh{d}_" + nm].ap[b * 128:(b + 1) * 128, :])
                        yield

                    return dict(gdn_prep=gdn_prep, gdn_recur=gdn_recur, hg_prep=hg_prep, hg_recur=hg_recur, Sg=Sg, Sgb=Sgb, Sh=Sh, run_chains=run_chains)

                DD = [make_dir(0), make_dir(1)]
                orders = [list(range(NB)), list(range(NB - 1, -1, -1))]
                run_chains = DD[0]["run_chains"]
                for d in range(2):
                    Sg, Sgb, Sh = DD[d]["Sg"], DD[d]["Sgb"], DD[d]["Sh"]
                    if s.sample:
                        self.load("sp", Sg, I["st_gdn"], srcap=I["st_gdn"].ap[l, d])
                        self.load("sp", Sh, I["st_hg"], srcap=I["st_hg"].ap[l, d])
                    else:
                        S.op("pool", lambda e: e.memset(Sg.ap, 0.0), writes=[Sg])
                        S.op("pool", lambda e: e.memset(Sh.ap, 0.0), writes=[Sh])
                    S.op("act", lambda e: e.copy(Sgb.ap, Sg.ap), reads=[Sg], writes=[Sgb])

                def gdn_chain(d, i):
                    if i + 1 < NB:
                        yield from DD[d]["gdn_prep"](orders[d][i + 1], d, (i + 1) % 2, 0)

                def gdn_rchain(d, i):
                    yield from DD[d]["gdn_recur"](orders[d][i], d, i % 2, i)

                def hg_chain(d, i):
                    if i + 1 < NB:
                        yield from DD[d]["hg_prep"](orders[d][i + 1], d, (i + 1) % 2)
                    yield from DD[d]["hg_recur"](orders[d][i], d, i % 2, i)

                run_chains([DD[0]["gdn_prep"](orders[0][0], 0, 0, 0), DD[1]["gdn_prep"](orders[1][0], 1, 0, 0),
                            DD[0]["hg_prep"](orders[0][0], 0, 0), DD[1]["hg_prep"](orders[1][0], 1, 0)])
                def hg_all(d):
                    for i in range(NB):
                        yield from hg_chain(d, i)
                hgs = [hg_all(0), hg_all(1)]
                for i in range(NB):
                    gs = [gdn_chain(0, i), gdn_chain(1, i), gdn_rchain(0, i), gdn_rchain(1, i)]
                    tick = 0
                    while gs:
                        for c in list(gs):
                            try:
                                next(c)
                            except StopIteration:
                                gs.remove(c)
                        tick += 1
                        if tick % 2 == 0:
                            for c in list(hgs):
                                try:
                                    next(c)
                                except StopIteration:
                                    hgs.remove(c)
                run_chains(hgs)
                if not s.sample:
                    for d in range(2):
                        Sg, Sh = DD[d]["Sg"], DD[d]["Sh"]
                        for hp in range(2):
                            self.store("sp", O["sg"], Sg, dstap=O["sg"].ap[bidx, l, d, :, :, :].rearrange("(c two) k v -> two k c v", two=2)[hp], srcap=Sg.ap[hp * 64:(hp + 1) * 64])
                            self.store("sp", O["sh"], Sh, dstap=O["sh"].ap[bidx, l, d, :, :, :].rearrange("(c two) k v -> two k c v", two=2)[hp], srcap=Sh.ap[hp * 64:(hp + 1) * 64])
                self.phase_end()
            with ExitStack() as st:
                sb = lambda shape, dt=F32, name=None: self.sb(shape, dt, name, st)
                nwt = sb([128, 2, 64], F32, "nwt")
                self.load("sp", nwt, I["normw"], srcap=I["normw"].ap[l])
                of_, ob_, gsrc = self.sbn(2, [128, 256], F32, "of", st), self.sbn(2, [128, 256], F32, "ob", st), self.sbn(2, [128, 256], F32, "gsrc", st)
                sq2 = sb([128, 256], F32, "sq2")
                ss4, rs4 = sb([128, 4], F32, "ss4"), sb([128, 4], F32, "rs4")
                mxb = self.sbn(2, [128, 256], BF16, "mxb", st)
                mst = self.sbn(2, [128, 2, 128], BF16, "mst", st)
                it = 0
                for b in range(NB):
                    for mi, (pre, gsc, gcol, fn) in enumerate((("og", "zab_", slice(0, 256), AF.Silu), ("oh", "hg_", slice(0, 256), AF.Sigmoid))):
                        it += 1
                        a_, b_, g_ = of_[it % 2], ob_[it % 2], gsrc[it % 2]
                        rows = slice(b * 128, (b + 1) * 128)
                        self.load("sp", a_, Sc[f"{pre}0_" + nm], srcap=Sc[f"{pre}0_" + nm].ap[rows, :])
                        self.load("sp", b_, Sc[f"{pre}1_" + nm], srcap=Sc[f"{pre}1_" + nm].ap[rows, :])
                        self.load("sp", g_, Sc[gsc + nm], srcap=Sc[gsc + nm].ap[rows, gcol])
                        S.op("pool", lambda e: e.tensor_tensor(a_.ap, a_.ap, b_.ap, ALU.add), reads=[a_, b_], writes=[a_])
                        S.op("pool", lambda e: e.tensor_tensor(sq2.ap, a_.ap, a_.ap, ALU.mult), reads=[a_], writes=[sq2])
                        S.op("dve", lambda e: e.tensor_reduce(ss4.ap, sq2.ap.rearrange("p (h v) -> p h v", h=4), AX.X, ALU.add), reads=[sq2], writes=[ss4])
                        self.rstd(rs4, ss4, 1.0 / 64, 4)
                        a3 = a_.ap.rearrange("p (h v) -> p h v", h=4)
                        S.op("dve", lambda e: e.tensor_tensor(a3, a3, bc3(rs4.ap, [128, 4, 64], 2), ALU.mult), reads=[a_, rs4], writes=[a_])
                        S.op("pool", lambda e: e.tensor_tensor(a3, a3, bc3(nwt.ap[:, mi, :], [128, 4, 64], 1), ALU.mult), reads=[a_, nwt], writes=[a_])
                        S.op("act", lambda e: e.activation(g_.ap, g_.ap, fn), reads=[g_], writes=[g_])
                        mx_ = mxb[it % 2]
                        S.op("dve", lambda e: e.tensor_tensor(mx_.ap, a_.ap, g_.ap, ALU.mult), reads=[a_, g_], writes=[mx_])
                        tp = self.ps()
                        tpv = tp.ap.bitcast(BF16)[:, 0:256].rearrange("p (c t) -> p c t", c=2)
                        for c in range(2):
                            S.op("pe", lambda e: e.transpose(tpv[:, c, :], mx_.ap[:, c * 128:(c + 1) * 128], C["ident_b"].ap), reads=[mx_, C["ident_b"]], writes=[tp])
                        ms_ = mst[it % 2]
                        S.op("act", lambda e: e.copy(ms_.ap, tpv), reads=[tp], writes=[ms_])
                        self.store("sp", Sc["mixT_" + nm], ms_, dstap=Sc["mixT_" + nm].ap[mi * 256:(mi + 1) * 256, rows].rearrange("(c p) t -> p c t", p=128))
                self.phase_end()

    def phaseC(self, l):
        S, C, I, Sc = self.S, self.C, self.I, self.Sc
        for s in self.seqs:
            nm, T, Tk = s.name, s.T, s.Tk
            nkb = Tk // 128
            TT = min(512, T)
            with ExitStack() as st:
                KT = self.sb([128, 4, Tk], BF16, "KT", st)
                krT = self.sb([64, Tk], BF16, "krT", st)
                V = self.sb([128, nkb, 512], BF16, "V", st)
                self.load("sp", KT, Sc["KnT_" + nm], srcap=Sc["KnT_" + nm].ap.rearrange("h p t -> p h t"))
                self.load("sp", krT, Sc["krT_" + nm])
                self.load("sp", V, Sc["V_" + nm], srcap=Sc["V_" + nm].ap.rearrange("(b p) n -> p b n", p=128))
                Qn = self.sbn(2, [128, 4, TT], BF16, "Qn", st)
                Qr = self.sbn(2, [64, 4, TT], BF16, "Qr", st)
                PT = self.sbn(3, [128, TT], BF16, "PT", st)
                rden = self.sbn(2, [128, TT], F32, "rden", st)
                ost = self.sbn(2, [128, TT], BF16, "ost", st)
                negc = self.sb([128, 4], F32, "negc", st)
                qk2 = self.qk2[nm]
                S.op("dve", lambda e: e.tensor_tensor(negc.ap, qk2.ap[:, 0, :], qk2.ap[:, 1, :], ALU.mult), reads=[qk2], writes=[negc])
                S.op("pool", lambda e: e.tensor_tensor(negc.ap, negc.ap, C["p05"].ap[:, 0:4], ALU.pow), reads=[negc, C["p05"]], writes=[negc])
                S.op("dve", lambda e: e.tensor_scalar(negc.ap, negc.ap, -1.01 * SCALE, None, op0=ALU.mult), reads=[negc], writes=[negc])
                nq = T // TT
                for qi in range(nq):
                    q0 = qi * TT
                    qn, qr = Qn[qi % 2], Qr[qi % 2]
                    self.load("sp", qn, Sc["QnT_" + nm], srcap=Sc["QnT_" + nm].ap[:, :, q0:q0 + TT].rearrange("h p t -> p h t"))
                    self.load("sp", qr, Sc["QrT_" + nm], srcap=Sc["QrT_" + nm].ap[:, :, q0:q0 + TT].rearrange("h p t -> p h t"))
                    for h in range(4):
                        ops, dps = self.psb[h % 2], self.psb[2 + h % 2]
                        sps = [None] * nkb

                        def scores(kb):
                            p = self.psb[4 + kb % 4]
                            sps[kb] = p
                            S.op("pe", lambda e: e.matmul(p.ap[:, 0:TT], lhsT=KT.ap[:, h, kb * 128:(kb + 1) * 128], rhs=qn.ap[:, h, :], start=True, stop=False), reads=[KT, qn], writes=[p])
                            S.op("pe", lambda e: e.matmul(p.ap[:, 0:TT], lhsT=krT.ap[:, kb * 128:(kb + 1) * 128], rhs=qr.ap[:, h, :], start=False, stop=True), reads=[krT, qr], writes=[p])
                        scores(0)
                        for kb in range(nkb):
                            if kb + 1 < nkb:
                                scores(kb + 1)
                            pt = PT[kb % 3]
                            p = sps[kb]
                            S.op("act", lambda e: e.activation(pt.ap, p.ap[:, 0:TT], AF.Exp, bias=negc.ap[:, h:h + 1], scale=SCALE), reads=[p, negc], writes=[pt])
                            S.op("pe", lambda e: e.matmul(ops.ap[:, 0:TT], lhsT=V.ap[:, kb, h * 128:(h + 1) * 128], rhs=pt.ap, start=(kb == 0), stop=(kb == nkb - 1)), reads=[V, pt], writes=[ops])
                            S.op("pe", lambda e: e.matmul(dps.ap[:, 0:TT], lhsT=C["ones_b"].ap, rhs=pt.ap, start=(kb == 0), stop=(kb == nkb - 1)), reads=[C["ones_b"], pt], writes=[dps])
                        rd, os_ = rden[h % 2], ost[h % 2]
                        S.op("dve", lambda e: e.reciprocal(rd.ap, dps.ap[:, 0:TT]), reads=[dps], writes=[rd])
                        S.op("dve", lambda e: e.tensor_tensor(os_.ap, ops.ap[:, 0:TT], rd.ap, ALU.mult), reads=[ops, rd], writes=[os_])
                        self.store("pool", Sc["mixT_" + nm], os_, dstap=Sc["mixT_" + nm].ap[512 + h * 128:512 + (h + 1) * 128, q0:q0 + TT])
                self.phase_end()

    def load_G(self, st, which, ci):
        S, C, Sc = self.S, self.C, self.Sc
        if not getattr(self, "_gvec_l", None) == self.l:
            self._gvec_l = self.l
            gp = self.psb[0]
            for k in range(4):
                S.op("pe", lambda e: e.transpose(gp.ap[0:8, k * 128:(k + 1) * 128], self.gg.ap[:, k // 2, k % 2, :], C["ident"].ap), reads=[self.gg, C["ident"]], writes=[gp])
            gsb = self.sb([8, 512], F32, "gsb", st)
            S.op("dve", lambda e: e.tensor_copy(gsb.ap, gp.ap[0:8, :]), reads=[gp], writes=[gsb])
            self.store("sp", Sc["gvec"], gsb, dstap=Sc["gvec"].ap.rearrange("k (c p) -> c k p", p=128), srcap=gsb.ap.rearrange("c (k p) -> c k p", p=128))
        G = self.sb([128, D], F32, "G", st)
        k = which * 2 + ci
        self.load("sp", G, Sc["gvec"], srcap=Sc["gvec"].ap[k:k + 1, :].partition_broadcast(128))
        return G

    def post_norm_residual(self, yp, ybanks, G, xin, tt, out_ap_tl, ss, rs, junk):
        S = self.S
        S.op("act", lambda e: e.activation(junk.ap, yp, AF.Square, accum_out=ss.ap[:, 0:1]), reads=ybanks, writes=[junk, ss])
        self.rstd(rs, ss, 1.0 / D, 1)
        S.op("dve", lambda e: e.scalar_tensor_tensor(tt.ap, yp, rs.ap[:, 0:1], G.ap, op0=ALU.mult, op1=ALU.mult), reads=ybanks + [rs, G], writes=[tt])
        S.op("dve", lambda e: e.tensor_tensor(out_ap_tl.ap, xin.ap, tt.ap, ALU.add), reads=[xin, tt], writes=[out_ap_tl])

    def phaseD(self, l):
        S, C, I, Sc, O = self.S, self.C, self.I, self.Sc, self.O
        pall = self.pall
        with ExitStack() as st:
            Wout = self.sb([128, 8, D], BF16, "Wout", st)
            self.load("pool", Wout, I["w_out"], srcap=I["w_out"].ap[l])
            Wf1 = self.sb([128, 8, 2 * DFF], BF16, "Wf1", st)
            for k in range(0, 8, 2):
                S.dma("pool", Wf1.ap[:, k:k + 2, :], I["w_f1"].ap[l, :, k:k + 2, :], reads=[I["w_f1"]], writes=[Wf1], key=f"wf1{k}")
            mixT = self.sbn(2, [128, 8, 512], BF16, "mixT", st)
            xt_s = self.sbn(4, [128, D], F32, "xt", st)
            tt = self.sbn(2, [128, D], F32, "tt", st)
            nt_tiles = (self.sbn(2, [128, D], BF16, "xn", st), self.sbn(1, [128, D], BF16, "junk", st) * 2,
                        self.sb([128, 4], F32, "ss", st), self.sb([128, 4], F32, "rs", st))
            junk = nt_tiles[1][0]
            ss1, rs1 = self.sb([128, 1], F32, "ss1", st), self.sb([128, 1], F32, "rs1", st)
            h2T = self.sb([128, 8, 512], BF16, "h2T", st)
            sa = self.sbn(2, [128, 512], F32, "sa", st)
            actst = self.sbn(4, [128, 512], BF16, "actst", st)
            Gc = {}
            for s in self.seqs:
                nm, T, ci = s.name, s.T, s.ci
                if ci not in Gc:
                    Gc[ci] = self.load_G(st, 0, ci)
                G1 = Gc[ci]
                X = I["x_" + nm] if l == 0 else Sc["x1_" + nm]
                TT = min(512, T)
                nblk = TT // 128
                for ti in range(T // TT):
                    tok0 = ti * TT
                    mt = mixT[ti % 2]
                    self.load("sp", mt, Sc["mixT_" + nm], dstap=mt.ap[:, :, 0:TT], srcap=Sc["mixT_" + nm].ap[:, tok0:tok0 + TT].rearrange("(c p) t -> p c t", p=128))
                    for blk in range(nblk):
                        self.load("sp", xt_s[blk], X, srcap=X.ap[tok0 + blk * 128:tok0 + (blk + 1) * 128, :])
                    for blk in range(nblk):
                        b0 = 2 * (blk % 2)
                        ybanks = [self.psb[b0], self.psb[b0 + 1]]
                        yp = pall[:, b0:b0 + 2, :].rearrange("p b n -> p (b n)")
                        for hf in range(2):
                            for k in range(8):
                                S.op("pe", lambda e: e.matmul(self.psb[b0 + hf].ap, lhsT=mt.ap[:, k, blk * 128:(blk + 1) * 128], rhs=Wout.ap[:, k, hf * 512:(hf + 1) * 512], start=(k == 0), stop=(k == 7)),
                                     reads=[mt, Wout], writes=[self.psb[b0 + hf]])
                        self.post_norm_residual(yp, ybanks, G1, xt_s[blk], tt[blk % 2], xt_s[blk], ss1, rs1, junk)
                        self.store("pool", Sc["xmid_" + nm], xt_s[blk], dstap=Sc["xmid_" + nm].ap[tok0 + blk * 128:tok0 + (blk + 1) * 128, :])
                    self.norm_transpose(nt_tiles, lambda blk: xt_s[blk], nblk, self.scale2, 24, ci, h2T, banks=[self.psb[4], self.psb[5], self.psb[6], self.psb[7]])
                    for j in range(22):
                        pA, pB = self.psb[(2 * j) % 4], self.psb[(2 * j + 1) % 4]
                        for (p, c0) in ((pA, j * 128), (pB, DFF + j * 128)):
                            for k in range(8):
                                S.op("pe", lambda e: e.matmul(p.ap[:, 0:TT], lhsT=Wf1.ap[:, k, c0:c0 + 128], rhs=h2T.ap[:, k, 0:TT], start=(k == 0), stop=(k == 7)), reads=[Wf1, h2T], writes=[p])
                        sj, aj = sa[j % 2], actst[j % 4]
                        S.op("act", lambda e: e.activation(sj.ap[:, 0:TT], pA.ap[:, 0:TT], AF.Silu), reads=[pA], writes=[sj])
                        S.op("dve", lambda e: e.tensor_tensor(aj.ap[:, 0:TT], pB.ap[:, 0:TT], sj.ap[:, 0:TT], ALU.mult), reads=[pB, sj], writes=[aj])
                        self.store("pool", Sc["actT_" + nm], aj, dstap=Sc["actT_" + nm].ap[j * 128:(j + 1) * 128, tok0:tok0 + TT], srcap=aj.ap[:, 0:TT])
            self.phase_end()
        with ExitStack() as st:
            Wf2 = self.sb([128, 22, D], BF16, "Wf2", st)
            for k in range(0, 22, 11):
                S.dma("pool", Wf2.ap[:, k:k + 11, :], I["w_f2"].ap[l, :, k:k + 11, :], reads=[I["w_f2"]], writes=[Wf2], key=f"wf2{k}")
            actT = self.sbn(2, [128, 22, 512], BF16, "actT", st)
            xt_s = self.sbn(4, [128, D], F32, "xt", st)
            tt = self.sbn(2, [128, D], F32, "tt", st)
            junk = self.sb([128, D], BF16, "junk", st)
            ss1, rs1 = self.sb([128, 1], F32, "ss1", st), self.sb([128, 1], F32, "rs1", st)
            Gc = {}
            for s in self.seqs:
                nm, T, ci = s.name, s.T, s.ci
                if ci not in Gc:
                    Gc[ci] = self.load_G(st, 1, ci)
                G2 = Gc[ci]
                Xo = Sc["x1_" + nm] if l == 0 else O["y_" + nm]
                TT = min(512, T)
                nblk = TT // 128
                for ti in range(T // TT):
                    tok0 = ti * TT
                    at = actT[ti % 2]
                    self.load("sp", at, Sc["actT_" + nm], dstap=at.ap[:, :, 0:TT], srcap=Sc["actT_" + nm].ap[:, tok0:tok0 + TT].rearrange("(c p) t -> p c t", p=128))
                    for blk in range(nblk):
                        self.load("sp", xt_s[blk], Sc["xmid_" + nm], srcap=Sc["xmid_" + nm].ap[tok0 + blk * 128:tok0 + (blk + 1) * 128, :])
                    for blk in range(nblk):
                        b0 = 2 * (blk % 4)
                        ybanks = [self.psb[b0], self.psb[b0 + 1]]
                        yp = pall[:, b0:b0 + 2, :].rearrange("p b n -> p (b n)")
                        for hf in range(2):
                            for j in range(22):
                                S.op("pe", lambda e: e.matmul(self.psb[b0 + hf].ap, lhsT=at.ap[:, j, blk * 128:(blk + 1) * 128], rhs=Wf2.ap[:, j, hf * 512:(hf + 1) * 512], start=(j == 0), stop=(j == 21)),
                                     reads=[at, Wf2], writes=[self.psb[b0 + hf]])
                        self.post_norm_residual(yp, ybanks, G2, xt_s[blk], tt[blk % 2], xt_s[blk], ss1, rs1, junk)
                        self.store("pool", Xo, xt_s[blk], dstap=Xo.ap[tok0 + blk * 128:tok0 + (blk + 1) * 128, :])
            self.phase_end()


def _rk(w, nk):
    Lw, K, N = w.shape
    return np.ascontiguousarray(w.reshape(Lw, nk, 128, N).transpose(0, 2, 1, 3))


def _fm(v, nch):
    return np.ascontiguousarray(v.reshape(v.shape[0], nch, 128).transpose(0, 2, 1))


def _consts_np():
    c = np.zeros((128, 13, 128), np.float32)
    p = np.arange(128)[:, None]
    i = np.arange(128)[None, :]
    c[:, 0] = (p == i)
    c[:, 1] = 1.0
    c[:, 2] = (p // 64 == i // 64)
    sm_ = (p // 64 == i // 64)
    c[:, 3] = sm_ & (p <= i)
    c[:, 4] = sm_ & (p >= i)
    c[:, 5] = np.where(sm_ & (p >= i), 0.0, NEG)
    c[:, 6] = np.where(sm_ & (p <= i), 0.0, NEG)
    c[:, 7] = (p != i)
    same = (p // 64 == i // 64)
    c[:, 8] = same & (p <= i) & (p % 64 < 32)
    c[:, 9] = same & (p >= i) & (p % 64 >= 32)
    c[:, 11] = same & (p <= i) & (p % 64 >= 32)
    c[:, 12] = same & (p >= i) & (p % 64 < 32)
    for a in range(2):
        for f in range(16):
            c[a * 32 + 16 + f, 10, a * 32 + f] = -1.0
            c[a * 32 + f, 10, a * 32 + 16 + f] = 1.0
    return c


def _rope_np(T):
    t = np.arange(T)
    row = (t // 64).astype(np.float32)
    col = (t % 64).astype(np.float32)
    inv = (np.float32(10000.0) ** (-np.arange(16, dtype=np.float32) / np.float32(16))).astype(np.float32)
    out = np.zeros((2, 64, T), np.float32)
    for a, pos in enumerate((row, col)):
        ang = (pos[None, :] * inv[:, None]).astype(np.float32)
        for hlf in range(2):
            out[0, a * 32 + hlf * 16:a * 32 + hlf * 16 + 16] = np.cos(ang)
            out[1, a * 32 + hlf * 16:a * 32 + hlf * 16 + 16] = np.sin(ang)
    return out


def _shared_inputs(inp, Ts):
    f = lambda k: np.asarray(inp[k], np.float32)
    w_in = f("w_in")
    order = np.concatenate([np.arange(0, 768), np.arange(1040, 1296), np.arange(1552, 2064), np.arange(2320, 2704),
                            np.arange(2704, 2960), np.arange(2960, 3024), np.arange(768, 1040), np.arange(1296, 1552),
                            np.arange(2064, 2320)])
    w_ukv = f("mla_w_ukv")
    kvorder = np.concatenate([np.arange(h * 256, h * 256 + 128) for h in range(4)] + [np.arange(h * 256 + 128, h * 256 + 256) for h in range(4)])
    conv = f("gdn_conv_w")
    convd = np.zeros((L, 128, 6, 5, 128), np.float32)
    idx = np.arange(128)
    for cc in range(6):
        for j in range(5):
            convd[:, idx, cc, j, idx] = conv[:, cc * 128 + idx, j]
    sm = np.ones((128, 256), np.float32)
    sm[:, ::64] = 0.0
    sh = {
        "w_ada": _rk(f("w_ada"), 8),
        "b_ada": _fm(f("b_ada"), 48),
        "gains": np.ascontiguousarray(np.stack([_fm(f(k), 8) for k in ("g_pre_mix", "g_post_mix", "g_pre_ffn", "g_post_ffn")], axis=2)),
        "w_in": _rk(np.ascontiguousarray(w_in[:, :, order]), 8),
        "w_out": _rk(f("w_out"), 8),
        "w_f1": _rk(f("w_ffn_in"), 8),
        "w_f2": _rk(f("w_ffn_out"), 22),
        "w_uq": _rk(f("mla_w_uq"), 3),
        "w_ukv": _rk(np.ascontiguousarray(w_ukv[:, :, kvorder]), 2),
        "convd": convd,
        "gdn_ad": np.ascontiguousarray(np.broadcast_to(np.stack([f("gdn_a_log").reshape(L, 8), f("gdn_dt_bias").reshape(L, 8)], axis=1)[:, None], (L, 128, 2, 8))),
        "normw": np.ascontiguousarray(np.broadcast_to(np.stack([f("gdn_norm_w"), f("hgrn_norm_w")], axis=1)[:, None], (L, 128, 2, 64))),
        "lbraw": np.ascontiguousarray(f("hgrn_lb").reshape(L, 2, 128).transpose(2, 0, 1)),
        "mlanw": np.ascontiguousarray(np.concatenate([_fm(f("mla_q_norm_w"), 3), _fm(f("mla_kv_norm_w"), 2)], axis=2)),
        "kvnw_rep": np.ascontiguousarray(np.broadcast_to(f("mla_kv_norm_w")[:, None, :], (L, 128, 256))),
        "consts": _consts_np(),
        "scanmask": sm,
        "ropeT": _rope_np(Ts),
    }
    return sh


def _core_inputs(inp, sh, b, Ts, Tp):
    f = lambda k: np.asarray(inp[k], np.float32)
    m = dict(sh)
    m["x_s"] = np.ascontiguousarray(f("x_sample")[b, :Ts])
    m["x_p0"] = np.ascontiguousarray(f("x_prompt")[2 * b])
    m["x_p1"] = np.ascontiguousarray(f("x_prompt")[2 * b + 1])
    cond = np.stack([f("c")[b], f("c_ctx")], axis=-1)
    m["cond"] = np.ascontiguousarray(cond.reshape(8, 128, 2).transpose(1, 0, 2))
    ck = f("cache_mla_ckv")[b]
    m["ctx_ckvT"] = np.ascontiguousarray(ck.reshape(L, 256, 2, 128).transpose(0, 3, 2, 1))
    m["ctx_krT"] = np.ascontiguousarray(f("cache_mla_krope")[b].transpose(0, 2, 1))
    for k, src in (("st_gdn", "state_gdn"), ("st_hg", "state_hgrn")):
        s_ = f(src)[b]
        m[k] = np.ascontiguousarray(s_.reshape(L, 2, 2, 2, 64, 64).transpose(0, 1, 3, 4, 2, 5).reshape(L, 2, 128, 2, 64))
    return m


_NC_CACHE = {}


def kernel(**inputs):
    Ts = int(np.asarray(inputs["x_sample"]).shape[1])
    Tp = int(np.asarray(inputs["x_prompt"]).shape[1])
    nb = int(np.asarray(inputs["x_sample"]).shape[0])
    kb = KB(Ts, Tp)
    sh = _shared_inputs(inputs, Ts)
    in_maps = [_core_inputs(inputs, sh, b, Ts, Tp) for b in range(nb)]
    res = run_bass_kernel_spmd(kb.nc, in_maps, core_ids=list(range(nb)))
    R = res.results
    y_p = np.stack([R[b][k] for b in range(nb) for k in ("y_p0", "y_p1")], axis=0)
    y_s = np.stack([R[b]["y_s"] for b in range(nb)], axis=0)
    cat = lambda k: np.concatenate([R[b][k] for b in range(nb)], axis=0)
    return (y_p.astype(np.float32), y_s.astype(np.float32), cat("ckv").astype(np.float32), cat("kr").astype(np.float32),
            cat("sg").astype(np.float32), cat("sh").astype(np.float32))
```

```python
import math
from contextlib import ExitStack

import numpy as np
import concourse.bass as bass
import concourse.mybir as mybir
from concourse.bass_utils import run_bass_kernel_spmd

F32 = mybir.dt.float32
BF16 = mybir.dt.bfloat16
AF = mybir.ActivationFunctionType
ALU = mybir.AluOpType
AX = mybir.AxisListType

D = 1024
L = 2
EPS = 1e-6
IN_DIM = 3024
DFF = 2816
NEG = -30000.0
SCALE = 192 ** -0.5


class Buf:
    __slots__ = ("name", "w", "r", "excl", "rb")

    def __init__(self, name="", excl=False):
        self.name = name
        self.w = None
        self.r = []
        self.rb = 0
        self.excl = excl


class Tl:
    __slots__ = ("ap", "b")

    def __init__(self, ap, b=None):
        self.ap = ap
        self.b = b if b is not None else Buf()

    def __getitem__(self, k):
        return self.ap[k]


class Sched:
    def __init__(self, nc):
        self.nc = nc
        self.engs = {"pe": nc.tensor, "dve": nc.vector, "act": nc.scalar, "pool": nc.gpsimd, "sp": nc.sync}
        self.sems, self.cnt = {}, {}
        self.seen = {k: {} for k in self.engs}
        for k in self.engs:
            self.sems[k] = nc.alloc_semaphore("s_" + k)
            self.cnt[k] = 0
        self.free_dma = []
        import os
        self.limit = int(os.environ.get('KLIMIT', '100000000'))
        self.ninst = 0
        self.nwait = 0
        self.ndma = 0

    def _deps(self, reads, writes, eng=None):
        deps = {}
        for b in reads:
            d = b.w
            if d is not None and deps.get(d[0], 0) < d[1]:
                deps[d[0]] = d[1]
            if b.excl:
                for d in b.r:
                    if d[0] != eng and deps.get(d[0], 0) < d[1]:
                        deps[d[0]] = d[1]
        for b in writes:
            d = b.w
            if d is not None and deps.get(d[0], 0) < d[1]:
                deps[d[0]] = d[1]
            for d in b.r:
                if deps.get(d[0], 0) < d[1]:
                    deps[d[0]] = d[1]
        return deps

    def _wait(self, eng, deps, defer=False):
        e = self.engs[eng]
        seen = self.seen[eng]
        need = []
        for k, c in deps.items():
            if eng == "pe" and k == "pe":
                continue
            if seen.get(k, 0) >= c:
                continue
            if k not in self.engs:
                c = self.cnt[k]
            need.append((k, c))
            seen[k] = c
        last = need.pop() if (defer and need) else None
        for k, c in need:
            e.wait_ge(self.sems[k], c)
            self.nwait += 1
        return last

    def _mark(self, key, reads, writes):
        d = (key, self.cnt[key])
        for b in writes:
            b.w = d
            b.r = []
        for b in reads:
            r = b.r
            r.append(d)
            if len(r) > 48:
                m = {}
                for k, c in r:
                    if m.get(k, 0) < c:
                        m[k] = c
                b.r = list(m.items())

    def op(self, eng, fn, reads=(), writes=(), rb=0):
        if self.ninst >= self.limit:
            return
        reads = [t.b if isinstance(t, Tl) else t for t in reads]
        writes = [t.b if isinstance(t, Tl) else t for t in writes]
        last = self._wait(eng, self._deps(reads, writes, eng), defer=True)
        if eng == "pe" and writes and writes[0].excl:
            b = writes[0]
            if b.rb != rb:
                d = b.w
                if d is not None and d[0] == "pe" and self.seen["pe"].get("pe", 0) < d[1]:
                    self.engs["pe"].wait_ge(self.sems["pe"], d[1])
                    self.seen["pe"]["pe"] = d[1]
                    self.nwait += 1
                b.rb = rb
        ins = fn(self.engs[eng])
        if last is not None:
            ins._wait_ge(self.sems[last[0]], last[1])
        self.cnt[eng] += 1
        ins.then_inc(self.sems[eng], 1)
        self._mark(eng, reads, writes)
        self.ninst += 1

    def dma(self, q, out, in_, reads=(), writes=(), key=None):
        reads = [t.b if isinstance(t, Tl) else t for t in reads]
        writes = [t.b if isinstance(t, Tl) else t for t in writes]
        if self.ninst >= self.limit:
            return
        if key not in self.sems:
            self.sems[key] = self.nc.alloc_semaphore("d_" + key)
            self.cnt[key] = 0
        self._wait(q, self._deps(reads, writes))
        ins = self.engs[q].dma_start(out=out, in_=in_)
        self.cnt[key] += 16
        ins.then_inc(self.sems[key], 16)
        self._mark(key, reads, writes)
        self.ninst += 1
        self.ndma += 1

    def barrier(self):
        for eng in self.engs:
            deps = {k: c for k, c in self.cnt.items() if c > 0}
            e = self.engs[eng]
            seen = self.seen[eng]
            for k, c in deps.items():
                if seen.get(k, 0) >= c:
                    continue
                e.wait_ge(self.sems[k], c)
                seen[k] = c
                self.nwait += 1


class Seq:
    def __init__(self, name, T, ci, sample):
        self.name, self.T, self.ci, self.sample = name, T, ci, sample
        self.Tk = T + (256 if sample else 0)
        self.NB = T // 128


class KB:
    def __init__(self, Ts=4096, Tp=256, dbg=()):
        self.Ts, self.Tp = Ts, Tp
        self.dbg = set(dbg)
        nc = self.nc = bass.Bass("TRN2", target_bir_lowering=False)
        self.S = Sched(nc)
        self.seqs = [Seq("s", Ts, 0, True), Seq("p0", Tp, 1, False), Seq("p1", Tp, 1, False)]
        self.uid = 0
        self.dkeys = {}
        self.dkn = {}
        self._declare_io()
        self._build()

    def din(self, name, shape, dt=F32):
        return Tl(self.nc.dram_tensor(name, list(shape), dt, kind="ExternalInput").ap())

    def dout(self, name, shape, dt=F32):
        return Tl(self.nc.dram_tensor(name, list(shape), dt, kind="ExternalOutput").ap())

    def dscr(self, name, shape, dt=F32):
        kind = "ExternalOutput" if name in self.dbg else "Internal"
        return Tl(self.nc.dram_tensor(name, list(shape), dt, kind=kind).ap())

    def sb(self, shape, dt=F32, name=None, stack=None):
        self.uid += 1
        name = f"{name or 't'}_{self.uid}"
        g = self.nc.sbuf_tensor(name, list(shape), dt)
        h = (stack or self.gstack).enter_context(g)
        return Tl(h.ap())

    def sbn(self, n, shape, dt=F32, name=None, stack=None):
        return [self.sb(shape, dt, name, stack) for _ in range(n)]

    def ps(self):
        i = self.ps_i
        self.ps_i = (i + 1) % 8
        return self.psb[i]

    def dmak(self, t, q):
        k = (id(t.b), q)
        if k not in self.dkeys:
            n = self.dkn.get(q, 0)
            self.dkn[q] = n + 1
            self.dkeys[k] = (f"{q}{n}", t.b)
        return self.dkeys[k][0]

    def phase_end(self):
        self.S.barrier()
        self.dkeys = {}
        self.dkn = {}

    def load(self, q, dst, src, dstap=None, srcap=None):
        self.S.dma(q, dstap if dstap is not None else dst.ap, srcap if srcap is not None else src.ap,
                   reads=[src], writes=[dst], key=self.dmak(dst, q))

    def store(self, q, dst, src, dstap=None, srcap=None):
        self.S.dma(q, dstap if dstap is not None else dst.ap, srcap if srcap is not None else src.ap,
                   reads=[src], writes=[dst], key=self.dmak(src, q))

    def _declare_io(self):
        Ts, Tp = self.Ts, self.Tp
        I = self.I = {}
        I["x_s"] = self.din("x_s", [Ts, D])
        I["x_p0"] = self.din("x_p0", [Tp, D])
        I["x_p1"] = self.din("x_p1", [Tp, D])
        I["cond"] = self.din("cond", [128, 8, 2])
        I["ctx_ckvT"] = self.din("ctx_ckvT", [L, 128, 2, 256])
        I["ctx_krT"] = self.din("ctx_krT", [L, 64, 256])
        I["st_gdn"] = self.din("st_gdn", [L, 2, 128, 2, 64])
        I["st_hg"] = self.din("st_hg", [L, 2, 128, 2, 64])
        I["w_ada"] = self.din("w_ada", [L, 128, 8, 6 * D])
        I["b_ada"] = self.din("b_ada", [L, 128, 48])
        I["gains"] = self.din("gains", [L, 128, 4, 8])
        I["w_in"] = self.din("w_in", [L, 128, 8, IN_DIM])
        I["w_out"] = self.din("w_out", [L, 128, 8, D])
        I["w_f1"] = self.din("w_f1", [L, 128, 8, 2 * DFF])
        I["w_f2"] = self.din("w_f2", [L, 128, 22, D])
        I["w_uq"] = self.din("w_uq", [L, 128, 3, 768])
        I["w_ukv"] = self.din("w_ukv", [L, 128, 2, 1024])
        I["convd"] = self.din("convd", [L, 128, 6, 5, 128])
        I["gdn_ad"] = self.din("gdn_ad", [L, 128, 2, 8])
        I["normw"] = self.din("normw", [L, 128, 2, 64])
        I["lbraw"] = self.din("lbraw", [128, 2, 2])
        I["mlanw"] = self.din("mlanw", [L, 128, 5])
        I["kvnw_rep"] = self.din("kvnw_rep", [L, 128, 256])
        I["consts"] = self.din("consts", [128, 13, 128])
        I["scanmask"] = self.din("scanmask", [128, 256])
        I["ropeT"] = self.din("ropeT", [2, 64, Ts])
        O = self.O = {}
        O["y_s"] = self.dout("y_s", [Ts, D])
        O["y_p0"] = self.dout("y_p0", [Tp, D])
        O["y_p1"] = self.dout("y_p1", [Tp, D])
        O["ckv"] = self.dout("ckv", [2, L, Tp, 256])
        O["kr"] = self.dout("kr", [2, L, Tp, 64])
        O["sg"] = self.dout("sg", [2, L, 2, 4, 64, 64])
        O["sh"] = self.dout("sh", [2, L, 2, 4, 64, 64])
        Sc = self.Sc = {}
        Sc["gvec"] = self.dscr("gvec", [4, D])
        for s in self.seqs:
            n, T, Tk = s.name, s.T, s.Tk
            Sc["xmid_" + n] = self.dscr("xmid_" + n, [T, D])
            Sc["actT_" + n] = self.dscr("actT_" + n, [DFF, T], BF16)
            Sc["x1_" + n] = self.dscr("x1_" + n, [T, D])
            Sc["qkvT_" + n] = self.dscr("qkvT_" + n, [768, T], BF16)
            Sc["zab_" + n] = self.dscr("zab_" + n, [T, 272])
            Sc["hqT_" + n] = self.dscr("hqT_" + n, [256, T])
            Sc["hfT_" + n] = self.dscr("hfT_" + n, [512, T])
            Sc["hv_" + n] = self.dscr("hv_" + n, [T, 256], BF16)
            Sc["hg_" + n] = self.dscr("hg_" + n, [T, 256])
            Sc["QnT_" + n] = self.dscr("QnT_" + n, [4, 128, T], BF16)
            Sc["QrT_" + n] = self.dscr("QrT_" + n, [4, 64, T], BF16)
            Sc["KnT_" + n] = self.dscr("KnT_" + n, [4, 128, Tk], BF16)
            Sc["krT_" + n] = self.dscr("krT_" + n, [64, Tk], BF16)
            Sc["V_" + n] = self.dscr("V_" + n, [Tk, 512], BF16)
            Sc["mixT_" + n] = self.dscr("mixT_" + n, [1024, T], BF16)
            Sc["gsh_" + n] = self.dscr("gsh_" + n, [T // 128, 128, 1024], BF16)
            for d in range(2):
                Sc[f"og{d}_" + n] = self.dscr(f"og{d}_" + n, [T, 256])
                Sc[f"oh{d}_" + n] = self.dscr(f"oh{d}_" + n, [T, 256])

    def _build(self):
        nc, S = self.nc, self.S
        with ExitStack() as gst:
            self.gstack = gst
            pall = gst.enter_context(nc.psum_tensor("psall", [128, 8, 512], F32)).ap()
            self.pall = pall
            self.psb = [Tl(pall[:, i, :], Buf(f'ps{i}', excl=True)) for i in range(8)]
            self.ps_i = 0
            self._consts()
            for l in range(L):
                self.l = l
                self.phase0(l)
                if "stop0" in self.dbg:
                    break
                self.phaseA(l)
                if "stopA" in self.dbg:
                    break
                if "skipB" in self.dbg:
                    with ExitStack() as st:
                        z = self.sb([128, 512], BF16, "zz", st)
                        S.op("pool", lambda e: e.memset(z.ap, 0.0), writes=[z])
                        for s_ in self.seqs:
                            for c in range(4):
                                for t0 in range(0, s_.T, 512):
                                    n_ = min(512, s_.T - t0)
                                    self.store("sp", self.Sc["mixT_" + s_.name], z, dstap=self.Sc["mixT_" + s_.name].ap[c * 128:(c + 1) * 128, t0:t0 + n_], srcap=z.ap[:, 0:n_])
                        self.phase_end()
                else:
                    self.phaseB(l)
                if "stopB" in self.dbg:
                    break
                self.phaseC(l)
                if "stopC" in self.dbg:
                    break
                self.phaseD(l)
                if "stopD0" in self.dbg:
                    break
            self.phase_end()

    def _consts(self):
        S = self.S
        C = self.C = {}
        cf = self.sb([128, 13, 128], F32, "constf")
        self.load("sp", cf, self.I["consts"])
        names = ["ident", "ones", "blockones", "Lf", "Lb", "negf", "negb", "noteye", "hmfA", "hmbA", "RT", "hmfB", "hmbB"]
        for i, n in enumerate(names):
            C[n] = Tl(cf.ap[:, i, :], cf.b)
        cb = self.sb([128, 13, 128], BF16, "constb")
        S.op("dve", lambda e: e.tensor_copy(cb.ap, cf.ap), reads=[cf], writes=[cb])
        for i, n in enumerate(names):
            C[n + "_b"] = Tl(cb.ap[:, i, :], cb.b)
        m05 = self.sb([128, 512], F32, "m05")
        p05 = self.sb([128, 8], F32, "p05")
        S.op("pool", lambda e: e.memset(m05.ap, -0.5), writes=[m05])
        S.op("pool", lambda e: e.memset(p05.ap, 0.5), writes=[p05])
        C["m05"], C["p05"] = m05, p05
        epsb = self.sb([128, 1], F32, "epsb")
        S.op("pool", lambda e: e.memset(epsb.ap, EPS), writes=[epsb])
        C["epsb"] = epsb
        sm = self.sb([128, 256], F32, "scanmask")
        self.load("sp", sm, self.I["scanmask"])
        C["scanmask"] = sm
        cond = self.sb([128, 8, 2], F32, "cond")
        self.load("sp", cond, self.I["cond"])
        sc = self.sb([128, 8, 2], BF16, "scond")
        S.op("act", lambda e: e.activation(sc.ap, cond.ap, AF.Silu), reads=[cond], writes=[sc])
        C["scond"] = sc
        self.modF = self.sb([128, 48, 2], F32, "modF")
        self.scale1 = self.sb([128, 8, 2], F32, "scale1")
        self.scale2 = self.sb([128, 8, 2], F32, "scale2")
        self.gg = self.sb([128, 2, 2, 8], F32, "gg")
        self.lbv = self.sb([128, 2], F32, "lbv")
        self.oml = self.sb([128, 2], F32, "oml")
        self.noml = self.sb([128, 2], F32, "noml")
        self.qk2 = {s.name: self.sb([128, 2, 4], F32, "qk2") for s in self.seqs}

    def rstd(self, out, src, mul, n, parts=128):
        S = self.S
        src_ap = src.ap[0:parts, 0:n] if isinstance(src, Tl) else src[0:parts, 0:n]
        S.op("act", lambda e: e.activation(out.ap[0:parts, 0:n], src_ap, AF.Ln, bias=self.C["epsb"].ap[0:parts, :], scale=mul), reads=([src] if isinstance(src, Tl) else []) + [self.C["epsb"]], writes=[out])
        S.op("act", lambda e: e.activation(out.ap[0:parts, 0:n], out.ap[0:parts, 0:n], AF.Exp, scale=-0.5), reads=[out], writes=[out])

    def phase0(self, l):
        S, C, I = self.S, self.C, self.I
        with ExitStack() as st:
            wa = self.sbn(2, [128, 8, 1024], BF16, "wada", st)
            bada = self.sb([128, 48], F32, "bada", st)
            gains = self.sb([128, 4, 8], F32, "gains", st)
            self.load("sp", bada, I["b_ada"], srcap=I["b_ada"].ap[l])
            self.load("sp", gains, I["gains"], srcap=I["gains"].ap[l])
            mp = self.ps()
            mpv = mp.ap[:, 0:96].rearrange("p (j c) -> p j c", c=2)
            for g in range(6):
                w = wa[g % 2]
                self.load("pool", w, I["w_ada"], srcap=I["w_ada"].ap[l, :, :, g * 1024:(g + 1) * 1024])
                for j in range(8):
                    for k in range(8):
                        S.op("pe", lambda e: e.matmul(mpv[:, g * 8 + j, :], lhsT=w.ap[:, k, j * 128:(j + 1) * 128], rhs=C["scond"].ap[:, k, :],
                                                      start=(k == 0), stop=(k == 7)), reads=[w, C["scond"]], writes=[mp])
            modF = self.modF
            S.op("dve", lambda e: e.tensor_tensor(modF.ap, mpv, bada.ap.unsqueeze(2).to_broadcast([128, 48, 2]), ALU.add),
                 reads=[mp, bada], writes=[modF])
            for (dst, mi, gi) in ((self.scale1, 1, 0), (self.scale2, 4, 2)):
                S.op("dve", lambda e: e.scalar_tensor_tensor(dst.ap, modF.ap[:, mi * 8:(mi + 1) * 8, :], 1.0,
                                                             gains.ap[:, gi, :].unsqueeze(2).to_broadcast([128, 8, 2]), op0=ALU.add, op1=ALU.mult),
                     reads=[modF, gains], writes=[dst])
            for (w_, mi, gi) in ((0, 2, 1), (1, 5, 3)):
                S.op("dve", lambda e: e.tensor_tensor(self.gg.ap[:, w_].rearrange("p c j -> p j c"), modF.ap[:, mi * 8:(mi + 1) * 8, :],
                                                      gains.ap[:, gi, :].unsqueeze(2).to_broadcast([128, 8, 2]), ALU.mult),
                     reads=[modF, gains], writes=[self.gg])
            lr = self.sb([128, 2, 2], F32, "lbraw", st)
            self.load("sp", lr, I["lbraw"])
            g0 = self.sb([128, 2], F32, "g0", st)
            a_, b_ = (0, 1) if l == 0 else (1, 0)
            S.op("dve", lambda e: e.tensor_tensor(g0.ap, lr.ap[:, a_, :], lr.ap[:, b_, :], ALU.subtract), reads=[lr], writes=[g0])
            S.op("act", lambda e: e.activation(g0.ap, g0.ap, AF.Sigmoid), reads=[g0], writes=[g0])
            if l == 0:
                S.op("dve", lambda e: e.tensor_tensor(self.lbv.ap, g0.ap, g0.ap, ALU.subtract), reads=[g0], writes=[self.lbv])
            else:
                S.op("dve", lambda e: e.tensor_copy(self.lbv.ap, g0.ap), reads=[g0], writes=[self.lbv])
            S.op("dve", lambda e: e.tensor_scalar(self.oml.ap, self.lbv.ap, -1.0, 1.0, op0=ALU.mult, op1=ALU.add), reads=[self.lbv], writes=[self.oml])
            S.op("dve", lambda e: e.tensor_scalar(self.noml.ap, self.oml.ap, -1.0, None, op0=ALU.mult), reads=[self.oml], writes=[self.noml])
            self.phase_end()

    def norm_transpose(self, st_tiles, xsrc_fn, nblk, scale, bias_j0, ci, hT, banks=None):
        S, C = self.S, self.C
        xn_s, junk_s, ss, rs = st_tiles
        hps = banks if banks is not None else [self.ps() for _ in range(4)]
        hv = [p.ap.bitcast(BF16).rearrange("p (c t) -> p c t", c=2) for p in hps]
        for blk in range(nblk):
            xt = xsrc_fn(blk)
            jk = junk_s[blk % 2]
            S.op("act", lambda e: e.activation(jk.ap, xt.ap, AF.Square, accum_out=ss.ap[:, blk:blk + 1]), reads=[xt], writes=[jk, ss])
        self.rstd(rs, ss, 1.0 / D, nblk)
        for blk in range(nblk):
            xt = xsrc_fn(blk)
            xn = xn_s[blk % 2]
            if blk % 2 == 0:
                S.op("dve", lambda e: e.tensor_scalar(xn.ap, xt.ap, rs.ap[:, blk:blk + 1], None, op0=ALU.mult), reads=[xt, rs], writes=[xn])
            else:
                S.op("act", lambda e: e.activation(xn.ap, xt.ap, AF.Copy, scale=rs.ap[:, blk:blk + 1]), reads=[xt, rs], writes=[xn])
            for c in range(8):
                S.op("pe", lambda e: e.transpose(hv[c // 2][:, c % 2, blk * 128:(blk + 1) * 128], xn.ap[:, c * 128:(c + 1) * 128], C["ident_b"].ap),
                     reads=[xn, C["ident_b"]], writes=[hps[c // 2]])
        n = nblk * 128
        for c in range(8):
            src = hv[c // 2][:, c % 2, 0:n]
            sc_ap = scale.ap[:, c, ci:ci + 1]
            b_ap = self.modF.ap[:, bias_j0 + c, ci:ci + 1]
            if c % 2 == 0:
                S.op("act", lambda e: e.activation(hT.ap[:, c, 0:n], src, AF.Identity, scale=sc_ap, bias=b_ap),
                     reads=[hps[c // 2], scale, self.modF], writes=[hT])
            else:
                S.op("dve", lambda e: e.tensor_scalar(hT.ap[:, c, 0:n], src, sc_ap, b_ap, op0=ALU.mult, op1=ALU.add),
                     reads=[hps[c // 2], scale, self.modF], writes=[hT])

    def phaseA(self, l):
        S, C, I, Sc, O = self.S, self.C, self.I, self.Sc, self.O
        with ExitStack() as st:
            Win = self.sb([128, 8, IN_DIM], BF16, "Win", st)
            for k in range(0, 8, 2):
                self.S.dma("pool", Win.ap[:, k:k + 2, :], I["w_in"].ap[l, :, k:k + 2, :], reads=[I["w_in"]], writes=[Win], key=f"win{k}")
            Wuq = self.sb([128, 3, 768], BF16, "Wuq", st)
            self.load("pool", Wuq, I["w_uq"], srcap=I["w_uq"].ap[l])
            Wukv = self.sb([128, 2, 1024], BF16, "Wukv", st)
            self.load("pool", Wukv, I["w_ukv"], srcap=I["w_ukv"].ap[l])
            nw = self.sb([128, 5], F32, "mlanw", st)
            self.load("sp", nw, I["mlanw"], srcap=I["mlanw"].ap[l])
            kvrep = self.sb([128, 256], F32, "kvrep", st)
            self.load("sp", kvrep, I["kvnw_rep"], srcap=I["kvnw_rep"].ap[l])
            xt_s = self.sbn(4, [128, D], F32, "xt", st)
            nt_tiles = (self.sbn(2, [128, D], BF16, "xn", st), self.sbn(1, [128, D], BF16, "junk", st) * 2,
                        self.sb([128, 4], F32, "ss", st), self.sb([128, 4], F32, "rs", st))
            hT = self.sb([128, 8, 512], BF16, "hT", st)
            qkv_st = self.sbn(1, [128, 6, 512], BF16, "qkvst", st) * 2
            hq_st = self.sbn(1, [128, 2, 512], F32, "hqst", st) * 2
            hf_st = self.sbn(1, [128, 2, 512], F32, "hfst", st) * 2
            zab_st = self.sbn(1, [128, 4, 272], F32, "zabst", st) * 2
            hv_st = self.sbn(1, [128, 4, 256], BF16, "hvst", st) * 2
            hg_st = self.sbn(1, [128, 4, 256], F32, "hgst", st) * 2
            mla2 = self.sbn(2, [128, 6, 512], F32, "mla", st)
            sq = self.sb([128, 3, 512], F32, "sq", st)
            rq = self.sbn(2, [128, 512], F32, "rq", st)
            cn = self.sb([128, 5, 512], BF16, "cn", st)
            Qn_st = self.sbn(1, [128, 4, 512], BF16, "Qnst", st) * 2
            Qr_st = self.sbn(1, [64, 4, 512], BF16, "Qrst", st) * 2
            Kn_st = self.sbn(1, [128, 4, 512], BF16, "Knst", st) * 2
            kr_st = self.sbn(1, [64, 512], BF16, "krst", st) * 2
            V_st = self.sbn(1, [128, 4, 512], BF16, "Vst", st) * 2
            xr = self.sbn(1, [64, 512], BF16, "xr", st) * 2
            t1 = self.sbn(1, [64, 512], F32, "t1", st) * 2
            t2 = self.sbn(1, [64, 512], F32, "t2", st) * 2
            sqb = self.sb([128, 4, 512], BF16, "sqb", st)
            sqr = self.sb([64, 4, 512], BF16, "sqr", st)
            mx = self.sb([128, 1], F32, "mx", st)
            pck = self.sbn(1, [128, 4, 320], F32, "pck", st) * 2
            pss = self.sb([128, 4], F32, "pss", st)
            prs = self.sb([128, 4], F32, "prs", st)
            ctxc = self.sb([128, 2, 256], BF16, "ctxc", st)
            ctxk = self.sb([64, 256], BF16, "ctxk", st)
            ropet = self.sb([64, 2, 512], F32, "ropet", st)
            cos, sin = ropet.ap[:, 0, :], ropet.ap[:, 1, :]
            ev = [0]
            cur = [None]
            rot = {"a": 5, "b": 1}

            def ps1():
                rot["a"] = (rot["a"] + 1) % 6
                return self.psb[rot["a"]]

            def ps2():
                rot["b"] ^= 1
                return self.psb[6 + rot["b"]]

            def PS():
                return cur[0]()

            def drive(chains):
                chains = list(chains)
                while chains:
                    for c in list(chains):
                        cur[0] = c[1]
                        try:
                            next(c[0])
                        except StopIteration:
                            chains.remove(c)

            def evac(dst_ap, dst_tl, src_ps, func=None):
                ev[0] += 1
                if ev[0] % 2:
                    S.op("act", lambda e: e.copy(dst_ap, src_ps.ap if isinstance(src_ps, Tl) else src_ps[0]), reads=[src_ps if isinstance(src_ps, Tl) else src_ps[1]], writes=[dst_tl])
                else:
                    S.op("dve", lambda e: e.tensor_copy(dst_ap, src_ps.ap if isinstance(src_ps, Tl) else src_ps[0]), reads=[src_ps if isinstance(src_ps, Tl) else src_ps[1]], writes=[dst_tl])

            def fm_proj(col0, M, n, W=Win, rhs=hT, nk=8):
                p = PS()
                for k in range(nk):
                    S.op("pe", lambda e: e.matmul(p.ap[0:M, 0:n], lhsT=W.ap[:, k, col0:col0 + M], rhs=rhs.ap[:, k, 0:n], start=(k == 0), stop=(k == nk - 1)),
                         reads=[W, rhs], writes=[p])
                return p

            def maxnorm(sqn_tl, sqr_tl, n, col):
                for h in range(4):
                    p = PS()
                    S.op("pe", lambda e: e.matmul(p.ap[:, 0:n], lhsT=C["ones_b"].ap, rhs=sqn_tl.ap[:, h, 0:n], start=True, stop=False), reads=[C["ones_b"], sqn_tl], writes=[p])
                    rr = sqr_tl.ap[:, h, 0:n] if len(sqr_tl.ap.shape) == 3 else sqr_tl.ap[:, 0:n]
                    S.op("pe", lambda e: e.matmul(p.ap[:, 0:n], lhsT=C["ones_b"].ap[0:64, :], rhs=rr, start=False, stop=True), reads=[C["ones_b"], sqr_tl], writes=[p])
                    S.op("dve", lambda e: e.tensor_reduce(mx.ap, p.ap[:, 0:n], AX.X, ALU.max), reads=[p], writes=[mx])
                    S.op("dve", lambda e: e.tensor_tensor(qk2.ap[:, col, h:h + 1], qk2.ap[:, col, h:h + 1], mx.ap, ALU.max), reads=[mx, qk2], writes=[qk2])

            def rope_apply(dst_ap, dst_tl, src_ap, src_tl, n, tok0, slot):
                x_, a_, b_ = xr[slot], t1[slot], t2[slot]
                S.op("act", lambda e: e.copy(x_.ap[:, 0:n], src_ap), reads=[src_tl], writes=[x_])
                rp = PS()
                S.op("pe", lambda e: e.matmul(rp.ap[0:64, 0:n], lhsT=C["RT_b"].ap[0:64, 0:64], rhs=x_.ap[:, 0:n], start=True, stop=True), reads=[C["RT_b"], x_], writes=[rp])
                S.op("dve", lambda e: e.tensor_tensor(a_.ap[:, 0:n], rp.ap[0:64, 0:n], sin[:, 0:n], ALU.mult), reads=[rp, ropet], writes=[a_])
                S.op("dve", lambda e: e.tensor_tensor(b_.ap[:, 0:n], src_ap, cos[:, 0:n], ALU.mult), reads=[src_tl, ropet], writes=[b_])
                S.op("pool", lambda e: e.tensor_tensor(dst_ap, a_.ap[:, 0:n], b_.ap[:, 0:n], ALU.add), reads=[a_, b_], writes=[dst_tl])

            def kv_from_cn(s, n, tok0, it, cn_tl, kc0, kr_src_ap, kr_src_tl, do_rope, key_off):
                nm = s.name
                Kn, krs, Vs = Kn_st[it % 2], kr_st[it % 2], V_st[it % 2]
                for h in range(4):
                    p = PS()
                    for k in range(2):
                        S.op("pe", lambda e: e.matmul(p.ap[:, 0:n], lhsT=Wukv.ap[:, k, h * 128:(h + 1) * 128], rhs=cn_tl.ap[:, kc0 + k, 0:n], start=(k == 0), stop=(k == 1)),
                             reads=[Wukv, cn_tl], writes=[p])
                    evac(Kn.ap[:, h, 0:n], Kn, p.ap[:, 0:n] if False else (p.ap[:, 0:n], p))
                self.store("sp", Sc["KnT_" + nm], Kn, dstap=Sc["KnT_" + nm].ap[:, :, key_off + tok0:key_off + tok0 + n].rearrange("h p t -> p h t"), srcap=Kn.ap[:, :, 0:n])
                if do_rope:
                    rope_apply(krs.ap[:, 0:n], krs, kr_src_ap, kr_src_tl, n, tok0, 0)
                else:
                    S.op("act", lambda e: e.copy(krs.ap[:, 0:n], kr_src_ap), reads=[kr_src_tl], writes=[krs])
                self.store("sp", Sc["krT_" + nm], krs, dstap=Sc["krT_" + nm].ap[:, key_off + tok0:key_off + tok0 + n], srcap=krs.ap[:, 0:n])
                for blk in range(n // 128):
                    p = PS()
                    for k in range(2):
                        S.op("pe", lambda e: e.matmul(p.ap, lhsT=cn_tl.ap[:, kc0 + k, blk * 128:(blk + 1) * 128], rhs=Wukv.ap[:, k, 512:1024], start=(k == 0), stop=(k == 1)),
                             reads=[Wukv, cn_tl], writes=[p])
                    evac(Vs.ap[:, blk, :], Vs, p)
                self.store("sp", Sc["V_" + nm], Vs, dstap=Sc["V_" + nm].ap[key_off + tok0:key_off + tok0 + n, :].rearrange("(b p) n -> p b n", p=128), srcap=Vs.ap[:, 0:n // 128, :])
                S.op("pool", lambda e: e.tensor_tensor(sqb.ap[:, :, 0:n], Kn.ap[:, :, 0:n], Kn.ap[:, :, 0:n], ALU.mult), reads=[Kn], writes=[sqb])
                S.op("pool", lambda e: e.tensor_tensor(sqr.ap[:, 0, 0:n], krs.ap[:, 0:n], krs.ap[:, 0:n], ALU.mult), reads=[krs], writes=[sqr])
                maxnorm(sqb, Tl(sqr.ap[:, 0, :], sqr.b), n, 1)

            for s in self.seqs:
                nm, T, ci = s.name, s.T, s.ci
                qk2 = self.qk2[nm]
                S.op("pool", lambda e: e.memset(qk2.ap, 0.0), writes=[qk2])
                X = I["x_" + nm] if l == 0 else Sc["x1_" + nm]
                TT = min(512, T)
                nblk = TT // 128
                it = 0
                cur[0] = ps2
                if s.sample:
                    self.load("pool", ctxc, I["ctx_ckvT"], srcap=I["ctx_ckvT"].ap[l])
                    self.load("pool", ctxk, I["ctx_krT"], srcap=I["ctx_krT"].ap[l])
                    kv_from_cn(s, 256, 0, 0, ctxc, 0, ctxk.ap, ctxk, False, T)
                def xload(ti_):
                    for blk in range(nblk):
                        self.load("sp", xt_s[blk], X, srcap=X.ap[ti_ * TT + blk * 128:ti_ * TT + (blk + 1) * 128, :])
                xload(0)

                def stage1(ti):
                    tok0 = ti * TT
                    n = TT
                    it = ti
                    mla = mla2[ti % 2]

                    def xsrc(blk):
                        return xt_s[blk]
                    self.norm_transpose(nt_tiles, xsrc, nblk, self.scale1, 0, ci, hT, banks=[PS() for _ in range(4)])
                    yield
                    if ti + 1 < T // TT:
                        xload(ti + 1)
                        yield
                    n = TT
                    qs = qkv_st[it % 2]
                    for c in range(6):
                        p = fm_proj(c * 128, 128, n)
                        yield
                        evac(qs.ap[:, c, 0:n], qs, (p.ap[:, 0:n], p))
                        yield
                    self.store("sp", Sc["qkvT_" + nm], qs, dstap=Sc["qkvT_" + nm].ap[:, tok0:tok0 + n].rearrange("(c p) t -> p c t", p=128), srcap=qs.ap[:, :, 0:n])
                    yield
                    hs = hq_st[it % 2]
                    for c in range(2):
                        p = fm_proj(768 + c * 128, 128, n)
                        yield
                        evac(hs.ap[:, c, 0:n], hs, (p.ap[:, 0:n], p))
                        yield
                    self.store("sp", Sc["hqT_" + nm], hs, dstap=Sc["hqT_" + nm].ap[:, tok0:tok0 + n].rearrange("(c p) t -> p c t", p=128), srcap=hs.ap[:, :, 0:n])
                    yield
                    fs = hf_st[it % 2]
                    for c in range(4):
                        p = fm_proj(1024 + c * 128, 128, n)
                        yield
                        evac(fs.ap[:, c % 2, 0:n], fs, (p.ap[:, 0:n], p))
                        yield
                        if c % 2 == 1:
                            self.store("sp", Sc["hfT_" + nm], fs, dstap=Sc["hfT_" + nm].ap[(c - 1) * 128:(c + 1) * 128, tok0:tok0 + n].rearrange("(c p) t -> p c t", p=128), srcap=fs.ap[:, :, 0:n])
                            yield
                    for c in range(5):
                        p = fm_proj(1536 + c * 128, 128, n)
                        yield
                        evac(mla.ap[:, c, 0:n], mla, (p.ap[:, 0:n], p))
                        yield
                    p = fm_proj(2176, 64, n)
                    yield
                    evac(mla.ap[0:64, 5, 0:n], mla, (p.ap[0:64, 0:n], p))
                    yield
                    zs, hvs, hgs = zab_st[it % 2], hv_st[it % 2], hg_st[it % 2]
                    for blk in range(nblk):
                        p = PS()
                        for k in range(8):
                            S.op("pe", lambda e: e.matmul(p.ap[:, 0:272], lhsT=hT.ap[:, k, blk * 128:(blk + 1) * 128], rhs=Win.ap[:, k, 2240:2512], start=(k == 0), stop=(k == 7)),
                                 reads=[Win, hT], writes=[p])
                            yield
                        evac(zs.ap[:, blk, :], zs, (p.ap[:, 0:272], p))
                        yield
                        p = PS()
                        for k in range(8):
                            S.op("pe", lambda e: e.matmul(p.ap, lhsT=hT.ap[:, k, blk * 128:(blk + 1) * 128], rhs=Win.ap[:, k, 2512:3024], start=(k == 0), stop=(k == 7)),
                                 reads=[Win, hT], writes=[p])
                            yield
                        S.op("act", lambda e: e.copy(hvs.ap[:, blk, :], p.ap[:, 0:256]), reads=[p], writes=[hvs])
                        yield
                        S.op("dve", lambda e: e.tensor_copy(hgs.ap[:, blk, :], p.ap[:, 256:512]), reads=[p], writes=[hgs])
                        yield
                    rr = lambda t_: t_.ap[tok0:tok0 + n, :].rearrange("(b p) n -> p b n", p=128)
                    self.store("sp", Sc["zab_" + nm], zs, dstap=rr(Sc["zab_" + nm]), srcap=zs.ap[:, 0:nblk, :])
                    yield
                    self.store("sp", Sc["hv_" + nm], hvs, dstap=rr(Sc["hv_" + nm]), srcap=hvs.ap[:, 0:nblk, :])
                    yield
                    self.store("sp", Sc["hg_" + nm], hgs, dstap=rr(Sc["hg_" + nm]), srcap=hgs.ap[:, 0:nblk, :])
                    yield
                    if not s.sample:
                        pk = pck[it % 2]
                        b_idx = 0 if nm == "p0" else 1
                        for blk in range(nblk):
                            p = PS()
                            for k in range(8):
                                S.op("pe", lambda e: e.matmul(p.ap[:, 0:320], lhsT=hT.ap[:, k, blk * 128:(blk + 1) * 128], rhs=Win.ap[:, k, 1920:2240], start=(k == 0), stop=(k == 7)),
                                     reads=[Win, hT], writes=[p])
                                yield
                            S.op("dve", lambda e: e.tensor_copy(pk.ap[:, blk, :], p.ap[:, 0:320]), reads=[p], writes=[pk])
                            yield
                            jk = nt_tiles[1][0]
                            S.op("act", lambda e: e.activation(jk.ap[:, 0:256], pk.ap[:, blk, 0:256], AF.Square, accum_out=pss.ap[:, blk:blk + 1]), reads=[pk], writes=[jk, pss])
                            yield
                        self.rstd(prs, pss, 1.0 / 256, nblk)
                        yield
                        for blk in range(nblk):
                            S.op("dve", lambda e: e.scalar_tensor_tensor(pk.ap[:, blk, 0:256], pk.ap[:, blk, 0:256], prs.ap[:, blk:blk + 1], kvrep.ap, op0=ALU.mult, op1=ALU.mult),
                                 reads=[pk, prs, kvrep], writes=[pk])
                            yield
                        self.store("sp", O["ckv"], pk, dstap=O["ckv"].ap[b_idx, l, tok0:tok0 + n, :].rearrange("(b p) n -> p b n", p=128), srcap=pk.ap[:, 0:nblk, 0:256])
                        yield
                        self.store("sp", O["kr"], pk, dstap=O["kr"].ap[b_idx, l, tok0:tok0 + n, :].rearrange("(b p) n -> p b n", p=128), srcap=pk.ap[:, 0:nblk, 256:320])
                        yield

                def stage2(ti):
                    tok0 = ti * TT
                    n = TT
                    it = ti
                    mla = mla2[ti % 2]
                    if s.sample:
                        self.load("sp", ropet, I["ropeT"], dstap=ropet.ap[:, :, 0:TT], srcap=I["ropeT"].ap[:, :, tok0:tok0 + TT].rearrange("a p t -> p a t"))
                        yield
                    for (c0, nc_, rqt) in ((0, 3, rq[0]), (3, 2, rq[1])):
                        S.op("act", lambda e: e.activation(sq.ap[:, 0:nc_, 0:n], mla.ap[:, c0:c0 + nc_, 0:n], AF.Square), reads=[mla], writes=[sq])
                        yield
                        p = PS()
                        for c in range(nc_):
                            S.op("pe", lambda e: e.matmul(p.ap[:, 0:n], lhsT=C["ones"].ap, rhs=sq.ap[:, c, 0:n], start=(c == 0), stop=(c == nc_ - 1)), reads=[C["ones"], sq], writes=[p])
                            yield
                        self.rstd(rqt, p, 1.0 / (128 * nc_), n)
                        yield
                        for c in range(nc_):
                            S.op("dve", lambda e: e.scalar_tensor_tensor(cn.ap[:, c0 + c, 0:n], mla.ap[:, c0 + c, 0:n], nw.ap[:, c0 + c:c0 + c + 1], rqt.ap[:, 0:n], op0=ALU.mult, op1=ALU.mult),
                                 reads=[mla, nw, rqt], writes=[cn])
                            yield
                    Qn, Qr = Qn_st[it % 2], Qr_st[it % 2]
                    for h in range(4):
                        p = fm_proj(h * 192, 128, n, W=Wuq, rhs=cn, nk=3)
                        yield
                        evac(Qn.ap[:, h, 0:n], Qn, (p.ap[:, 0:n], p))
                        yield
                        p = fm_proj(h * 192 + 128, 64, n, W=Wuq, rhs=cn, nk=3)
                        yield
                        if s.sample:
                            rope_apply(Qr.ap[:, h, 0:n], Qr, p.ap[0:64, 0:n], p, n, tok0, h % 2)
                            yield
                        else:
                            evac(Qr.ap[:, h, 0:n], Qr, (p.ap[0:64, 0:n], p))
                            yield
                    self.store("sp", Sc["QnT_" + nm], Qn, dstap=Sc["QnT_" + nm].ap[:, :, tok0:tok0 + n].rearrange("h p t -> p h t"), srcap=Qn.ap[:, :, 0:n])
                    yield
                    self.store("sp", Sc["QrT_" + nm], Qr, dstap=Sc["QrT_" + nm].ap[:, :, tok0:tok0 + n].rearrange("h p t -> p h t"), srcap=Qr.ap[:, :, 0:n])
                    yield
                    S.op("pool", lambda e: e.tensor_tensor(sqb.ap[:, :, 0:n], Qn.ap[:, :, 0:n], Qn.ap[:, :, 0:n], ALU.mult), reads=[Qn], writes=[sqb])
                    yield
                    S.op("pool", lambda e: e.tensor_tensor(sqr.ap[:, :, 0:n], Qr.ap[:, :, 0:n], Qr.ap[:, :, 0:n], ALU.mult), reads=[Qr], writes=[sqr])
                    yield
                    maxnorm(sqb, sqr, n, 0)
                    yield
                    kv_from_cn(s, n, tok0, it, cn, 3, mla.ap[0:64, 5, 0:n], mla, s.sample, 0)
                    yield

                ntile = T // TT
                drive([(stage1(0), ps1)])
                for ti in range(ntile):
                    ch = [(stage2(ti), ps2)]
                    if ti + 1 < ntile:
                        ch.insert(0, (stage1(ti + 1), ps1))
                    drive(ch)
            self.phase_end()

    def phaseB(self, l):
        S, C, I, Sc, O = self.S, self.C, self.I, self.Sc, self.O
        bc3 = lambda ap, shape, axis: ap.unsqueeze(axis).to_broadcast(shape)
        for s in self.seqs:
            nm, T, NB = s.name, s.T, s.NB
            bidx = 0 if nm == "p0" else 1
            with ExitStack() as st:
                sb = lambda shape, dt=F32, name=None: self.sb(shape, dt, name, st)
                convd = sb([128, 6, 5, 128], BF16, "convd")
                self.load("pool", convd, I["convd"], srcap=I["convd"].ap[l])
                gad = sb([128, 2, 8], F32, "gad")
                self.load("sp", gad, I["gdn_ad"], srcap=I["gdn_ad"].ap[l])
                ab = sb([128, NB, 16], F32, "ab")
                self.load("sp", ab, Sc["zab_" + nm], srcap=Sc["zab_" + nm].ap[:, 256:272].rearrange("(b p) n -> p b n", p=128))
                g_all, beta_all = sb([128, NB, 8], F32, "g_all"), sb([128, NB, 8], F32, "beta_all")
                xa, nx, l1 = sb([128, NB, 8], F32, "xa"), sb([128, NB, 8], F32, "nx"), sb([128, NB, 8], F32, "l1")
                eA = sb([128, 8], F32, "eA")
                S.op("dve", lambda e: e.tensor_tensor(xa.ap, ab.ap[:, :, 0:8], bc3(gad.ap[:, 1, :], [128, NB, 8], 1), ALU.add), reads=[ab, gad], writes=[xa])
                S.op("dve", lambda e: e.tensor_scalar(nx.ap, xa.ap, -1.0, None, op0=ALU.mult), reads=[xa], writes=[nx])
                S.op("dve", lambda e: e.tensor_tensor(nx.ap, nx.ap, xa.ap, ALU.min), reads=[xa, nx], writes=[nx])
                S.op("act", lambda e: e.activation(l1.ap, nx.ap, AF.Exp), reads=[nx], writes=[l1])
                S.op("act", lambda e: e.activation(l1.ap, l1.ap, AF.Ln, bias=1.0), reads=[l1], writes=[l1])
                S.op("dve", lambda e: e.tensor_scalar(xa.ap, xa.ap, 0.0, None, op0=ALU.max), reads=[xa], writes=[xa])
                S.op("dve", lambda e: e.tensor_tensor(xa.ap, xa.ap, l1.ap, ALU.add), reads=[xa, l1], writes=[xa])
                S.op("act", lambda e: e.activation(eA.ap, gad.ap[:, 0, :], AF.Exp), reads=[gad], writes=[eA])
                S.op("dve", lambda e: e.scalar_tensor_tensor(g_all.ap, xa.ap, -1.0, bc3(eA.ap, [128, NB, 8], 1), op0=ALU.mult, op1=ALU.mult), reads=[xa, eA], writes=[g_all])
                S.op("act", lambda e: e.activation(beta_all.ap, ab.ap[:, :, 8:16], AF.Sigmoid), reads=[ab], writes=[beta_all])
                gsh_t = [Tl(Sc["gsh_" + nm].ap[b_]) for b_ in range(NB)]

                def make_dir(d):
                    def prep_set():
                        t = {}
                        t["rawt"] = sb([128, 6, 132], BF16, "rawt")
                        t["qk_f"], t["sq"], t["rn"] = sb([128, 4, 128], F32, "qk_f"), sb([128, 4, 128], F32, "sq"), sb([128, 4, 128], F32, "rn")
                        t["vT"] = sb([128, 2, 128], BF16, "vT")
                        shr = sb([128, 1024], BF16, "shr")
                        t["shr"] = shr
                        t["qhT"] = Tl(shr.ap[:, 0:256].rearrange("p (c t) -> p c t", c=2), shr.b)
                        t["khT"] = Tl(shr.ap[:, 256:512].rearrange("p (c t) -> p c t", c=2), shr.b)
                        t["kv_tm"] = Tl(shr.ap[:, 512:1024].rearrange("p (a c) -> p a c", a=2), shr.b)
                        for n_ in ("gm", "Dm", "Dst", "egc", "tq"):
                            t[n_] = sb([128, 4, 128], F32, n_)
                        for n_ in ("gc_sb", "eg", "bneg", "be", "tl"):
                            t[n_] = sb([128, 4], F32, n_)
                        t["Qb"], t["Pb"], t["Yb"] = self.sbn(2, [128, 4, 128], F32, "Qb", st), self.sbn(2, [128, 4, 128], F32, "Pb", st), self.sbn(2, [128, 4, 128], F32, "Yb", st)
                        t["att"] = sb([128, 4, 128], BF16, "att")
                        t["vb"], t["kbe"] = sb([128, 4, 64], F32, "vb"), sb([128, 4, 64], F32, "kbe")
                        t["psi"] = [0]
                        return t
                    PSETS = [prep_set()]
                    G_ = {k: self.sbn(3, shp, dt, k, st) for k, shp, dt in (("wT", [128, 2, 128], BF16), ("qdT", [128, 2, 128], BF16), ("attT", [128, 4, 128], BF16),
                                                                              ("ktail", [128, 4, 64], BF16), ("u", [128, 4, 64], F32), ("glS", [128, 2, 2], F32))}
                    Sg, Sgb = sb([128, 2, 64], F32, "Sg"), sb([128, 2, 64], BF16, "Sgb")
                    vnew = sb([128, 4, 64], BF16, "vnew")
                    ogst = self.sbn(2, [128, 256], F32, "ogst", st)
                    hq, hf = sb([128, 2, 128], F32, "hq"), sb([128, 2, 128], F32, "hf")
                    sg, gate, kk, lf, bcp, bcr, arg, E1, E2 = (sb([128, 2, 128], F32, n_) for n_ in ("sg", "gate", "kk", "lf", "bcp", "bcr", "arg", "E1", "E2"))
                    kt2 = sb([128, 2, 128], BF16, "kt2")
                    H_ = {k: self.sbn(2, shp, dt, k, st) for k, shp, dt in (("qtT", [128, 2, 128], BF16), ("ktT", [128, 2, 128], BF16), ("hattT", [128, 4, 128], BF16),
                                                                              ("k2tm", [128, 256], BF16), ("hv", [128, 256], BF16), ("e_r", [128, 4], F32), ("e_l", [128, 4], F32))}
                    qtf, ktf = sb([128, 2, 128], BF16, "qtf"), sb([128, 2, 128], BF16, "ktf")
                    fr = [sb([128, 4, 128], F32, "fr0"), sb([128, 4, 128], F32, "fr1")]
                    Sh, Shp = sb([128, 2, 64], F32, "Sh"), sb([128, 2, 64], BF16, "Shp")
                    ohst = self.sbn(2, [128, 256], F32, "ohst", st)

                    def heads():
                        for h in range(4):
                            yield h, h // 2, (h % 2) * 64

                    def run_chains(chains):
                        chains = list(chains)
                        while chains:
                            for c in list(chains):
                                try:
                                    next(c)
                                except StopIteration:
                                    chains.remove(c)

                    def gdn_prep(b, d, sl, ch):
                        P_ = PSETS[ch]
                        rawt, qk_f, sq, rn, vT, qhT, khT, kv_tm = (P_[k] for k in ("rawt", "qk_f", "sq", "rn", "vT", "qhT", "khT", "kv_tm"))
                        gm, Dm, Dst, egc, tq, gc_sb, eg, bneg, be, tl = (P_[k] for k in ("gm", "Dm", "Dst", "egc", "tq", "gc_sb", "eg", "bneg", "be", "tl"))
                        Qb, Pb, Yb, att, vb, kbe = (P_[k] for k in ("Qb", "Pb", "Yb", "att", "vb", "kbe"))

                        def gp_ps():
                            P_["psi"][0] ^= 1
                            return self.psb[2 * d + P_["psi"][0]]
                        t0 = b * 128
                        Lm = C["Lf"] if d == 0 else C["Lb"]
                        negm = C["negf"] if d == 0 else C["negb"]
                        lasts = (63, 127) if d == 0 else (0, 64)
                        shr = P_["shr"]
                        rnd = (b - 1, NB - 2 - b)
                        mine, other = rnd[d], rnd[1 - d]
                        if other < mine:
                            self.load("sp", shr, gsh_t[b])
                            yield
                        else:
                            lo, hi = max(t0 - 2, 0), min(t0 + 130, T)
                            S.op("pool", lambda e: e.memset(rawt.ap, 0.0), writes=[rawt])
                            yield
                            self.load("sp", rawt, Sc["qkvT_" + nm], dstap=rawt.ap[:, :, lo - (t0 - 2):hi - (t0 - 2)], srcap=Sc["qkvT_" + nm].ap[:, lo:hi].rearrange("(c p) t -> p c t", p=128))
                            yield
                            cA, cB = gp_ps(), gp_ps()
                            for cc in range(6):
                                cp_ = cA if cc < 4 else cB
                                o_ap = cp_.ap[:, (cc % 4) * 128:(cc % 4 + 1) * 128]
                                for j in range(5):
                                    S.op("pe", lambda e: e.matmul(o_ap, lhsT=convd.ap[:, cc, j, :], rhs=rawt.ap[:, cc, j:j + 128], start=(j == 0), stop=(j == 4)), reads=[convd, rawt], writes=[cp_])
                                    yield
                            S.op("act", lambda e: e.activation(qk_f.ap.rearrange("p c t -> p (c t)"), cA.ap, AF.Silu), reads=[cA], writes=[qk_f])
                            yield
                            S.op("act", lambda e: e.activation(vT.ap.rearrange("p c t -> p (c t)"), cB.ap[:, 0:256], AF.Silu), reads=[cB], writes=[vT])
                            yield
                            S.op("pool", lambda e: e.tensor_tensor(sq.ap, qk_f.ap, qk_f.ap, ALU.mult), reads=[qk_f], writes=[sq])
                            yield
                            sp_ = gp_ps()
                            S.op("pe", lambda e: e.matmul(sp_.ap, lhsT=C["blockones"].ap, rhs=sq.ap.rearrange("p c t -> p (c t)"), start=True, stop=True), reads=[C["blockones"], sq], writes=[sp_])
                            yield
                            self.rstd(Tl(rn.ap.rearrange("p c t -> p (c t)"), rn.b), sp_, 1.0, 512)
                            yield
                            S.op("dve", lambda e: e.scalar_tensor_tensor(qhT.ap, qk_f.ap[:, 0:2, :], 0.125, rn.ap[:, 0:2, :], op0=ALU.mult, op1=ALU.mult), reads=[qk_f, rn], writes=[qhT])
                            yield
                            S.op("dve", lambda e: e.tensor_tensor(khT.ap, qk_f.ap[:, 2:4, :], rn.ap[:, 2:4, :], ALU.mult), reads=[qk_f, rn], writes=[khT])
                            yield
                            tp = gp_ps()
                            tpv = tp.ap.bitcast(BF16)[:, 0:512].rearrange("p (a c) -> p a c", a=4)
                            for a_, src in enumerate((khT.ap[:, 0, :], khT.ap[:, 1, :], vT.ap[:, 0, :], vT.ap[:, 1, :])):
                                S.op("pe", lambda e: e.transpose(tpv[:, a_, :], src, C["ident_b"].ap), reads=[khT, vT, C["ident_b"]], writes=[tp])
                                yield
                            S.op("dve", lambda e: e.tensor_copy(kv_tm.ap.rearrange("p a c -> p (a c)"), tp.ap.bitcast(BF16)[:, 0:512]), reads=[tp], writes=[kv_tm])
                            yield
                            if mine < other:
                                self.store("pool", gsh_t[b], shr)
                                yield
                        g_d = g_all.ap[:, b, d * 4:(d + 1) * 4]
                        b_d = beta_all.ap[:, b, d * 4:(d + 1) * 4]
                        gp = gp_ps()
                        S.op("pe", lambda e: e.matmul(gp.ap[:, 0:4], lhsT=Lm.ap, rhs=g_d, start=True, stop=True), reads=[Lm, g_all], writes=[gp])
                        yield
                        S.op("dve", lambda e: e.tensor_copy(gc_sb.ap, gp.ap[:, 0:4]), reads=[gp], writes=[gc_sb])
                        yield
                        S.op("dve", lambda e: e.tensor_tensor(gm.ap, bc3(Lm.ap, [128, 4, 128], 1), bc3(g_d, [128, 4, 128], 2), ALU.mult), reads=[Lm, g_all], writes=[gm])
                        yield
                        gb = gp_ps()
                        gbv = gb.ap.rearrange("p (h j) -> p h j", h=4)
                        S.op("pe", lambda e: e.matmul(gb.ap, lhsT=C["ones"].ap, rhs=gm.ap.rearrange("p h j -> p (h j)"), start=True, stop=True), reads=[C["ones"], gm], writes=[gb])
                        yield
                        S.op("dve", lambda e: e.scalar_tensor_tensor(Dm.ap, gbv, -1.0, bc3(gc_sb.ap, [128, 4, 128], 2), op0=ALU.mult, op1=ALU.add), reads=[gb, gc_sb], writes=[Dm])
                        yield
                        S.op("act", lambda e: e.activation(egc.ap, gbv, AF.Exp), reads=[gb], writes=[egc])
                        yield
                        for j_ in range(2):
                            S.op("dve", lambda e: e.tensor_tensor(tl.ap[j_ * 64:(j_ + 1) * 64], gbv[j_ * 64:(j_ + 1) * 64, :, lasts[j_]], gc_sb.ap[j_ * 64:(j_ + 1) * 64], ALU.subtract), reads=[gb, gc_sb], writes=[tl])
                            yield
                        S.op("pool", lambda e: e.tensor_tensor(Dm.ap, Dm.ap, bc3(negm.ap, [128, 4, 128], 1), ALU.add), reads=[Dm, negm], writes=[Dm])
                        yield
                        S.op("act", lambda e: e.activation(Dm.ap, Dm.ap, AF.Exp), reads=[Dm], writes=[Dm])
                        yield
                        S.op("act", lambda e: e.activation(tl.ap, tl.ap, AF.Exp), reads=[tl], writes=[tl])
                        yield
                        S.op("act", lambda e: e.activation(eg.ap, gc_sb.ap, AF.Exp), reads=[gc_sb], writes=[eg])
                        yield
                        S.op("pool", lambda e: e.tensor_tensor(Dst.ap, Dm.ap, bc3(C["noteye"].ap, [128, 4, 128], 1), ALU.mult), reads=[Dm, C["noteye"]], writes=[Dst])
                        yield
                        S.op("dve", lambda e: e.tensor_scalar(bneg.ap, b_d, -1.0, None, op0=ALU.mult), reads=[beta_all], writes=[bneg])
                        yield
                        S.op("dve", lambda e: e.tensor_tensor(be.ap, b_d, eg.ap, ALU.mult), reads=[beta_all, eg], writes=[be])
                        yield
                        glS = G_["glS"][sl]
                        egv = egc.ap.rearrange("p (c two) j -> p c two j", two=2)
                        for j_ in range(2):
                            S.op("dve", lambda e: e.tensor_copy(glS.ap[0:64, :, j_], egv[0:64, :, 0, lasts[j_]]), reads=[egc], writes=[glS])
                            yield
                            S.op("dve", lambda e: e.tensor_copy(glS.ap[64:128, :, j_], egv[64:128, :, 1, lasts[j_]]), reads=[egc], writes=[glS])
                            yield
                        Gp, Ap = gp_ps(), gp_ps()
                        for h, cc, pb in heads():
                            S.op("pe", lambda e: e.matmul(Gp.ap[:, h * 128:(h + 1) * 128], lhsT=khT.ap[pb:pb + 64, cc, :], rhs=khT.ap[pb:pb + 64, cc, :], start=True, stop=True), reads=[khT], writes=[Gp], rb=pb)
                            yield
                        for h, cc, pb in heads():
                            S.op("pe", lambda e: e.matmul(Ap.ap[:, h * 128:(h + 1) * 128], lhsT=qhT.ap[pb:pb + 64, cc, :], rhs=khT.ap[pb:pb + 64, cc, :], start=True, stop=True), reads=[qhT, khT], writes=[Ap], rb=pb)
                            yield
                        f4 = lambda t_: t_.ap.rearrange("p h j -> p (h j)")
                        S.op("dve", lambda e: e.tensor_tensor(f4(tq), Gp.ap, f4(Dst), ALU.mult), reads=[Gp, Dst], writes=[tq])
                        yield
                        Q0 = Qb[0]
                        S.op("pool", lambda e: e.tensor_tensor(Q0.ap, tq.ap, bc3(bneg.ap, [128, 4, 128], 2), ALU.mult), reads=[tq, bneg], writes=[Q0])
                        yield
                        S.op("dve", lambda e: e.tensor_tensor(f4(att), Ap.ap, f4(Dm), ALU.mult), reads=[Ap, Dm], writes=[att])
                        yield
                        tp1, tp2 = gp_ps(), gp_ps()
                        v1 = tp1.ap.rearrange("p (h j) -> p h j", h=4)
                        v2 = tp2.ap.bitcast(BF16)[:, 0:512].rearrange("p (h j) -> p h j", h=4)
                        for h in range(4):
                            S.op("pe", lambda e: e.transpose(v1[:, h, :], Q0.ap[:, h, :], C["ident"].ap), reads=[Q0, C["ident"]], writes=[tp1])
                            yield
                        for h in range(4):
                            S.op("pe", lambda e: e.transpose(v2[:, h, :], att.ap[:, h, :], C["ident_b"].ap), reads=[att, C["ident_b"]], writes=[tp2])
                            yield
                        P0, Y0 = Pb[0], Yb[0]
                        S.op("act", lambda e: e.copy(P0.ap, v1), reads=[tp1], writes=[P0])
                        yield
                        S.op("dve", lambda e: e.tensor_tensor(Y0.ap, v1, bc3(C["ident"].ap, [128, 4, 128], 1), ALU.add), reads=[tp1, C["ident"]], writes=[Y0])
                        yield
                        attT = G_["attT"][sl]
                        S.op("act", lambda e: e.copy(attT.ap, v2), reads=[tp2], writes=[attT])
                        yield
                        for stp in range(5):
                            Qc, Pc, Yc = Qb[stp % 2], Pb[stp % 2], Yb[stp % 2]
                            Qn_, Pn_, Yn_ = Qb[(stp + 1) % 2], Pb[(stp + 1) % 2], Yb[(stp + 1) % 2]
                            qp = gp_ps()
                            for h in range(4):
                                S.op("pe", lambda e: e.matmul(qp.ap[:, h * 128:(h + 1) * 128], lhsT=Pc.ap[:, h, :], rhs=Qc.ap[:, h, :], start=True, stop=True), reads=[Pc, Qc], writes=[qp])
                                yield
                            S.op("act", lambda e: e.copy(f4(Qn_), qp.ap), reads=[qp], writes=[Qn_])
                            yield
                            if stp < 4:
                                pp = gp_ps()
                                for h in range(4):
                                    S.op("pe", lambda e: e.transpose(pp.ap[:, h * 128:(h + 1) * 128], Qn_.ap[:, h, :], C["ident"].ap), reads=[Qn_, C["ident"]], writes=[pp])
                                    yield
                                S.op("act", lambda e: e.copy(f4(Pn_), pp.ap), reads=[pp], writes=[Pn_])
                                yield
                            yp = gp_ps()
                            for h in range(4):
                                S.op("pe", lambda e: e.matmul(yp.ap[:, h * 128:(h + 1) * 128], lhsT=Qn_.ap[:, h, :], rhs=Yc.ap[:, h, :], start=True, stop=True), reads=[Qn_, Yc], writes=[yp])
                                yield
                            S.op("dve", lambda e: e.tensor_tensor(f4(Yn_), f4(Yc), yp.ap, ALU.add), reads=[yp, Yc], writes=[Yn_])
                            yield
                        Yf = Yb[1]
                        S.op("pool", lambda e: e.tensor_tensor(vb.ap, kv_tm.ap[:, 1, :].rearrange("p (h v) -> p h v", h=4), bc3(b_d, [128, 4, 64], 2), ALU.mult), reads=[kv_tm, beta_all], writes=[vb])
                        yield
                        S.op("pool", lambda e: e.tensor_tensor(kbe.ap, kv_tm.ap[:, 0, :].rearrange("p (h v) -> p h v", h=4), bc3(be.ap, [128, 4, 64], 2), ALU.mult), reads=[kv_tm, be], writes=[kbe])
                        yield
                        up, wp = gp_ps(), gp_ps()
                        for h in range(4):
                            S.op("pe", lambda e: e.matmul(up.ap[:, h * 64:(h + 1) * 64], lhsT=Yf.ap[:, h, :], rhs=vb.ap[:, h, :], start=True, stop=True), reads=[Yf, vb], writes=[up])
                            yield
                        for h, cc, pb in heads():
                            S.op("pe", lambda e: e.matmul(wp.ap[pb:pb + 64, cc * 128:(cc + 1) * 128], lhsT=kbe.ap[:, h, :], rhs=Yf.ap[:, h, :], start=True, stop=True), reads=[Yf, kbe], writes=[wp])
                            yield
                        u, wT, qdT, ktail = G_["u"][sl], G_["wT"][sl], G_["qdT"][sl], G_["ktail"][sl]
                        S.op("act", lambda e: e.copy(u.ap.rearrange("p h v -> p (h v)"), up.ap[:, 0:256]), reads=[up], writes=[u])
                        yield
                        S.op("dve", lambda e: e.tensor_copy(wT.ap.rearrange("p c t -> p (c t)"), wp.ap[:, 0:256]), reads=[wp], writes=[wT])
                        yield
                        S.op("dve", lambda e: e.tensor_tensor(qdT.ap[0:64], qhT.ap[0:64], egv[0:64, :, 0, :], ALU.mult), reads=[qhT, egc], writes=[qdT])
                        yield
                        S.op("dve", lambda e: e.tensor_tensor(qdT.ap[64:128], qhT.ap[64:128], egv[64:128, :, 1, :], ALU.mult), reads=[qhT, egc], writes=[qdT])
                        yield
                        S.op("pool", lambda e: e.tensor_tensor(ktail.ap, kv_tm.ap[:, 0, :].rearrange("p (h v) -> p h v", h=4), bc3(tl.ap, [128, 4, 64], 2), ALU.mult), reads=[kv_tm, tl], writes=[ktail])
                        yield

                    def gdn_recur(b, d, sl, i):
                        u, wT, qdT, ktail, attT, glS = (G_[k][sl] for k in ("u", "wT", "qdT", "ktail", "attT", "glS"))
                        op_ = self.psb[4 + d]
                        for j in ((0, 1) if d == 0 else (1, 0)):
                            r0 = j * 64
                            wsp = kvp = self.psb[4 + d]
                            for h, cc, pb in heads():
                                S.op("pe", lambda e: e.matmul(wsp.ap[r0:r0 + 64, 256 + h * 64:256 + (h + 1) * 64], lhsT=wT.ap[pb:pb + 64, cc, r0:r0 + 64], rhs=Sgb.ap[pb:pb + 64, cc, :], start=True, stop=True), reads=[wT, Sgb], writes=[wsp], rb=pb)
                                yield
                            S.op("dve", lambda e: e.tensor_tensor(vnew.ap[r0:r0 + 64].rearrange("p h v -> p (h v)"), u.ap[r0:r0 + 64].rearrange("p h v -> p (h v)"), wsp.ap[r0:r0 + 64, 256:512], ALU.subtract), reads=[u, wsp], writes=[vnew])
                            yield
                            for h, cc, pb in heads():
                                o_ap = op_.ap[r0:r0 + 64, h * 64:(h + 1) * 64]
                                S.op("pe", lambda e: e.matmul(o_ap, lhsT=qdT.ap[pb:pb + 64, cc, r0:r0 + 64], rhs=Sgb.ap[pb:pb + 64, cc, :], start=True, stop=False), reads=[qdT, Sgb], writes=[op_], rb=pb)
                                yield
                                S.op("pe", lambda e: e.matmul(o_ap, lhsT=attT.ap[r0:r0 + 64, h, r0:r0 + 64], rhs=vnew.ap[r0:r0 + 64, h, :], start=False, stop=True), reads=[attT, vnew], writes=[op_], rb=r0)
                                yield
                            for h, cc, pb in heads():
                                S.op("pe", lambda e: e.matmul(kvp.ap[pb:pb + 64, 256 + cc * 64:256 + (cc + 1) * 64], lhsT=ktail.ap[r0:r0 + 64, h, :], rhs=vnew.ap[r0:r0 + 64, h, :], start=True, stop=True), reads=[ktail, vnew], writes=[kvp], rb=r0)
                                yield
                            S.op("dve", lambda e: e.tensor_tensor(Sg.ap, Sg.ap, bc3(glS.ap[:, :, j], [128, 2, 64], 2), ALU.mult), reads=[Sg, glS], writes=[Sg])
                            yield
                            S.op("dve", lambda e: e.tensor_tensor(Sg.ap.rearrange("p c v -> p (c v)"), Sg.ap.rearrange("p c v -> p (c v)"), kvp.ap[:, 256:384], ALU.add), reads=[Sg, kvp], writes=[Sg])
                            yield
                            S.op("act", lambda e: e.copy(Sgb.ap, Sg.ap), reads=[Sg], writes=[Sgb])
                            yield
                        og = ogst[i % 2]
                        S.op("act", lambda e: e.copy(og.ap, op_.ap[:, 0:256]), reads=[op_], writes=[og])
                        yield
                        self.store("pool", Sc[f"og{d}_" + nm], og, dstap=Sc[f"og{d}_" + nm].ap[b * 128:(b + 1) * 128, :])
                        yield

                    def hg_prep(b, d, sl):
                        t0 = b * 128
                        frames = ((C["hmfA"], 15), (C["hmfB"], 47)) if d == 0 else ((C["hmbA"], 48), (C["hmbB"], 16))
                        qdT, hattT, k2tm, hv, e_l = (H_[k][sl] for k in ("qtT", "hattT", "k2tm", "hv", "e_l"))
                        self.load("sp", hq, Sc["hqT_" + nm], srcap=Sc["hqT_" + nm].ap[:, t0:t0 + 128].rearrange("(c p) t -> p c t", p=128))
                        yield
                        self.load("sp", hf, Sc["hfT_" + nm], srcap=Sc["hfT_" + nm].ap[d * 256:(d + 1) * 256, t0:t0 + 128].rearrange("(c p) t -> p c t", p=128))
                        yield
                        self.load("sp", hv, Sc["hv_" + nm], srcap=Sc["hv_" + nm].ap[t0:t0 + 128, :])
                        yield
                        S.op("act", lambda e: e.activation(sg.ap, hf.ap, AF.Sigmoid), reads=[hf], writes=[sg])
                        yield
                        for cc in range(2):
                            S.op("dve", lambda e: e.tensor_scalar(gate.ap[:, cc, :], sg.ap[:, cc, :], self.oml.ap[:, cc:cc + 1], self.lbv.ap[:, cc:cc + 1], op0=ALU.mult, op1=ALU.add),
                                 reads=[sg, self.oml, self.lbv], writes=[gate])
                            yield
                        S.op("pool", lambda e: e.tensor_scalar(kk.ap, gate.ap, -1.0, 1.0, op0=ALU.mult, op1=ALU.add), reads=[gate], writes=[kk])
                        yield
                        S.op("dve", lambda e: e.tensor_scalar(gate.ap, gate.ap, 1e-30, None, op0=ALU.max), reads=[gate], writes=[gate])
                        yield
                        S.op("act", lambda e: e.activation(lf.ap, gate.ap, AF.Ln), reads=[gate], writes=[lf])
                        yield
                        fl = lambda t_: t_.ap.rearrange("p c t -> p (c t)")
                        v4 = lambda t_: t_.ap.rearrange("p c (j t) -> p (c j) t", j=2)
                        S.op("dve", lambda e: e.tensor_tensor_scan(fl(bcp), C["scanmask"].ap, fl(lf), 0.0, ALU.mult, ALU.add), reads=[C["scanmask"], lf], writes=[bcp])
                        yield
                        bl = v4(bcp)[:, :, 63:64]
                        if d == 0:
                            bx = bcp
                        else:
                            S.op("dve", lambda e: e.tensor_tensor(bcr.ap, lf.ap, bcp.ap, ALU.subtract), reads=[lf, bcp], writes=[bcr])
                            yield
                            S.op("dve", lambda e: e.tensor_tensor(v4(bcr), v4(bcr), bl.to_broadcast([128, 4, 64]), ALU.add), reads=[bcr, bcp], writes=[bcr])
                            yield
                            bx = bcr
                        S.op("act", lambda e: e.activation(e_l.ap, bl.rearrange("p a b -> p (a b)"), AF.Exp), reads=[bcp], writes=[e_l])
                        yield
                        S.op("act", lambda e: e.activation(E1.ap, bx.ap, AF.Exp), reads=[bx], writes=[E1])
                        yield
                        S.op("dve", lambda e: e.tensor_tensor(qdT.ap, hq.ap, E1.ap, ALU.mult), reads=[hq, E1], writes=[qdT])
                        yield
                        S.op("dve", lambda e: e.tensor_tensor(v4(arg), bl.to_broadcast([128, 4, 64]), v4(bx), ALU.subtract), reads=[bx, bcp], writes=[arg])
                        yield
                        S.op("act", lambda e: e.activation(E2.ap, arg.ap, AF.Exp), reads=[arg], writes=[E2])
                        yield
                        S.op("pool", lambda e: e.tensor_tensor(kt2.ap, kk.ap, E2.ap, ALU.mult), reads=[kk, E2], writes=[kt2])
                        yield
                        tp = self.psb[6 + d]
                        tpv = tp.ap.bitcast(BF16)[:, 768:1024].rearrange("p (c k) -> p c k", c=2)
                        for cc in range(2):
                            S.op("pe", lambda e: e.transpose(tpv[:, cc, :], kt2.ap[:, cc, :], C["ident_b"].ap), reads=[kt2, C["ident_b"]], writes=[tp])
                            yield
                        S.op("act", lambda e: e.copy(k2tm.ap, tp.ap.bitcast(BF16)[:, 768:1024]), reads=[tp], writes=[k2tm])
                        yield
                        for fi, (hm, r) in enumerate(frames):
                            br = v4(bx)[:, :, r:r + 1]
                            S.op("dve", lambda e: e.tensor_tensor(v4(arg), v4(bx), br.to_broadcast([128, 4, 64]), ALU.subtract), reads=[bx], writes=[arg])
                            yield
                            S.op("dve", lambda e: e.tensor_scalar(arg.ap, arg.ap, 40.0, -40.0, op0=ALU.min, op1=ALU.max), reads=[arg], writes=[arg])
                            yield
                            S.op("act", lambda e: e.activation(E1.ap, arg.ap, AF.Exp), reads=[arg], writes=[E1])
                            yield
                            S.op("act", lambda e: e.activation(E2.ap, arg.ap, AF.Exp, scale=-1.0), reads=[arg], writes=[E2])
                            yield
                            S.op("dve", lambda e: e.tensor_tensor(qtf.ap, hq.ap, E1.ap, ALU.mult), reads=[hq, E1], writes=[qtf])
                            yield
                            S.op("pool", lambda e: e.tensor_tensor(ktf.ap, kk.ap, E2.ap, ALU.mult), reads=[kk, E2], writes=[ktf])
                            yield
                            ap_ = self.psb[6 + d]
                            for h, cc, pb in heads():
                                S.op("pe", lambda e: e.matmul(ap_.ap[:, h * 128:(h + 1) * 128], lhsT=ktf.ap[pb:pb + 64, cc, :], rhs=qtf.ap[pb:pb + 64, cc, :], start=True, stop=True), reads=[ktf, qtf], writes=[ap_], rb=pb)
                                yield
                            S.op("dve", lambda e: e.tensor_tensor(fr[fi].ap, ap_.ap.rearrange("p (h t) -> p h t", h=4), bc3(hm.ap, [128, 4, 128], 1), ALU.mult), reads=[ap_, hm], writes=[fr[fi]])
                            yield
                        S.op("pool", lambda e: e.tensor_tensor(hattT.ap, fr[0].ap, fr[1].ap, ALU.add), reads=[fr[0], fr[1]], writes=[hattT])
                        yield

                    def hg_recur(b, d, sl, i):
                        qdT, hattT, k2tm, hv, e_l = (H_[k][sl] for k in ("qtT", "hattT", "k2tm", "hv", "e_l"))
                        op_ = self.psb[6 + d]
                        for j in ((0, 1) if d == 0 else (1, 0)):
                            r0 = j * 64
                            elj = e_l.ap.rearrange("p (c j) -> p c j", j=2)[:, :, j]
                            S.op("act", lambda e: e.copy(Shp.ap, Sh.ap), reads=[Sh], writes=[Shp])
                            yield
                            kvp = self.psb[6 + d]
                            for h, cc, pb in heads():
                                o_ap = op_.ap[r0:r0 + 64, h * 64:(h + 1) * 64]
                                S.op("pe", lambda e: e.matmul(o_ap, lhsT=qdT.ap[pb:pb + 64, cc, r0:r0 + 64], rhs=Shp.ap[pb:pb + 64, cc, :], start=True, stop=False), reads=[qdT, Shp], writes=[op_], rb=pb)
                                yield
                                S.op("pe", lambda e: e.matmul(o_ap, lhsT=hattT.ap[r0:r0 + 64, h, r0:r0 + 64], rhs=hv.ap[r0:r0 + 64, h * 64:(h + 1) * 64], start=False, stop=True), reads=[hattT, hv], writes=[op_], rb=r0)
                                yield
                            for h, cc, pb in heads():
                                S.op("pe", lambda e: e.matmul(kvp.ap[pb:pb + 64, 256 + cc * 64:256 + (cc + 1) * 64], lhsT=k2tm.ap[r0:r0 + 64, h * 64:(h + 1) * 64], rhs=hv.ap[r0:r0 + 64, h * 64:(h + 1) * 64], start=True, stop=True),
                                     reads=[k2tm, hv], writes=[kvp], rb=r0)
                                yield
                            S.op("dve", lambda e: e.tensor_tensor(Sh.ap, Sh.ap, bc3(elj, [128, 2, 64], 2), ALU.mult), reads=[Sh, e_l], writes=[Sh])
                            yield
                            S.op("dve", lambda e: e.tensor_tensor(Sh.ap.rearrange("p c v -> p (c v)"), Sh.ap.rearrange("p c v -> p (c v)"), kvp.ap[:, 256:384], ALU.add), reads=[Sh, kvp], writes=[Sh])
                            yield
                        oh = ohst[i % 2]
                        S.op("act", lambda e: e.copy(oh.ap, op_.ap[:, 0:256]), reads=[op_], writes=[oh])
                        yield
                        self.store("pool", Sc[f"oh{d}_" + nm], oh, dstap=Sc[f"oh{d}_" + nm].ap[b * 128:(b + 1) * 128, :])
                        yield

                    return dict(gdn_prep=gdn_prep, gdn_recur=gdn_recur, hg_prep=hg_prep, hg_recur=hg_recur, Sg=Sg, Sgb=Sgb, Sh=Sh, run_chains=run_chains)

                DD = [make_dir(0), make_dir(1)]
                orders = [list(range(NB)), list(range(NB - 1, -1, -1))]
                run_chains = DD[0]["run_chains"]
                for d in range(2):
                    Sg, Sgb, Sh = DD[d]["Sg"], DD[d]["Sgb"], DD[d]["Sh"]
                    if s.sample:
                        self.load("sp", Sg, I["st_gdn"], srcap=I["st_gdn"].ap[l, d])
                        self.load("sp", Sh, I["st_hg"], srcap=I["st_hg"].ap[l, d])
                    else:
                        S.op("pool", lambda e: e.memset(Sg.ap, 0.0), writes=[Sg])
                        S.op("pool", lambda e: e.memset(Sh.ap, 0.0), writes=[Sh])
                    S.op("act", lambda e: e.copy(Sgb.ap, Sg.ap), reads=[Sg], writes=[Sgb])

                def gdn_chain(d, i):
                    if i + 1 < NB:
                        yield from DD[d]["gdn_prep"](orders[d][i + 1], d, (i + 1) % 2, 0)

                def gdn_rchain(d, i):
                    yield from DD[d]["gdn_recur"](orders[d][i], d, i % 2, i)

                def hg_chain(d, i):
                    if i + 1 < NB:
                        yield from DD[d]["hg_prep"](orders[d][i + 1], d, (i + 1) % 2)
                    yield from DD[d]["hg_recur"](orders[d][i], d, i % 2, i)

                run_chains([DD[0]["gdn_prep"](orders[0][0], 0, 0, 0), DD[1]["gdn_prep"](orders[1][0], 1, 0, 0),
                            DD[0]["hg_prep"](orders[0][0], 0, 0), DD[1]["hg_prep"](orders[1][0], 1, 0)])
                def hg_all(d):
                    for i in range(NB):
                        yield from hg_chain(d, i)
                hgs = [hg_all(0), hg_all(1)]
                for i in range(NB):
                    gs = [gdn_chain(0, i), gdn_chain(1, i), gdn_rchain(0, i), gdn_rchain(1, i)]
                    tick = 0
                    while gs:
                        for c in list(gs):
                            try:
                                next(c)
                            except StopIteration:
                                gs.remove(c)
                        tick += 1
                        if tick % 2 == 0:
                            for c in list(hgs):
                                try:
                                    next(c)
                                except StopIteration:
                                    hgs.remove(c)
                run_chains(hgs)
                if not s.sample:
                    for d in range(2):
                        Sg, Sh = DD[d]["Sg"], DD[d]["Sh"]
                        for hp in range(2):
                            self.store("sp", O["sg"], Sg, dstap=O["sg"].ap[bidx, l, d, :, :, :].rearrange("(c two) k v -> two k c v", two=2)[hp], srcap=Sg.ap[hp * 64:(hp + 1) * 64])
                            self.store("sp", O["sh"], Sh, dstap=O["sh"].ap[bidx, l, d, :, :, :].rearrange("(c two) k v -> two k c v", two=2)[hp], srcap=Sh.ap[hp * 64:(hp + 1) * 64])
                self.phase_end()
            with ExitStack() as st:
                sb = lambda shape, dt=F32, name=None: self.sb(shape, dt, name, st)
                nwt = sb([128, 2, 64], F32, "nwt")
                self.load("sp", nwt, I["normw"], srcap=I["normw"].ap[l])
                of_, ob_, gsrc = self.sbn(2, [128, 256], F32, "of", st), self.sbn(2, [128, 256], F32, "ob", st), self.sbn(2, [128, 256], F32, "gsrc", st)
                sq2 = sb([128, 256], F32, "sq2")
                ss4, rs4 = sb([128, 4], F32, "ss4"), sb([128, 4], F32, "rs4")
                mxb = self.sbn(2, [128, 256], BF16, "mxb", st)
                mst = self.sbn(2, [128, 2, 128], BF16, "mst", st)
                it = 0
                for b in range(NB):
                    for mi, (pre, gsc, gcol, fn) in enumerate((("og", "zab_", slice(0, 256), AF.Silu), ("oh", "hg_", slice(0, 256), AF.Sigmoid))):
                        it += 1
                        a_, b_, g_ = of_[it % 2], ob_[it % 2], gsrc[it % 2]
                        rows = slice(b * 128, (b + 1) * 128)
                        self.load("sp", a_, Sc[f"{pre}0_" + nm], srcap=Sc[f"{pre}0_" + nm].ap[rows, :])
                        self.load("sp", b_, Sc[f"{pre}1_" + nm], srcap=Sc[f"{pre}1_" + nm].ap[rows, :])
                        self.load("sp", g_, Sc[gsc + nm], srcap=Sc[gsc + nm].ap[rows, gcol])
                        S.op("pool", lambda e: e.tensor_tensor(a_.ap, a_.ap, b_.ap, ALU.add), reads=[a_, b_], writes=[a_])
                        S.op("pool", lambda e: e.tensor_tensor(sq2.ap, a_.ap, a_.ap, ALU.mult), reads=[a_], writes=[sq2])
                        S.op("dve", lambda e: e.tensor_reduce(ss4.ap, sq2.ap.rearrange("p (h v) -> p h v", h=4), AX.X, ALU.add), reads=[sq2], writes=[ss4])
                        self.rstd(rs4, ss4, 1.0 / 64, 4)
                        a3 = a_.ap.rearrange("p (h v) -> p h v", h=4)
                        S.op("dve", lambda e: e.tensor_tensor(a3, a3, bc3(rs4.ap, [128, 4, 64], 2), ALU.mult), reads=[a_, rs4], writes=[a_])
                        S.op("pool", lambda e: e.tensor_tensor(a3, a3, bc3(nwt.ap[:, mi, :], [128, 4, 64], 1), ALU.mult), reads=[a_, nwt], writes=[a_])
                        S.op("act", lambda e: e.activation(g_.ap, g_.ap, fn), reads=[g_], writes=[g_])
                        mx_ = mxb[it % 2]
                        S.op("dve", lambda e: e.tensor_tensor(mx_.ap, a_.ap, g_.ap, ALU.mult), reads=[a_, g_], writes=[mx_])
                        tp = self.ps()
                        tpv = tp.ap.bitcast(BF16)[:, 0:256].rearrange("p (c t) -> p c t", c=2)
                        for c in range(2):
                            S.op("pe", lambda e: e.transpose(tpv[:, c, :], mx_.ap[:, c * 128:(c + 1) * 128], C["ident_b"].ap), reads=[mx_, C["ident_b"]], writes=[tp])
                        ms_ = mst[it % 2]
                        S.op("act", lambda e: e.copy(ms_.ap, tpv), reads=[tp], writes=[ms_])
                        self.store("sp", Sc["mixT_" + nm], ms_, dstap=Sc["mixT_" + nm].ap[mi * 256:(mi + 1) * 256, rows].rearrange("(c p) t -> p c t", p=128))
                self.phase_end()

    def phaseC(self, l):
        S, C, I, Sc = self.S, self.C, self.I, self.Sc
        for s in self.seqs:
            nm, T, Tk = s.name, s.T, s.Tk
            nkb = Tk // 128
            TT = min(512, T)
            with ExitStack() as st:
                KT = self.sb([128, 4, Tk], BF16, "KT", st)
                krT = self.sb([64, Tk], BF16, "krT", st)
                V = self.sb([128, nkb, 512], BF16, "V", st)
                self.load("sp", KT, Sc["KnT_" + nm], srcap=Sc["KnT_" + nm].ap.rearrange("h p t -> p h t"))
                self.load("sp", krT, Sc["krT_" + nm])
                self.load("sp", V, Sc["V_" + nm], srcap=Sc["V_" + nm].ap.rearrange("(b p) n -> p b n", p=128))
                Qn = self.sbn(2, [128, 4, TT], BF16, "Qn", st)
                Qr = self.sbn(2, [64, 4, TT], BF16, "Qr", st)
                PT = self.sbn(3, [128, TT], BF16, "PT", st)
                rden = self.sbn(2, [128, TT], F32, "rden", st)
                ost = self.sbn(2, [128, TT], BF16, "ost", st)
                negc = self.sb([128, 4], F32, "negc", st)
                qk2 = self.qk2[nm]
                S.op("dve", lambda e: e.tensor_tensor(negc.ap, qk2.ap[:, 0, :], qk2.ap[:, 1, :], ALU.mult), reads=[qk2], writes=[negc])
                S.op("pool", lambda e: e.tensor_tensor(negc.ap, negc.ap, C["p05"].ap[:, 0:4], ALU.pow), reads=[negc, C["p05"]], writes=[negc])
                S.op("dve", lambda e: e.tensor_scalar(negc.ap, negc.ap, -1.01 * SCALE, None, op0=ALU.mult), reads=[negc], writes=[negc])
                nq = T // TT
                for qi in range(nq):
                    q0 = qi * TT
                    qn, qr = Qn[qi % 2], Qr[qi % 2]
                    self.load("sp", qn, Sc["QnT_" + nm], srcap=Sc["QnT_" + nm].ap[:, :, q0:q0 + TT].rearrange("h p t -> p h t"))
                    self.load("sp", qr, Sc["QrT_" + nm], srcap=Sc["QrT_" + nm].ap[:, :, q0:q0 + TT].rearrange("h p t -> p h t"))
                    for h in range(4):
                        ops, dps = self.psb[h % 2], self.psb[2 + h % 2]
                        sps = [None] * nkb

                        def scores(kb):
                            p = self.psb[4 + kb % 4]
                            sps[kb] = p
                            S.op("pe", lambda e: e.matmul(p.ap[:, 0:TT], lhsT=KT.ap[:, h, kb * 128:(kb + 1) * 128], rhs=qn.ap[:, h, :], start=True, stop=False), reads=[KT, qn], writes=[p])
                            S.op("pe", lambda e: e.matmul(p.ap[:, 0:TT], lhsT=krT.ap[:, kb * 128:(kb + 1) * 128], rhs=qr.ap[:, h, :], start=False, stop=True), reads=[krT, qr], writes=[p])
                        scores(0)
                        for kb in range(nkb):
                            if kb + 1 < nkb:
                                scores(kb + 1)
                            pt = PT[kb % 3]
                            p = sps[kb]
                            S.op("act", lambda e: e.activation(pt.ap, p.ap[:, 0:TT], AF.Exp, bias=negc.ap[:, h:h + 1], scale=SCALE), reads=[p, negc], writes=[pt])
                            S.op("pe", lambda e: e.matmul(ops.ap[:, 0:TT], lhsT=V.ap[:, kb, h * 128:(h + 1) * 128], rhs=pt.ap, start=(kb == 0), stop=(kb == nkb - 1)), reads=[V, pt], writes=[ops])
                            S.op("pe", lambda e: e.matmul(dps.ap[:, 0:TT], lhsT=C["ones_b"].ap, rhs=pt.ap, start=(kb == 0), stop=(kb == nkb - 1)), reads=[C["ones_b"], pt], writes=[dps])
                        rd, os_ = rden[h % 2], ost[h % 2]
                        S.op("dve", lambda e: e.reciprocal(rd.ap, dps.ap[:, 0:TT]), reads=[dps], writes=[rd])
                        S.op("dve", lambda e: e.tensor_tensor(os_.ap, ops.ap[:, 0:TT], rd.ap, ALU.mult), reads=[ops, rd], writes=[os_])
                        self.store("pool", Sc["mixT_" + nm], os_, dstap=Sc["mixT_" + nm].ap[512 + h * 128:512 + (h + 1) * 128, q0:q0 + TT])
                self.phase_end()

    def load_G(self, st, which, ci):
        S, C, Sc = self.S, self.C, self.Sc
        if not getattr(self, "_gvec_l", None) == self.l:
            self._gvec_l = self.l
            gp = self.psb[0]
            for k in range(4):
                S.op("pe", lambda e: e.transpose(gp.ap[0:8, k * 128:(k + 1) * 128], self.gg.ap[:, k // 2, k % 2, :], C["ident"].ap), reads=[self.gg, C["ident"]], writes=[gp])
            gsb = self.sb([8, 512], F32, "gsb", st)
            S.op("dve", lambda e: e.tensor_copy(gsb.ap, gp.ap[0:8, :]), reads=[gp], writes=[gsb])
            self.store("sp", Sc["gvec"], gsb, dstap=Sc["gvec"].ap.rearrange("k (c p) -> c k p", p=128), srcap=gsb.ap.rearrange("c (k p) -> c k p", p=128))
        G = self.sb([128, D], F32, "G", st)
        k = which * 2 + ci
        self.load("sp", G, Sc["gvec"], srcap=Sc["gvec"].ap[k:k + 1, :].partition_broadcast(128))
        return G

    def post_norm_residual(self, yp, ybanks, G, xin, tt, out_ap_tl, ss, rs, junk):
        S = self.S
        S.op("act", lambda e: e.activation(junk.ap, yp, AF.Square, accum_out=ss.ap[:, 0:1]), reads=ybanks, writes=[junk, ss])
        self.rstd(rs, ss, 1.0 / D, 1)
        S.op("dve", lambda e: e.scalar_tensor_tensor(tt.ap, yp, rs.ap[:, 0:1], G.ap, op0=ALU.mult, op1=ALU.mult), reads=ybanks + [rs, G], writes=[tt])
        S.op("dve", lambda e: e.tensor_tensor(out_ap_tl.ap, xin.ap, tt.ap, ALU.add), reads=[xin, tt], writes=[out_ap_tl])

    def phaseD(self, l):
        S, C, I, Sc, O = self.S, self.C, self.I, self.Sc, self.O
        pall = self.pall
        with ExitStack() as st:
            Wout = self.sb([128, 8, D], BF16, "Wout", st)
            self.load("pool", Wout, I["w_out"], srcap=I["w_out"].ap[l])
            Wf1 = self.sb([128, 8, 2 * DFF], BF16, "Wf1", st)
            for k in range(0, 8, 2):
                S.dma("pool", Wf1.ap[:, k:k + 2, :], I["w_f1"].ap[l, :, k:k + 2, :], reads=[I["w_f1"]], writes=[Wf1], key=f"wf1{k}")
            mixT = self.sbn(2, [128, 8, 512], BF16, "mixT", st)
            xt_s = self.sbn(4, [128, D], F32, "xt", st)
            tt = self.sbn(2, [128, D], F32, "tt", st)
            nt_tiles = (self.sbn(2, [128, D], BF16, "xn", st), self.sbn(1, [128, D], BF16, "junk", st) * 2,
                        self.sb([128, 4], F32, "ss", st), self.sb([128, 4], F32, "rs", st))
            junk = nt_tiles[1][0]
            ss1, rs1 = self.sb([128, 1], F32, "ss1", st), self.sb([128, 1], F32, "rs1", st)
            h2T = self.sb([128, 8, 512], BF16, "h2T", st)
            sa = self.sbn(2, [128, 512], F32, "sa", st)
            actst = self.sbn(4, [128, 512], BF16, "actst", st)
            Gc = {}
            for s in self.seqs:
                nm, T, ci = s.name, s.T, s.ci
                if ci not in Gc:
                    Gc[ci] = self.load_G(st, 0, ci)
                G1 = Gc[ci]
                X = I["x_" + nm] if l == 0 else Sc["x1_" + nm]
                TT = min(512, T)
                nblk = TT // 128
                for ti in range(T // TT):
                    tok0 = ti * TT
                    mt = mixT[ti % 2]
                    self.load("sp", mt, Sc["mixT_" + nm], dstap=mt.ap[:, :, 0:TT], srcap=Sc["mixT_" + nm].ap[:, tok0:tok0 + TT].rearrange("(c p) t -> p c t", p=128))
                    for blk in range(nblk):
                        self.load("sp", xt_s[blk], X, srcap=X.ap[tok0 + blk * 128:tok0 + (blk + 1) * 128, :])
                    for blk in range(nblk):
                        b0 = 2 * (blk % 2)
                        ybanks = [self.psb[b0], self.psb[b0 + 1]]
                        yp = pall[:, b0:b0 + 2, :].rearrange("p b n -> p (b n)")
                        for hf in range(2):
                            for k in range(8):
                                S.op("pe", lambda e: e.matmul(self.psb[b0 + hf].ap, lhsT=mt.ap[:, k, blk * 128:(blk + 1) * 128], rhs=Wout.ap[:, k, hf * 512:(hf + 1) * 512], start=(k == 0), stop=(k == 7)),
                                     reads=[mt, Wout], writes=[self.psb[b0 + hf]])
                        self.post_norm_residual(yp, ybanks, G1, xt_s[blk], tt[blk % 2], xt_s[blk], ss1, rs1, junk)
                        self.store("pool", Sc["xmid_" + nm], xt_s[blk], dstap=Sc["xmid_" + nm].ap[tok0 + blk * 128:tok0 + (blk + 1) * 128, :])
                    self.norm_transpose(nt_tiles, lambda blk: xt_s[blk], nblk, self.scale2, 24, ci, h2T, banks=[self.psb[4], self.psb[5], self.psb[6], self.psb[7]])
                    for j in range(22):
                        pA, pB = self.psb[(2 * j) % 4], self.psb[(2 * j + 1) % 4]
                        for (p, c0) in ((pA, j * 128), (pB, DFF + j * 128)):
                            for k in range(8):
                                S.op("pe", lambda e: e.matmul(p.ap[:, 0:TT], lhsT=Wf1.ap[:, k, c0:c0 + 128], rhs=h2T.ap[:, k, 0:TT], start=(k == 0), stop=(k == 7)), reads=[Wf1, h2T], writes=[p])
                        sj, aj = sa[j % 2], actst[j % 4]
                        S.op("act", lambda e: e.activation(sj.ap[:, 0:TT], pA.ap[:, 0:TT], AF.Silu), reads=[pA], writes=[sj])
                        S.op("dve", lambda e: e.tensor_tensor(aj.ap[:, 0:TT], pB.ap[:, 0:TT], sj.ap[:, 0:TT], ALU.mult), reads=[pB, sj], writes=[aj])
                        self.store("pool", Sc["actT_" + nm], aj, dstap=Sc["actT_" + nm].ap[j * 128:(j + 1) * 128, tok0:tok0 + TT], srcap=aj.ap[:, 0:TT])
            self.phase_end()
        with ExitStack() as st:
            Wf2 = self.sb([128, 22, D], BF16, "Wf2", st)
            for k in range(0, 22, 11):
                S.dma("pool", Wf2.ap[:, k:k + 11, :], I["w_f2"].ap[l, :, k:k + 11, :], reads=[I["w_f2"]], writes=[Wf2], key=f"wf2{k}")
            actT = self.sbn(2, [128, 22, 512], BF16, "actT", st)
            xt_s = self.sbn(4, [128, D], F32, "xt", st)
            tt = self.sbn(2, [128, D], F32, "tt", st)
            junk = self.sb([128, D], BF16, "junk", st)
            ss1, rs1 = self.sb([128, 1], F32, "ss1", st), self.sb([128, 1], F32, "rs1", st)
            Gc = {}
            for s in self.seqs:
                nm, T, ci = s.name, s.T, s.ci
                if ci not in Gc:
                    Gc[ci] = self.load_G(st, 1, ci)
                G2 = Gc[ci]
                Xo = Sc["x1_" + nm] if l == 0 else O["y_" + nm]
                TT = min(512, T)
                nblk = TT // 128
                for ti in range(T // TT):
                    tok0 = ti * TT
                    at = actT[ti % 2]
                    self.load("sp", at, Sc["actT_" + nm], dstap=at.ap[:, :, 0:TT], srcap=Sc["actT_" + nm].ap[:, tok0:tok0 + TT].rearrange("(c p) t -> p c t", p=128))
                    for blk in range(nblk):
                        self.load("sp", xt_s[blk], Sc["xmid_" + nm], srcap=Sc["xmid_" + nm].ap[tok0 + blk * 128:tok0 + (blk + 1) * 128, :])
                    for blk in range(nblk):
                        b0 = 2 * (blk % 4)
                        ybanks = [self.psb[b0], self.psb[b0 + 1]]
                        yp = pall[:, b0:b0 + 2, :].rearrange("p b n -> p (b n)")
                        for hf in range(2):
                            for j in range(22):
                                S.op("pe", lambda e: e.matmul(self.psb[b0 + hf].ap, lhsT=at.ap[:, j, blk * 128:(blk + 1) * 128], rhs=Wf2.ap[:, j, hf * 512:(hf + 1) * 512], start=(j == 0), stop=(j == 21)),
                                     reads=[at, Wf2], writes=[self.psb[b0 + hf]])
                        self.post_norm_residual(yp, ybanks, G2, xt_s[blk], tt[blk % 2], xt_s[blk], ss1, rs1, junk)
                        self.store("pool", Xo, xt_s[blk], dstap=Xo.ap[tok0 + blk * 128:tok0 + (blk + 1) * 128, :])
            self.phase_end()


def _rk(w, nk):
    Lw, K, N = w.shape
    return np.ascontiguousarray(w.reshape(Lw, nk, 128, N).transpose(0, 2, 1, 3))


def _fm(v, nch):
    return np.ascontiguousarray(v.reshape(v.shape[0], nch, 128).transpose(0, 2, 1))


def _consts_np():
    c = np.zeros((128, 13, 128), np.float32)
    p = np.arange(128)[:, None]
    i = np.arange(128)[None, :]
    c[:, 0] = (p == i)
    c[:, 1] = 1.0
    c[:, 2] = (p // 64 == i // 64)
    sm_ = (p // 64 == i // 64)
    c[:, 3] = sm_ & (p <= i)
    c[:, 4] = sm_ & (p >= i)
    c[:, 5] = np.where(sm_ & (p >= i), 0.0, NEG)
    c[:, 6] = np.where(sm_ & (p <= i), 0.0, NEG)
    c[:, 7] = (p != i)
    same = (p // 64 == i // 64)
    c[:, 8] = same & (p <= i) & (p % 64 < 32)
    c[:, 9] = same & (p >= i) & (p % 64 >= 32)
    c[:, 11] = same & (p <= i) & (p % 64 >= 32)
    c[:, 12] = same & (p >= i) & (p % 64 < 32)
    for a in range(2):
        for f in range(16):
            c[a * 32 + 16 + f, 10, a * 32 + f] = -1.0
            c[a * 32 + f, 10, a * 32 + 16 + f] = 1.0
    return c


def _rope_np(T):
    t = np.arange(T)
    row = (t // 64).astype(np.float32)
    col = (t % 64).astype(np.float32)
    inv = (np.float32(10000.0) ** (-np.arange(16, dtype=np.float32) / np.float32(16))).astype(np.float32)
    out = np.zeros((2, 64, T), np.float32)
    for a, pos in enumerate((row, col)):
        ang = (pos[None, :] * inv[:, None]).astype(np.float32)
        for hlf in range(2):
            out[0, a * 32 + hlf * 16:a * 32 + hlf * 16 + 16] = np.cos(ang)
            out[1, a * 32 + hlf * 16:a * 32 + hlf * 16 + 16] = np.sin(ang)
    return out


def _shared_inputs(inp, Ts):
    f = lambda k: np.asarray(inp[k], np.float32)
    w_in = f("w_in")
    order = np.concatenate([np.arange(0, 768), np.arange(1040, 1296), np.arange(1552, 2064), np.arange(2320, 2704),
                            np.arange(2704, 2960), np.arange(2960, 3024), np.arange(768, 1040), np.arange(1296, 1552),
                            np.arange(2064, 2320)])
    w_ukv = f("mla_w_ukv")
    kvorder = np.concatenate([np.arange(h * 256, h * 256 + 128) for h in range(4)] + [np.arange(h * 256 + 128, h * 256 + 256) for h in range(4)])
    conv = f("gdn_conv_w")
    convd = np.zeros((L, 128, 6, 5, 128), np.float32)
    idx = np.arange(128)
    for cc in range(6):
        for j in range(5):
            convd[:, idx, cc, j, idx] = conv[:, cc * 128 + idx, j]
    sm = np.ones((128, 256), np.float32)
    sm[:, ::64] = 0.0
    sh = {
        "w_ada": _rk(f("w_ada"), 8),
        "b_ada": _fm(f("b_ada"), 48),
        "gains": np.ascontiguousarray(np.stack([_fm(f(k), 8) for k in ("g_pre_mix", "g_post_mix", "g_pre_ffn", "g_post_ffn")], axis=2)),
        "w_in": _rk(np.ascontiguousarray(w_in[:, :, order]), 8),
        "w_out": _rk(f("w_out"), 8),
        "w_f1": _rk(f("w_ffn_in"), 8),
        "w_f2": _rk(f("w_ffn_out"), 22),
        "w_uq": _rk(f("mla_w_uq"), 3),
        "w_ukv": _rk(np.ascontiguousarray(w_ukv[:, :, kvorder]), 2),
        "convd": convd,
        "gdn_ad": np.ascontiguousarray(np.broadcast_to(np.stack([f("gdn_a_log").reshape(L, 8), f("gdn_dt_bias").reshape(L, 8)], axis=1)[:, None], (L, 128, 2, 8))),
        "normw": np.ascontiguousarray(np.broadcast_to(np.stack([f("gdn_norm_w"), f("hgrn_norm_w")], axis=1)[:, None], (L, 128, 2, 64))),
        "lbraw": np.ascontiguousarray(f("hgrn_lb").reshape(L, 2, 128).transpose(2, 0, 1)),
        "mlanw": np.ascontiguousarray(np.concatenate([_fm(f("mla_q_norm_w"), 3), _fm(f("mla_kv_norm_w"), 2)], axis=2)),
        "kvnw_rep": np.ascontiguousarray(np.broadcast_to(f("mla_kv_norm_w")[:, None, :], (L, 128, 256))),
        "consts": _consts_np(),
        "scanmask": sm,
        "ropeT": _rope_np(Ts),
    }
    return sh


def _core_inputs(inp, sh, b, Ts, Tp):
    f = lambda k: np.asarray(inp[k], np.float32)
    m = dict(sh)
    m["x_s"] = np.ascontiguousarray(f("x_sample")[b, :Ts])
    m["x_p0"] = np.ascontiguousarray(f("x_prompt")[2 * b])
    m["x_p1"] = np.ascontiguousarray(f("x_prompt")[2 * b + 1])
    cond = np.stack([f("c")[b], f("c_ctx")], axis=-1)
    m["cond"] = np.ascontiguousarray(cond.reshape(8, 128, 2).transpose(1, 0, 2))
    ck = f("cache_mla_ckv")[b]
    m["ctx_ckvT"] = np.ascontiguousarray(ck.reshape(L, 256, 2, 128).transpose(0, 3, 2, 1))
    m["ctx_krT"] = np.ascontiguousarray(f("cache_mla_krope")[b].transpose(0, 2, 1))
    for k, src in (("st_gdn", "state_gdn"), ("st_hg", "state_hgrn")):
        s_ = f(src)[b]
        m[k] = np.ascontiguousarray(s_.reshape(L, 2, 2, 2, 64, 64).transpose(0, 1, 3, 4, 2, 5).reshape(L, 2, 128, 2, 64))
    return m


_NC_CACHE = {}


def kernel(**inputs):
    Ts = int(np.asarray(inputs["x_sample"]).shape[1])
    Tp = int(np.asarray(inputs["x_prompt"]).shape[1])
    nb = int(np.asarray(inputs["x_sample"]).shape[0])
    kb = KB(Ts, Tp)
    sh = _shared_inputs(inputs, Ts)
    in_maps = [_core_inputs(inputs, sh, b, Ts, Tp) for b in range(nb)]
    res = run_bass_kernel_spmd(kb.nc, in_maps, core_ids=list(range(nb)))
    R = res.results
    y_p = np.stack([R[b][k] for b in range(nb) for k in ("y_p0", "y_p1")], axis=0)
    y_s = np.stack([R[b]["y_s"] for b in range(nb)], axis=0)
    cat = lambda k: np.concatenate([R[b][k] for b in range(nb)], axis=0)
    return (y_p.astype(np.float32), y_s.astype(np.float32), cat("ckv").astype(np.float32), cat("kr").astype(np.float32),
            cat("sg").astype(np.float32), cat("sh").astype(np.float32))
```
